# Optimizing a Trainium2 kernel written in Bass

```python
import math
import jax, jax.numpy as jnp
from jax import lax
import numpy as np

D_MODEL = 2048
BATCH = 2
SEQ = 8192
DEPTH = 2

CTX_LEN = 256
GRID_W = 64
HEAD_DIM = 128
ATTN_WIDTH = D_MODEL // 2
ATTN_HEADS = ATTN_WIDTH // HEAD_DIM
KV_HEADS = ATTN_HEADS // 4
GQA_GROUP = ATTN_HEADS // KV_HEADS
KV_WIDTH = KV_HEADS * HEAD_DIM
POOL_WIDTH = D_MODEL // 4
POOL_WINDOWS = (2, 4, 8, 16)
POOL_GROUP = POOL_WIDTH // len(POOL_WINDOWS)
CONV_WIDTH = D_MODEL // 4
CONV_K = 3
D_FF = 4 * D_MODEL
Q_BLOCK = 128
ROPE_BASE = 10000.0
EPS = 1e-6
ATTN_SCALE = 1.0 / math.sqrt(HEAD_DIM)

K_OFF = ATTN_WIDTH
V_OFF = K_OFF + KV_WIDTH
POOL_OFF = V_OFF + KV_WIDTH
CB_OFF = POOL_OFF + POOL_WIDTH
CC_OFF = CB_OFF + CONV_WIDTH
CV_OFF = CC_OFF + CONV_WIDTH
IN_WIDTH = CV_OFF + CONV_WIDTH
MIX_WIDTH = ATTN_WIDTH + POOL_WIDTH + CONV_WIDTH

kernel_name = "hybrid_pool_conv_gqa_diffusion_block"


def _rmsnorm(x, g):
    x32 = x.astype(jnp.float32)
    y = x32 * lax.rsqrt(jnp.mean(x32 * x32, axis=-1, keepdims=True) + EPS)
    return (y * g.astype(jnp.float32)).astype(x.dtype)


def _axial_rope(n):
    rows = n // GRID_W
    row = jnp.repeat(jnp.arange(rows, dtype=jnp.float32), GRID_W)
    col = jnp.tile(jnp.arange(GRID_W, dtype=jnp.float32), rows)
    n_freq = HEAD_DIM // 4
    inv = ROPE_BASE ** (-jnp.arange(n_freq, dtype=jnp.float32) / n_freq)
    ang_r = row[:, None] * inv
    ang_c = col[:, None] * inv
    ang = jnp.concatenate([ang_r, ang_r, ang_c, ang_c], axis=-1)
    return jnp.cos(ang), jnp.sin(ang)


def _rope(x, cos, sin):
    x1, x2, x3, x4 = jnp.split(x, 4, axis=-1)
    rot = jnp.concatenate([-x2, x1, -x4, x3], axis=-1)
    return (x * cos + rot * sin).astype(x.dtype)


def _attend(q, k, v):
    s = jnp.einsum('bqhgd,bkhd->bhgqk', q, k, preferred_element_type=jnp.float32) * ATTN_SCALE
    p = jax.nn.softmax(s, axis=-1)
    return jnp.einsum('bhgqk,bkhd->bqhgd', p.astype(v.dtype), v, preferred_element_type=jnp.float32)


def _latent_attention(q, k, v, kc, vc, qg, kg, cos, sin):
    b, n, _ = q.shape
    q = q.reshape(b, n, KV_HEADS, GQA_GROUP, HEAD_DIM)
    k = k.reshape(b, n, KV_HEADS, HEAD_DIM)
    v = v.reshape(b, n, KV_HEADS, HEAD_DIM)
    q = _rope(_rmsnorm(q, qg), cos[:, None, None, :], sin[:, None, None, :])
    k = _rope(_rmsnorm(k, kg), cos[:, None, :], sin[:, None, :])
    keys = jnp.concatenate([kc.astype(k.dtype), k], axis=1)
    vals = jnp.concatenate([vc.astype(v.dtype), v], axis=1)
    nb = n // Q_BLOCK
    qb = q.reshape(b, nb, Q_BLOCK, KV_HEADS, GQA_GROUP, HEAD_DIM).swapaxes(0, 1)
    out = lax.map(lambda qq: _attend(qq, keys, vals), qb)
    return out.swapaxes(0, 1).reshape(b, n, ATTN_WIDTH).astype(q.dtype)


def _context_attention(qc, kc, vc, qg):
    b, m, _ = qc.shape
    qc = _rmsnorm(qc.reshape(b, m, KV_HEADS, GQA_GROUP, HEAD_DIM), qg)
    return _attend(qc, kc, vc).reshape(b, m, ATTN_WIDTH).astype(qc.dtype)


def _pool_mixer(u, w_pool, pool_scale):
    b, n, _ = u.shape
    u32 = u.astype(jnp.float32)
    csum = jnp.concatenate([jnp.zeros((b, 1, POOL_WIDTH), jnp.float32), jnp.cumsum(u32, axis=1)], axis=1)
    t = jnp.arange(n)
    diffs = []
    for gi, w in enumerate(POOL_WINDOWS):
        lo = w // 2
        hi = w - 1 - lo
        start = jnp.clip(t - lo, 0, n)
        end = jnp.clip(t + hi + 1, 0, n)
        seg = csum[:, :, gi * POOL_GROUP:(gi + 1) * POOL_GROUP]
        cnt = (end - start).astype(jnp.float32)[None, :, None]
        diffs.append((seg[:, end] - seg[:, start]) / cnt - u32[..., gi * POOL_GROUP:(gi + 1) * POOL_GROUP])
    d = jnp.stack(diffs, axis=2)
    y = jnp.einsum('bngc,gce->bnge', d, w_pool.astype(jnp.float32)).reshape(b, n, POOL_WIDTH)
    return (y * pool_scale.astype(jnp.float32)).astype(u.dtype)


def _short_conv_mixer(gb, gc, v, conv_w):
    z = gc * v
    rhs = conv_w[:, None, :].astype(z.dtype)
    conv = lax.conv_general_dilated(z, rhs, window_strides=(1,), padding=[(CONV_K // 2, CONV_K // 2)],
                                    dimension_numbers=('NWC', 'WIO', 'NWC'), feature_group_count=CONV_WIDTH)
    return gb * conv


def _mix_out(att, pu, gb, gc, cv, w_pool, pool_scale, conv_w, w_out):
    pool = _pool_mixer(pu, w_pool, pool_scale)
    conv = _short_conv_mixer(gb, gc, cv, conv_w)
    cat = jnp.concatenate([att, pool.astype(att.dtype), conv.astype(att.dtype)], axis=-1)
    return jnp.einsum('bne,ed->bnd', cat, w_out)


def _sqrelu_mlp(h, w1, w2):
    u = jax.nn.relu(jnp.einsum('bnd,df->bnf', h, w1))
    return jnp.einsum('bnf,fd->bnd', u * u, w2)


def _split_in(p):
    return jnp.split(p, [K_OFF, V_OFF, POOL_OFF, CB_OFF, CC_OFF, CV_OFF], axis=-1)


def setup_inputs(seed: int = 0) -> dict:
    key = jax.random.key(seed)
    ks = jax.random.split(key, 20)
    f32 = jnp.float32
    nrm = lambda k, shape, s: jax.random.normal(k, shape, f32) * s
    return {
        "x": nrm(ks[0], (BATCH, SEQ, D_MODEL), 1.0),
        "c": nrm(ks[1], (BATCH, D_MODEL), 1.0),
        "ctx": nrm(ks[2], (BATCH, CTX_LEN, D_MODEL), 1.0),
        "c_ctx": nrm(ks[3], (D_MODEL,), 1.0),
        "w_mod": nrm(ks[4], (DEPTH, D_MODEL, 6 * D_MODEL), 0.5 * D_MODEL ** -0.5),
        "b_mod": nrm(ks[5], (DEPTH, 6 * D_MODEL), 0.02),
        "norm1_g": 1.0 + nrm(ks[6], (DEPTH, D_MODEL), 0.05),
        "norm2_g": 1.0 + nrm(ks[7], (DEPTH, D_MODEL), 0.05),
        "w_in": nrm(ks[8], (DEPTH, D_MODEL, IN_WIDTH), D_MODEL ** -0.5),
        "q_norm_g": 1.0 + nrm(ks[9], (DEPTH, HEAD_DIM), 0.05),
        "k_norm_g": 1.0 + nrm(ks[10], (DEPTH, HEAD_DIM), 0.05),
        "w_pool": nrm(ks[11], (DEPTH, len(POOL_WINDOWS), POOL_GROUP, POOL_GROUP), POOL_GROUP ** -0.5),
        "pool_scale": 1.0 + nrm(ks[12], (DEPTH, POOL_WIDTH), 0.1),
        "conv_w": nrm(ks[13], (DEPTH, CONV_K, CONV_WIDTH), CONV_K ** -0.5),
        "w_out": nrm(ks[14], (DEPTH, MIX_WIDTH, D_MODEL), MIX_WIDTH ** -0.5),
        "w_ff1": nrm(ks[15], (DEPTH, D_MODEL, D_FF), D_MODEL ** -0.5),
        "w_ff2": nrm(ks[16], (DEPTH, D_FF, D_MODEL), D_FF ** -0.5),
    }


def reference(x, c, ctx, c_ctx, w_mod, b_mod, norm1_g, norm2_g, w_in, q_norm_g, k_norm_g,
              w_pool, pool_scale, conv_w, w_out, w_ff1, w_ff2):
    b, n, _ = x.shape
    m = ctx.shape[1]
    cos, sin = _axial_rope(n)
    c_act = jax.nn.silu(c)
    cc_act = jax.nn.silu(c_ctx)
    xc = ctx
    for l in range(DEPTH):
        last = l == DEPTH - 1
        mod = jnp.einsum('bd,de->be', c_act, w_mod[l]) + b_mod[l]
        sh1, sc1, gt1, sh2, sc2, gt2 = jnp.split(mod[:, None, :], 6, axis=-1)
        modc = jnp.einsum('d,de->e', cc_act, w_mod[l]) + b_mod[l]
        csh1, csc1, cgt1, csh2, csc2, cgt2 = jnp.split(modc, 6)

        hc = _rmsnorm(xc, norm1_g[l]) * (1.0 + csc1) + csh1
        if last:
            kc, vc = jnp.split(jnp.einsum('bnd,de->bne', hc, w_in[l][:, K_OFF:POOL_OFF]), 2, axis=-1)
        else:
            qc, kc, vc, puc, cbc, ccc, cvc = _split_in(jnp.einsum('bnd,de->bne', hc, w_in[l]))
        kc = _rmsnorm(kc.reshape(b, m, KV_HEADS, HEAD_DIM), k_norm_g[l])
        vc = vc.reshape(b, m, KV_HEADS, HEAD_DIM)

        h = _rmsnorm(x, norm1_g[l]) * (1.0 + sc1) + sh1
        q, k, v, pu, cb, cc, cv = _split_in(jnp.einsum('bnd,de->bne', h, w_in[l]))
        att = _latent_attention(q, k, v, kc, vc, q_norm_g[l], k_norm_g[l], cos, sin)
        x = x + gt1 * _mix_out(att, pu, cb, cc, cv, w_pool[l], pool_scale[l], conv_w[l], w_out[l])
        h2 = _rmsnorm(x, norm2_g[l]) * (1.0 + sc2) + sh2
        x = x + gt2 * _sqrelu_mlp(h2, w_ff1[l], w_ff2[l])

        if not last:
            attc = _context_attention(qc, kc, vc, q_norm_g[l])
            xc = xc + cgt1 * _mix_out(attc, puc, cbc, ccc, cvc, w_pool[l], pool_scale[l], conv_w[l], w_out[l])
            hc2 = _rmsnorm(xc, norm2_g[l]) * (1.0 + csc2) + csh2
            xc = xc + cgt2 * _sqrelu_mlp(hc2, w_ff1[l], w_ff2[l])
    return x
```

```python
import contextlib
import math
import numpy as np
import ml_dtypes
import concourse.bass as bass
import concourse.mybir as mybir
from concourse.bass_utils import run_bass_kernel_spmd

F32 = mybir.dt.float32
BF16 = mybir.dt.bfloat16
ALU = mybir.AluOpType
AF = mybir.ActivationFunctionType

D = 2048
KC = 16
NCTX = 256
NLAT = 2048
T = NCTX + NLAT
DEPTH = 2
NF = 64
EPS = 1e-6
SCALE = 1.0 / math.sqrt(128.0)
TL = [(0, 256), (256, 512), (768, 512), (1280, 512), (1792, 512)]
WN = 2336
ENGS = ("pe", "act", "dve", "pool", "sp")
SUM_OFFLOAD = True


class _Op:
    __slots__ = ("eng", "fn", "reads", "writes", "dma_key", "deps", "idx", "need_inc", "cnt", "dma_val", "final_wait")


class Prog:
    def __init__(self):
        self.ops = []
        self.last_w = {}
        self.readers = {}
        self.nblk = 0

    def add(self, eng, fn, reads=(), writes=(), dma_key=None, final_wait=False):
        op = _Op()
        op.final_wait = final_wait
        op.eng, op.fn, op.reads, op.writes, op.dma_key = eng, fn, tuple(reads), tuple(writes), dma_key
        op.deps = set()
        op.need_inc = False
        op.cnt = None
        op.dma_val = None
        op.idx = len(self.ops)
        for r in op.reads:
            w = self.last_w.get(r)
            if w is not None:
                op.deps.add(w)
        for t in op.writes:
            w = self.last_w.get(t)
            if w is not None:
                op.deps.add(w)
            for rd in self.readers.get(t, {}).values():
                if rd is not op:
                    op.deps.add(rd)
        for r in op.reads:
            d = self.readers.setdefault(r, {})
            k = ("dma", op.idx) if dma_key is not None else op.eng
            d[k] = op
        for t in op.writes:
            self.last_w[t] = op
            self.readers[t] = {}
        if dma_key is None:
            drop = set()
            for d in op.deps:
                if d.dma_key is None and d.eng == op.eng and op.eng == "pe":
                    if not any(r in d.writes for r in op.reads):
                        drop.add(d)
            op.deps -= drop
        self.ops.append(op)
        return op

    def dma(self, eng, key, out, in_, reads=(), writes=(), **kw):
        def fn(e):
            return e.dma_start(out=out, in_=in_, **kw)
        return self.add(eng, fn, reads, writes, dma_key=key)

    def setup(self, nc, stack):
        self.nc = nc
        self.stack = stack
        self.esem = {e: stack.enter_context(nc.semaphore(f"s_{e}")) for e in ENGS}
        self.dsem = {}
        self.cnt = {e: 0 for e in ENGS}
        self.dcnt = {}
        self.waited = {e: {} for e in ENGS}

    def emit(self, nc):
        ops = self.ops
        self.nblk += 1
        for op in ops:
            if op.final_wait:
                op.need_inc = True
            for d in op.deps:
                d.need_inc = True
        cnt, dcnt, esem, dsem = self.cnt, self.dcnt, self.esem, self.dsem
        for op in ops:
            if op.dma_key is not None:
                if op.dma_key not in dsem:
                    dsem[op.dma_key] = self.stack.enter_context(nc.semaphore(f"d_{len(dsem)}"))
                dcnt[op.dma_key] = dcnt.get(op.dma_key, 0) + 16
                op.dma_val = dcnt[op.dma_key]
            elif op.need_inc:
                cnt[op.eng] += 1
                op.cnt = cnt[op.eng]
        with nc.Block() as block:
            by_eng = {e: [op for op in ops if op.eng == e] for e in ENGS}

            def run(engname, engobj):
                waited = self.waited[engname]
                for op in by_eng[engname]:
                    need = {}
                    for d in op.deps:
                        if d.dma_key is not None:
                            s, v = dsem[d.dma_key], d.dma_val
                        else:
                            s, v = esem[d.eng], d.cnt
                        k = id(s)
                        if need.get(k, (None, 0))[1] < v:
                            need[k] = (s, v)
                    for k, (s, v) in need.items():
                        if waited.get(k, 0) < v:
                            engobj.wait_ge(s, v)
                            waited[k] = v
                    ins = op.fn(engobj)
                    if op.dma_key is not None:
                        ins.then_inc(dsem[op.dma_key], 16)
                    elif op.need_inc:
                        ins.then_inc(esem[op.eng], 1)
                fw = [op.cnt for op in by_eng[engname] if op.final_wait]
                if fw and waited.get(id(esem[engname]), 0) < max(fw):
                    engobj.wait_ge(esem[engname], max(fw))
                    waited[id(esem[engname])] = max(fw)
                last = {}
                for op in by_eng[engname]:
                    if op.dma_key is not None:
                        last[op.dma_key] = op.dma_val
                for k, v in last.items():
                    if waited.get(id(dsem[k]), 0) < v:
                        engobj.wait_ge(dsem[k], v)
                        waited[id(dsem[k])] = v

            @block.tensor
            def _(e):
                run("pe", e)

            @block.scalar
            def _(e):
                run("act", e)

            @block.vector
            def _(e):
                run("dve", e)

            @block.gpsimd
            def _(e):
                run("pool", e)

            @block.sync
            def _(e):
                run("sp", e)
        self.ops = []
        self.last_w = {}
        self.readers = {}


def wcol(c0):
    return 8 + c0 if c0 < NCTX else c0 + 24


def build(stop_after=None, debug=False, ncores=8, nl=DEPTH):
    nc = bass.Bass("TRN2", target_bir_lowering=False)

    def din(name, shape, dt=F32):
        return nc.dram_tensor(name, list(shape), dt, kind="ExternalInput").ap()

    def dint(name, shape, dt, kind="Internal"):
        return nc.dram_tensor(name, list(shape), dt, kind=kind).ap()

    xT = din("xT", [128, KC, T])
    xh = din("xh", [128, KC, 16])
    cvd = din("cv", [128, KC, 2])
    _wc = {}

    def wl(name, l, shape):
        key = f"{name}{l}"
        if key not in _wc:
            _wc[key] = din(key, shape)
        return _wc[key]

    def wmod_(l):
        return wl("wmod", l, [96, 128, KC, 128])

    def win_(l):
        return wl("win", l, [28, 128, KC, 128])

    def wout_(l):
        return wl("wout", l, [KC, 128, KC, 128])

    def wff1_(l):
        return wl("wff1", l, [NF, 128, KC, 128])

    def wff2_(l):
        return wl("wff2", l, [NF, 128, D])
    bmod = din("bmod", [128, DEPTH, 96, 2])
    g12 = din("g12", [128, DEPTH, 2, KC])
    qkg = din("qkg", [128, DEPTH, 4])
    csd = din("cs", [128, 2, T])
    rmatd = din("rmat", [128, 128], BF16)
    wpoold = din("wpool", [128, DEPTH, 4, 128])
    pscaled = din("pscale", [128, DEPTH, 4])
    convwd = din("convw", [128, DEPTH, 4, 3])
    ptabd = din("ptab", [128, 4, 4, 8])
    hmaskd = din("hmask", [128, 16])
    hseld = din("hsel", [128, 8])
    outT = nc.dram_tensor("outT", [128, KC, NLAT], F32, kind="ExternalOutput").ap()

    dk = "ExternalOutput" if debug else "Internal"
    xres = dint("xres", [128, KC, T], F32, dk)
    qbuf = dint("qbuf", [128, 8, T], BF16, dk)
    mixbuf = dint("mixbuf", [128, KC, T], BF16, dk)
    kvc = dint("kvc", [128, 2, 2, 256], BF16, dk)
    hdbg = dint("hdbg", [128, KC, T + 16], BF16, dk) if debug else None
    moddbg = dint("moddbg", [128, DEPTH, 96, 2], F32, dk) if debug else None
    kv_in = dint("kv_in", [256, 4096], BF16)
    kv_all = dint("kv_all", [2, 512, 4096], BF16)
    kvdbg = dint("kvdbg", [2, 512, 4096], BF16, dk) if debug else None
    edge_in = dint("edge_in", [128, 256], F32)
    edge_all = dint("edge_all", [512, 256], F32)
    RG = [[0, 1, 2, 3], [4, 5, 6, 7]] if ncores == 8 else [[0, 1, 2, 3]]

    P = Prog()
    es = contextlib.ExitStack()
    _nm = [0]
    with es:
        P.setup(nc, es)

        def sb(name, shape, dt=F32, stack=es):
            _nm[0] += 1
            return stack.enter_context(nc.sbuf_tensor(f"{name}_{_nm[0]}", list(shape), dt))

        ps = es.enter_context(nc.psum_tensor("ps", [128, 8, 512], F32))
        modsb = sb("modsb", [128, DEPTH, 96, 2])
        A12 = sb("A12", [128, DEPTH, 2, KC, 2])
        g12s = sb("g12s", [128, DEPTH, 2, KC])
        qkgs = sb("qkgs", [128, DEPTH, 4])
        pscs = sb("pscs", [128, DEPTH, 4])
        cws = sb("cws", [128, DEPTH, 4, 3])
        ptabs = sb("ptabs", [128, 4, 4, 8])
        hmasks = sb("hmasks", [128, 16])
        hsels = sb("hsels", [128, 8])
        ones = sb("ones", [128, 128], BF16)
        rmats = sb("rmats", [128, 128], BF16)
        wpools = sb("wpools", [128, DEPTH, 4, 128], BF16)

        def mv(l, k, c, s):
            return modsb[:, l, k * 16 + c, s:s + 1]

        def phase_M():
            with contextlib.ExitStack() as st:
                cvs = sb("cvs", [128, KC, 2], F32, st)
                cact = sb("cact", [128, KC, 2], BF16, st)
                bms = sb("bms", [128, DEPTH, 96, 2], F32, st)
                tmpA = sb("tmpA", [128, KC, 2], F32, st)
                wm = [sb(f"wm{i}", [128, 4, KC, 128], BF16, st) for i in range(2)]
                P.dma("sp", "ld_cv", cvs[:], cvd, writes=["cvs"])
                P.dma("sp", "ld_bm", bms[:], bmod, writes=["bms"])
                P.dma("sp", "ld_g12", g12s[:], g12, writes=["g12s"])
                P.dma("sp", "ld_qkg", qkgs[:], qkg, writes=["qkgs"])
                P.dma("sp", "ld_psc", pscs[:], pscaled, writes=["pscs"])
                P.dma("sp", "ld_cw", cws[:], convwd, writes=["cws"])
                P.dma("sp", "ld_ptab", ptabs[:], ptabd, writes=["ptabs"])
                P.dma("sp", "ld_hm", hmasks[:], hmaskd, writes=["hmasks"])
                P.dma("sp", "ld_hs", hsels[:], hseld, writes=["hsels"])
                P.dma("sp", "ld_rm", rmats[:], rmatd, writes=["rmats"])
                P.dma("pool", "ld_wp", wpools[:], wpoold, writes=["wpools"])
                P.add("dve", lambda e: e.memset(ones[:], 1.0), writes=["ones"])
                P.add("act", lambda e: e.activation(out=cact[:], in_=cvs[:], func=AF.Silu),
                      reads=["cvs"], writes=["cact"])
                for l in range(nl):
                    psm = ps[:, l, 0:192].rearrange("p (e s) -> p e s", s=2)
                    for eg in range(24):
                        slot = (l * 24 + eg) % 2
                        P.dma("pool", f"wm{slot}", wm[slot][:],
                              wmod_(l)[eg * 4:(eg + 1) * 4].rearrange("e p k m -> p e k m"),
                              writes=[f"wm{slot}"])

                        def mm(e, slot=slot, eg=eg, psm=psm):
                            last = None
                            for e4 in range(4):
                                for kc in range(KC):
                                    last = e.matmul(psm[:, eg * 4 + e4, :], lhsT=wm[slot][:, e4, kc, :],
                                                    rhs=cact[:, kc, :], start=(kc == 0), stop=(kc == KC - 1))
                            return last
                        P.add("pe", mm, reads=[f"wm{slot}", "cact"], writes=[f"psmod{l}"])
                    P.add("dve", lambda e, l=l, psm=psm: e.tensor_tensor(out=modsb[:, l], in0=psm, in1=bms[:, l], op=ALU.add),
                          reads=[f"psmod{l}", "bms"], writes=["modsb"])
                    for w, ksc in enumerate((1, 4)):
                        P.add("dve", lambda e, l=l, ksc=ksc: e.tensor_scalar(
                            out=tmpA[:], in0=modsb[:, l, ksc * 16:(ksc + 1) * 16, :], scalar1=1.0, scalar2=None, op0=ALU.add),
                            reads=["modsb"], writes=["tmpA"])
                        P.add("dve", lambda e, l=l, w=w: e.tensor_tensor(
                            out=A12[:, l, w], in0=tmpA[:],
                            in1=g12s[:, l, w, :].unsqueeze(2).broadcast_to([128, KC, 2]), op=ALU.mult),
                            reads=["tmpA", "g12s"], writes=["A12"])
                if debug:
                    P.dma("sp", "dbg_mod", moddbg[:, 0:nl], modsb[:, 0:nl], reads=["modsb"])
                P.emit(nc)

        class NormBufs:
            def __init__(self, st, tag, n):
                self.nr = 6
                self.sq = [sb(f"nsq{tag}{i}", [128, n], BF16, st) for i in range(self.nr)]
                self.tc = [sb(f"ntc{tag}{i}", [128, n], F32, st) for i in range(self.nr)]
                self.rt = sb(f"nrt{tag}", [128, n], F32, st)
                self.rstd = sb(f"nrstd{tag}", [128, n], F32, st)
                self.cnt = 0
                self.tag = tag

        def emit_norm(nb, xc_fn, n, out_fn, l, w, s, psb, rtok, wtok):
            tg = nb.tag
            for c in range(KC):
                i = nb.cnt % nb.nr
                nb.cnt += 1
                P.add("act", lambda e, c=c, i=i: e.activation(out=nb.sq[i][:, :n], in_=xc_fn(c), func=AF.Square),
                      reads=rtok, writes=[f"nsq{tg}{i}"])
                P.add("pe", lambda e, c=c, i=i: e.matmul(ps[:, psb, :n], lhsT=ones[:], rhs=nb.sq[i][:, :n],
                                                         start=(c == 0), stop=(c == KC - 1)),
                      reads=[f"nsq{tg}{i}", "ones"], writes=[f"ps{psb}"])
            P.add("act", lambda e: e.activation(out=nb.rt[:, :n], in_=ps[:, psb, :n], func=AF.Sqrt, bias=EPS, scale=1.0 / D),
                  reads=[f"ps{psb}"], writes=[f"nrt{tg}"])
            P.add("dve", lambda e: e.reciprocal(out=nb.rstd[:, :n], in_=nb.rt[:, :n]),
                  reads=[f"nrt{tg}"], writes=[f"nrstd{tg}"])
            ksh = 0 if w == 0 else 3
            for c in range(KC):
                i = nb.cnt % nb.nr
                nb.cnt += 1
                P.add("dve", lambda e, c=c, i=i: e.tensor_tensor(out=nb.tc[i][:, :n], in0=xc_fn(c), in1=nb.rstd[:, :n], op=ALU.mult),
                      reads=list(rtok) + [f"nrstd{tg}"], writes=[f"ntc{tg}{i}"])
                P.add("act", lambda e, c=c, i=i: e.activation(out=out_fn(c), in_=nb.tc[i][:, :n], func=AF.Identity,
                                                              bias=mv(l, ksh, c, s), scale=A12[:, l, w, c, s:s + 1]),
                      reads=[f"ntc{tg}{i}"], writes=wtok)

        def phase_A(l):
            last = (l == DEPTH - 1)
            with contextlib.ExitStack() as stA:
                hT = sb("hT", [128, KC, T + 16], BF16, stA)
                with contextlib.ExitStack() as st:
                    xs = [sb(f"xs{i}", [128, KC, 512], F32, st) for i in range(2)]
                    xhs = sb("xhs", [128, KC, 16], F32, st)
                    nb = NormBufs(st, "a", 512)
                    src = xT if l == 0 else xres
                    for i, (c0, n) in enumerate(TL):
                        slot = i % 2
                        s = 1 if c0 < NCTX else 0
                        P.dma("sp", f"xs{slot}", xs[slot][:, :, :n], src[:, :, c0:c0 + n], writes=[f"xs{slot}"])
                        emit_norm(nb, lambda c, slot=slot, n=n: xs[slot][:, c, :n], n,
                                  lambda c, c0=c0, n=n: hT[:, c, c0:c0 + n], l, 0, s, i % 2,
                                  [f"xs{slot}"], [f"hT{c0}"])
                    if l == 0:
                        P.dma("sp", "xhs", xhs[:], xh, writes=["xhs"])
                    else:
                        egs = sb("egs", [128, 4, KC, 16], F32, st)
                        P.dma("sp", "egs", egs[:], edge_all.rearrange("(r p) (c t) -> p r c t", p=128, t=16), writes=["egs"])
                        for side in range(2):
                            dst = xhs[:, :, side * 8:(side + 1) * 8]
                            for r in range(4):
                                srcc = egs[:, r, :, (1 - side) * 8:(2 - side) * 8]
                                sc = hsels[:, side * 4 + r:side * 4 + r + 1]
                                if r == 0:
                                    P.add("dve", lambda e, dst=dst, srcc=srcc, sc=sc: e.tensor_scalar(
                                        out=dst, in0=srcc, scalar1=sc, scalar2=None, op0=ALU.mult),
                                        reads=["egs"], writes=["xhs"])
                                else:
                                    P.add("dve", lambda e, dst=dst, srcc=srcc, sc=sc: e.scalar_tensor_tensor(
                                        out=dst, in0=srcc, scalar=sc, in1=dst, op0=ALU.mult, op1=ALU.add),
                                        reads=["egs", "xhs"], writes=["xhs"])
                    emit_norm(nb, lambda c: xhs[:, c, :], 16, lambda c: hT[:, c, T:T + 16], l, 0, 0, 2, ["xhs"], ["hTh"])
                    if debug and l == 0:
                        P.dma("sp", "dbg_h", hdbg, hT[:], reads=[f"hT{c0}" for c0, _ in TL] + ["hTh"])
                    P.emit(nc)
                if stop_after == f"A0_{l}":
                    return True
                with contextlib.ExitStack() as st:
                    wr = [sb(f"wr{i}", [128, KC, 128], BF16, st) for i in range(3)]
                    wv = sb("wv", [128, KC, 256], BF16, st)
                    css = sb("css", [128, 2, T], F32, st)
                    W = [sb(f"W{i}", [128, WN], F32, st) for i in range(4)]
                    Dbf = sb("Dbf", [128, WN], BF16, st)
                    tmp8 = sb("tmp8", [128, 8], F32, st)
                    sqb = [sb(f"sqb{i}", [128, 512], BF16, st) for i in range(2)]
                    qbb = [sb(f"qbb{i}", [128, 512], BF16, st) for i in range(2)]
                    rt = [sb(f"rt{i}", [128, 512], F32, st) for i in range(2)]
                    rstd = [sb(f"rstd{i}", [128, 512], F32, st) for i in range(2)]
                    u1 = [sb(f"u1{i}", [128, 512], F32, st) for i in range(2)]
                    u2 = [sb(f"u2{i}", [128, 512], F32, st) for i in range(2)]
                    ost = [sb(f"ost{i}", [128, 512], BF16, st) for i in range(3)]
                    P.dma("sp", "ld_cs", css[:], csd, writes=["css"])
                    for i in range(4):
                        P.add("dve", lambda e, i=i: e.memset(W[i][:], 0.0), writes=[f"W{i}"])
                    cnt = {"w": 0, "mb": 0, "ep": 0, "ost": 0, "aux": 0}

                    def load_w(chunk):
                        slot = cnt["w"] % 3
                        cnt["w"] += 1
                        P.dma("pool", f"wr{slot}", wr[slot][:], win_(l)[chunk], writes=[f"wr{slot}"])
                        return slot

                    def mm_tile(wslot, c0, n):
                        psb = cnt["mb"] % 4
                        cnt["mb"] += 1

                        def fn(e):
                            last = None
                            for kc in range(KC):
                                last = e.matmul(ps[:, psb, :n], lhsT=wr[wslot][:, kc, :], rhs=hT[:, kc, c0:c0 + n],
                                                start=(kc == 0), stop=(kc == KC - 1))
                            return last
                        P.add("pe", fn, reads=[f"wr{wslot}"], writes=[f"ps{psb}"])
                        return psb

                    def new_ost():
                        o = cnt["ost"] % 3
                        cnt["ost"] += 1
                        return o

                    def qk_chunk(chunk, kind, hidx):
                        wslot = load_w(chunk)
                        gcol = 0 if kind == "q" else 2
                        tiles = TL if (kind == "k" or not last) else TL[1:]
                        for (c0, n) in tiles:
                            psb = mm_tile(wslot, c0, n)
                            i = cnt["ep"] % 2
                            cnt["ep"] += 1
                            ssb, rtb = 4 + i, 6 + i
                            P.add("act", lambda e, psb=psb, i=i, n=n: e.activation(out=sqb[i][:, :n], in_=ps[:, psb, :n], func=AF.Square),
                                  reads=[f"ps{psb}"], writes=[f"sqb{i}"])
                            P.add("act", lambda e, psb=psb, i=i, n=n: e.activation(out=qbb[i][:, :n], in_=ps[:, psb, :n], func=AF.Copy),
                                  reads=[f"ps{psb}"], writes=[f"qbb{i}"])
                            P.add("pe", lambda e, ssb=ssb, i=i, n=n: e.matmul(ps[:, ssb, :n], lhsT=ones[:], rhs=sqb[i][:, :n], start=True, stop=True),
                                  reads=[f"sqb{i}"], writes=[f"ps{ssb}"])
                            P.add("pe", lambda e, rtb=rtb, i=i, n=n: e.matmul(ps[:, rtb, :n], lhsT=rmats[:], rhs=qbb[i][:, :n], start=True, stop=True),
                                  reads=[f"qbb{i}"], writes=[f"ps{rtb}"])
                            P.add("act", lambda e, ssb=ssb, i=i, n=n: e.activation(out=rt[i][:, :n], in_=ps[:, ssb, :n], func=AF.Sqrt,
                                                                                   bias=EPS, scale=1.0 / 128.0),
                                  reads=[f"ps{ssb}"], writes=[f"rt{i}"])
                            P.add("dve", lambda e, i=i, n=n: e.reciprocal(out=rstd[i][:, :n], in_=rt[i][:, :n]),
                                  reads=[f"rt{i}"], writes=[f"rstd{i}"])
                            P.add("dve", lambda e, psb=psb, i=i, n=n, c0=c0: e.scalar_tensor_tensor(
                                out=u1[i][:, :n], in0=ps[:, psb, :n], scalar=qkgs[:, l, gcol:gcol + 1], in1=css[:, 0, c0:c0 + n],
                                op0=ALU.mult, op1=ALU.mult), reads=[f"ps{psb}", "css"], writes=[f"u1{i}"])
                            P.add("dve", lambda e, rtb=rtb, i=i, n=n, c0=c0: e.scalar_tensor_tensor(
                                out=u2[i][:, :n], in0=ps[:, rtb, :n], scalar=qkgs[:, l, gcol + 1:gcol + 2], in1=css[:, 1, c0:c0 + n],
                                op0=ALU.mult, op1=ALU.mult), reads=[f"ps{rtb}", "css"], writes=[f"u2{i}"])
                            P.add("dve", lambda e, i=i, n=n: e.tensor_tensor(out=u1[i][:, :n], in0=u1[i][:, :n], in1=u2[i][:, :n], op=ALU.add),
                                  reads=[f"u1{i}", f"u2{i}"], writes=[f"u1{i}"])
                            o = new_ost()
                            P.add("dve", lambda e, i=i, n=n, o=o: e.tensor_tensor(out=ost[o][:, :n], in0=u1[i][:, :n], in1=rstd[i][:, :n], op=ALU.mult),
                                  reads=[f"u1{i}", f"rstd{i}"], writes=[f"ost{o}"])
                            if kind == "q":
                                dst = qbuf[:, hidx, c0:c0 + n]
                                wt = []
                            elif c0 < NCTX:
                                dst = kvc[:, 0, hidx, :]
                                wt = []
                            else:
                                dst = kv_in[0:128, hidx * NLAT + c0 - NCTX: hidx * NLAT + c0 - NCTX + n]
                                wt = [f"kv_in:k{hidx}:{c0}"]
                                kvtoks.append(wt[0])
                            P.dma("sp", f"ost{o}", dst, ost[o][:, :n], reads=[f"ost{o}"], writes=wt)

                    kvtoks = []

                    def v_chunk():
                        for j in range(2):
                            P.dma("pool", f"wv{j}", wv[:, :, j * 128:(j + 1) * 128], win_(l)[10 + j], writes=["wv"])
                        for tt in range(18):
                            psb = cnt["mb"] % 4
                            cnt["mb"] += 1

                            def fn(e, tt=tt, psb=psb):
                                last = None
                                for kc in range(KC):
                                    last = e.matmul(ps[:, psb, :256], lhsT=hT[:, kc, tt * 128:(tt + 1) * 128], rhs=wv[:, kc, :],
                                                    start=(kc == 0), stop=(kc == KC - 1))
                                return last
                            P.add("pe", fn, reads=["wv"], writes=[f"ps{psb}"])
                            o = new_ost()
                            P.add("act", lambda e, psb=psb, o=o: e.activation(out=ost[o][:, :256], in_=ps[:, psb, :256], func=AF.Copy),
                                  reads=[f"ps{psb}"], writes=[f"ost{o}"])
                            if tt < 2:
                                dst = kvc[:, 1, tt, :]
                                wt = []
                            else:
                                dst = kv_in[128:256, (tt - 2) * 256:(tt - 1) * 256]
                                wt = [f"kv_in:v{tt}"]
                                kvtoks.append(wt[0])
                            P.dma("sp", f"ost{o}", dst, ost[o][:, :256], reads=[f"ost{o}"], writes=wt)

                    mtiles = TL[1:] if last else TL
                    HL = (T, 16)

                    def halo_pair(fn):
                        fn(272, 0)
                        fn(2328, 8)

                    def conv_triple(j):
                        Wa, Wb = W[2 * (j % 2)], W[2 * (j % 2) + 1]
                        ta, tb = f"W{2 * (j % 2)}", f"W{2 * (j % 2) + 1}"
                        wslot = load_w(24 + j)
                        for (c0, n) in mtiles:
                            psb = mm_tile(wslot, c0, n)
                            P.add("act", lambda e, psb=psb, c0=c0, n=n: e.activation(out=Wa[:, wcol(c0):wcol(c0) + n], in_=ps[:, psb, :n], func=AF.Copy),
                                  reads=[f"ps{psb}"], writes=[ta])
                        psb = mm_tile(wslot, *HL)
                        halo_pair(lambda wc, hc, psb=psb: P.add("dve", lambda e: e.tensor_tensor(
                            out=Wa[:, wc:wc + 8], in0=ps[:, psb, hc:hc + 8], in1=hmasks[:, hc:hc + 8], op=ALU.mult),
                            reads=[f"ps{psb}"], writes=[ta]))
                        wslot = load_w(20 + j)
                        for (c0, n) in mtiles:
                            psb = mm_tile(wslot, c0, n)
                            P.add("dve", lambda e, psb=psb, c0=c0, n=n: e.tensor_tensor(
                                out=Wa[:, wcol(c0):wcol(c0) + n], in0=ps[:, psb, :n], in1=Wa[:, wcol(c0):wcol(c0) + n], op=ALU.mult),
                                reads=[f"ps{psb}", ta], writes=[ta])
                        psb = mm_tile(wslot, *HL)
                        halo_pair(lambda wc, hc, psb=psb: P.add("dve", lambda e: e.tensor_tensor(
                            out=Wa[:, wc:wc + 8], in0=ps[:, psb, hc:hc + 8], in1=Wa[:, wc:wc + 8], op=ALU.mult),
                            reads=[f"ps{psb}", ta], writes=[ta]))
                        P.add("dve", lambda e: e.tensor_scalar(out=Wb[:, 1:WN - 1], in0=Wa[:, 1:WN - 1], scalar1=cws[:, l, j, 1:2], scalar2=None, op0=ALU.mult),
                              reads=[ta], writes=[tb])
                        P.add("dve", lambda e: e.scalar_tensor_tensor(out=Wb[:, 1:WN - 1], in0=Wa[:, 0:WN - 2], scalar=cws[:, l, j, 0:1],
                                                                      in1=Wb[:, 1:WN - 1], op0=ALU.mult, op1=ALU.add),
                              reads=[ta, tb], writes=[tb])
                        P.add("dve", lambda e: e.scalar_tensor_tensor(out=Wb[:, 1:WN - 1], in0=Wa[:, 2:WN], scalar=cws[:, l, j, 2:3],
                                                                      in1=Wb[:, 1:WN - 1], op0=ALU.mult, op1=ALU.add),
                              reads=[ta, tb], writes=[tb])
                        wslot = load_w(16 + j)
                        for (c0, n) in mtiles:
                            psb = mm_tile(wslot, c0, n)
                            o = new_ost()
                            P.add("dve", lambda e, psb=psb, c0=c0, n=n, o=o: e.tensor_tensor(
                                out=ost[o][:, :n], in0=ps[:, psb, :n], in1=Wb[:, wcol(c0):wcol(c0) + n], op=ALU.mult),
                                reads=[f"ps{psb}", tb], writes=[f"ost{o}"])
                            P.dma("sp", f"ost{o}", mixbuf[:, 12 + j, c0:c0 + n], ost[o][:, :n], reads=[f"ost{o}"])

                    def pool_chunk(g):
                        U, Bb, Cb = W[0], W[1], W[3]
                        w = (2, 4, 8, 16)[g]
                        wslot = load_w(12 + g)
                        for (c0, n) in mtiles:
                            psb = mm_tile(wslot, c0, n)
                            P.add("act", lambda e, psb=psb, c0=c0, n=n: e.activation(out=U[:, wcol(c0):wcol(c0) + n], in_=ps[:, psb, :n], func=AF.Copy),
                                  reads=[f"ps{psb}"], writes=["W0"])
                        psb = mm_tile(wslot, *HL)
                        halo_pair(lambda wc, hc, psb=psb: P.add("dve", lambda e: e.tensor_tensor(
                            out=U[:, wc:wc + 8], in0=ps[:, psb, hc:hc + 8], in1=hmasks[:, hc:hc + 8], op=ALU.mult),
                            reads=[f"ps{psb}"], writes=["W0"]))
                        N = WN

                        def add(dst, dlo, a, alo, b, blo, ln, rd, wt):
                            P.add("dve", lambda e: e.tensor_tensor(out=dst[:, dlo:dlo + ln], in0=a[:, alo:alo + ln], in1=b[:, blo:blo + ln], op=ALU.add),
                                  reads=rd, writes=wt)
                        if g == 0:
                            add(Bb, 1, U, 0, U, 1, N - 1, ["W0"], ["W1"])
                            S, ts = Bb, "W1"
                        else:
                            add(Bb, 0, U, 0, U, 1, N - 1, ["W0"], ["W1"])
                            if g == 1:
                                add(Cb, 2, Bb, 0, Bb, 2, N - 3, ["W1"], ["W3"])
                                S, ts = Cb, "W3"
                            else:
                                add(Cb, 0, Bb, 0, Bb, 2, N - 3, ["W1"], ["W3"])
                                if g == 2:
                                    add(Bb, 4, Cb, 0, Cb, 4, N - 7, ["W3"], ["W1"])
                                    S, ts = Bb, "W1"
                                else:
                                    add(Bb, 0, Cb, 0, Cb, 4, N - 7, ["W3"], ["W1"])
                                    add(Cb, 8, Bb, 0, Bb, 8, N - 15, ["W1"], ["W3"])
                                    S, ts = Cb, "W3"
                        P.add("dve", lambda e: e.scalar_tensor_tensor(out=Dbf[:, 8:N - 8], in0=S[:, 8:N - 8], scalar=1.0 / w, in1=U[:, 8:N - 8],
                                                                      op0=ALU.mult, op1=ALU.subtract),
                              reads=[ts, "W0"], writes=["Dbf"])
                        for ei, ec in enumerate((8, 256, 280, 2320)):
                            P.add("dve", lambda e, ei=ei, ec=ec: e.tensor_tensor(out=tmp8[:], in0=S[:, ec:ec + 8], in1=ptabs[:, g, ei, :], op=ALU.mult),
                                  reads=[ts], writes=["tmp8"])
                            P.add("dve", lambda e, ec=ec: e.tensor_tensor(out=Dbf[:, ec:ec + 8], in0=tmp8[:], in1=U[:, ec:ec + 8], op=ALU.subtract),
                                  reads=["tmp8", "W0", "Dbf"], writes=["Dbf"])
                        for (c0, n) in mtiles:
                            ab = 4 + cnt["aux"] % 4
                            cnt["aux"] += 1
                            P.add("pe", lambda e, ab=ab, c0=c0, n=n: e.matmul(ps[:, ab, :n], lhsT=wpools[:, l, g, :], rhs=Dbf[:, wcol(c0):wcol(c0) + n],
                                                                              start=True, stop=True),
                                  reads=["Dbf"], writes=[f"ps{ab}"])
                            o = new_ost()
                            P.add("act", lambda e, ab=ab, n=n, o=o: e.activation(out=ost[o][:, :n], in_=ps[:, ab, :n], func=AF.Copy, scale=pscs[:, l, g:g + 1]),
                                  reads=[f"ps{ab}"], writes=[f"ost{o}"])
                            P.dma("sp", f"ost{o}", mixbuf[:, 8 + g, c0:c0 + n], ost[o][:, :n], reads=[f"ost{o}"])

                    sub = stop_after.split(":")[1] if (stop_after and ":" in stop_after and stop_after.startswith(f"A1_{l}")) else "all"
                    qk_chunk(8, "k", 0)
                    qk_chunk(9, "k", 1)
                    if sub != "k":
                        v_chunk()
                    if sub not in ("k", "v"):
                        for j in range(4):
                            conv_triple(j)
                    if sub not in ("k", "v", "conv"):
                        for g in range(4):
                            pool_chunk(g)
                    if sub not in ("k", "v", "conv", "pool"):
                        for h in range(8):
                            qk_chunk(h, "q", h)
                    if sub not in ("k", "v", "conv", "pool", "q"):
                        for kvi in range(2):
                            P.add("pool", lambda e, kvi=kvi: e.collective_compute(
                                "AllGather", ALU.bypass, replica_groups=RG, ins=[kv_in[kvi * 128:(kvi + 1) * 128, :]], outs=[kv_all[kvi]]),
                                reads=kvtoks + ["kv_all"], writes=["kv_all"], final_wait=True)
                        if debug and l == 0:
                            P.dma("sp", "dbg_kv", kvdbg, kv_all, reads=["kv_all"])
                    P.emit(nc)
                    if sub != "all":
                        return True
            return stop_after == f"A1_{l}"

        def phase_B(l):
            last = (l == DEPTH - 1)
            with contextlib.ExitStack() as st:
                Kf = sb("Kf", [128, 2, NCTX + 4 * NLAT], BF16, st)
                Vf = sb("Vf", [128, 66, 256], BF16, st)
                qa = sb("qa", [128, 8, T], BF16, st)
                Pt = [sb(f"Pt{i}", [128, 2, 512], BF16, st) for i in range(4)]
                rec = [sb(f"rec{i}", [128, 512], F32, st) for i in range(2)]
                sacc = [sb(f"sacc{i}", [128, 2, 512], F32, st) for i in range(2)]
                ssb = sb("ssb", [128, 512], BF16, st)
                kva = kv_all.rearrange("k (r p) c -> k p r c", r=4, p=128)
                for h in range(2):
                    P.dma("sp", f"ldk{h}", Kf[:, h, NCTX:].rearrange("p (r t) -> p r t", r=4), kva[0][:, :, h * NLAT:(h + 1) * NLAT],
                          writes=["Kf"])
                P.dma("sp", "ldkc", Kf[:, :, 0:NCTX], kvc[:, 0], writes=["Kf"])
                P.dma("sp", "ldv", Vf[:, 2:66, :].rearrange("p (r i) d -> p r (i d)", r=4), kva[1], writes=["Vf"])
                P.dma("sp", "ldvc", Vf[:, 0:2, :], kvc[:, 1], writes=["Vf"])
                P.dma("sp", "ldq", qa[:], qbuf, writes=["qa"])
                iters = []
                if not last:
                    for h in range(8):
                        iters.append((h, 0, 256, [0, 1]))
                for h in range(8):
                    for (c0, n) in TL[1:]:
                        iters.append((h, c0, n, list(range(66))))
                import os as _os
                if _os.environ.get("KDBG_BITERS"):
                    _n = int(_os.environ["KDBG_BITERS"])
                    _keep = iters[:_n] + iters[8:8 + _n]
                    for (h_, c0_, n_, _k) in iters:
                        if (h_, c0_, n_, _k) not in _keep:
                            P.dma("sp", f"dbgq{h_ % 2}", mixbuf[:, h_, c0_:c0_ + n_], qa[:, h_, c0_:c0_ + n_], reads=["qa"])
                    iters = _keep
                cnt = {"s": 0, "p": 0}

                def do_iter(it, h, c0, n, kts):
                    ab = it % 2
                    kvh = h // 4
                    bo = 6 + ab
                    qtok = f"q:{h}:{c0}"
                    nk = len(kts)
                    npair = nk // 2
                    offl = True

                    def mode(j):
                        return "pool" if j % 3 == 2 else "dve"
                    pe_js = []
                    sbank = {}
                    pslot = {}
                    first = {"dve": True, "pool": True}

                    def S2(j):
                        b0 = 2 * (cnt["s"] % 3)
                        cnt["s"] += 1
                        sbank[j] = b0
                        k0, k1 = kts[2 * j], kts[2 * j + 1]

                        def fn(e, b0=b0, k0=k0, k1=k1):
                            e.matmul(ps[:, b0, :n], lhsT=Kf[:, kvh, k0 * 128:(k0 + 1) * 128], rhs=qa[:, h, c0:c0 + n], start=True, stop=True)
                            return e.matmul(ps[:, b0 + 1, :n], lhsT=Kf[:, kvh, k1 * 128:(k1 + 1) * 128], rhs=qa[:, h, c0:c0 + n], start=True, stop=True)
                        P.add("pe", fn, reads=["Kf", "qa", qtok], writes=[f"ps{b0}", f"ps{b0 + 1}"])

                    def E2(j):
                        b0 = sbank[j]
                        p = cnt["p"] % 4
                        cnt["p"] += 1
                        pslot[j] = p
                        P.add("act", lambda e, b0=b0, p=p: e.activation(out=Pt[p][:, :, :n], in_=ps[:, b0:b0 + 2, :n], func=AF.Exp, scale=SCALE),
                              reads=[f"ps{b0}", f"ps{b0 + 1}"], writes=[f"Pt{p}"])

                    def PV2(j):
                        p = pslot[j]
                        k0, k1 = kts[2 * j], kts[2 * j + 1]
                        md = mode(j)

                        def fn(e, p=p, k0=k0, k1=k1, j=j, md=md):
                            e.matmul(ps[:, bo, :n], lhsT=Vf[:, k0, kvh * 128:(kvh + 1) * 128], rhs=Pt[p][:, 0, :n], start=(j == 0), stop=False)
                            r = e.matmul(ps[:, bo, :n], lhsT=Vf[:, k1, kvh * 128:(kvh + 1) * 128], rhs=Pt[p][:, 1, :n], start=False, stop=(j == npair - 1))
                            if md == "pe":
                                lastpe = (j == pe_js[-1]) and not offl
                                e.matmul(ps[:, bs, :n], lhsT=ones[:], rhs=Pt[p][:, 0, :n], start=(j == 0), stop=False)
                                r = e.matmul(ps[:, bs, :n], lhsT=ones[:], rhs=Pt[p][:, 1, :n], start=False, stop=lastpe)
                            return r
                        P.add("pe", fn, reads=[f"Pt{p}", "Vf"], writes=[f"acc{ab}"])
                        if md != "pe":
                            a = 0 if md == "dve" else 1
                            if first[md]:
                                first[md] = False
                                P.add(md, lambda e, a=a, p=p: e.tensor_copy(out=sacc[a][:, :, :n], in_=Pt[p][:, :, :n]),
                                      reads=[f"Pt{p}"], writes=[f"sacc{a}"])
                            else:
                                P.add(md, lambda e, a=a, p=p: e.tensor_tensor(out=sacc[a][:, :, :n], in0=sacc[a][:, :, :n], in1=Pt[p][:, :, :n], op=ALU.add),
                                      reads=[f"Pt{p}", f"sacc{a}"], writes=[f"sacc{a}"])
                    LA = 2
                    for j in range(min(LA, npair)):
                        S2(j)
                    for j in range(npair):
                        if j + LA < npair:
                            S2(j + LA)
                        E2(j)
                        PV2(j)
                    used_pool = not first["pool"]
                    if used_pool:
                        P.add("pool", lambda e: e.tensor_tensor(out=sacc[1][:, 0, :n], in0=sacc[1][:, 0, :n], in1=sacc[1][:, 1, :n], op=ALU.add),
                              reads=["sacc1"], writes=["sacc1"])
                        P.add("dve", lambda e: e.tensor_tensor(out=sacc[0][:, 0, :n], in0=sacc[0][:, 0, :n], in1=sacc[0][:, 1, :n], op=ALU.add),
                              reads=["sacc0"], writes=["sacc0"])
                        P.add("dve", lambda e: e.tensor_tensor(out=ssb[:, :n], in0=sacc[0][:, 0, :n], in1=sacc[1][:, 0, :n], op=ALU.add),
                              reads=["sacc0", "sacc1"], writes=["ssb"])
                    else:
                        P.add("dve", lambda e: e.tensor_tensor(out=ssb[:, :n], in0=sacc[0][:, 0, :n], in1=sacc[0][:, 1, :n], op=ALU.add),
                              reads=["sacc0"], writes=["ssb"])
                    bs = 2 * (cnt["s"] % 3)
                    cnt["s"] += 1
                    P.add("pe", lambda e: e.matmul(ps[:, bs, :n], lhsT=ones[:], rhs=ssb[:, :n], start=True, stop=True),
                          reads=["ssb"], writes=[f"ps{bs}"])
                    P.add("dve", lambda e: e.reciprocal(out=rec[ab][:, :n], in_=ps[:, bs, :n]),
                          reads=[f"ps{bs}"], writes=[f"rec{ab}"])
                    P.add("dve", lambda e: e.tensor_tensor(out=qa[:, h, c0:c0 + n], in0=ps[:, bo, :n], in1=rec[ab][:, :n], op=ALU.mult),
                          reads=[f"acc{ab}", f"rec{ab}"], writes=[qtok, f"acc{ab}"])
                    P.dma("sp", f"sta{it % 4}", mixbuf[:, h, c0:c0 + n], qa[:, h, c0:c0 + n], reads=[qtok])
                for it, (h, c0, n, kts) in enumerate(iters):
                    do_iter(it, h, c0, n, kts)
                P.emit(nc)
            return stop_after == f"B_{l}"

        def phase_C(l):
            last = (l == DEPTH - 1)
            with contextlib.ExitStack() as st:
                mix = sb("mix", [128, KC, T], BF16, st)
                wr = [sb(f"wo{i}", [128, KC, 128], BF16, st) for i in range(3)]
                xs = [sb(f"xc{i}", [128, 512], F32, st) for i in range(4)]
                src = xT if l == 0 else xres
                tiles = TL[1:] if last else TL
                P.dma("sp", "ldmix", mix[:], mixbuf, writes=["mix"])
                k = 0
                for m in range(KC):
                    ws = m % 3
                    P.dma("pool", f"wo{ws}", wr[ws][:], wout_(l)[m], writes=[f"wo{ws}"])
                    for (c0, n) in tiles:
                        s = 1 if c0 < NCTX else 0
                        xi = k % 4
                        psb = k % 4
                        k += 1
                        P.dma("sp", f"xc{xi}", xs[xi][:, :n], src[:, m, c0:c0 + n], writes=[f"xc{xi}"])

                        def fn(e, ws=ws, psb=psb, c0=c0, n=n):
                            last_ = None
                            for kc in range(KC):
                                last_ = e.matmul(ps[:, psb, :n], lhsT=wr[ws][:, kc, :], rhs=mix[:, kc, c0:c0 + n], start=(kc == 0), stop=(kc == KC - 1))
                            return last_
                        P.add("pe", fn, reads=[f"wo{ws}", "mix"], writes=[f"ps{psb}"])
                        P.add("dve", lambda e, xi=xi, psb=psb, n=n, m=m, s=s: e.scalar_tensor_tensor(
                            out=xs[xi][:, :n], in0=ps[:, psb, :n], scalar=mv(l, 2, m, s), in1=xs[xi][:, :n], op0=ALU.mult, op1=ALU.add),
                            reads=[f"ps{psb}", f"xc{xi}"], writes=[f"xc{xi}"])
                        P.dma("sp", f"xo{xi}", xres[:, m, c0:c0 + n], xs[xi][:, :n], reads=[f"xc{xi}"])
                P.emit(nc)
            return stop_after == f"C_{l}"

        def phase_D(l):
            last = (l == DEPTH - 1)
            halves = [TL[1:3], TL[3:5]] if last else [TL[0:3], TL[3:5]]
            G = 4

            def blk(a0, n):
                return list(range(a0 // 256, (a0 + n) // 256))

            def acct(a0, n):
                return [f"acc:{k}" for k in blk(a0, n)]

            def accwt(m, a0, n):
                return [f"accw:{m}:{k}" for k in blk(a0, n)]
            with contextlib.ExitStack() as st:
                acc = sb("acc", [128, KC, 1280], F32, st)
                h2 = sb("h2", [128, KC, 1280], BF16, st)
                w1 = [sb(f"w1_{i}", [128, KC, 128], BF16, st) for i in range(6)]
                w2 = [sb(f"w2_{i}", [128, D], BF16, st) for i in range(6)]
                ut = [sb(f"ut{i}", [128, G, 512], BF16, st) for i in range(2)]
                rl = [sb(f"rl{i}", [128, 512], F32, st) for i in range(2)]
                nb = NormBufs(st, "d", 256)
                for hi, tiles in enumerate(halves):
                    hc0 = tiles[0][0]
                    hn = sum(n for _, n in tiles)
                    acctoks = [f"accw:{m}:{c0 - hc0}" for m in range(KC) for (c0, n) in tiles]
                    for ti, (c0_, n_) in enumerate(tiles):
                        P.dma("sp", f"ldacc{ti}", acc[:, :, c0_ - hc0:c0_ - hc0 + n_], xres[:, :, c0_:c0_ + n_],
                              writes=acct(c0_ - hc0, n_) + [t_ for m in range(KC) for t_ in accwt(m, c0_ - hc0, n_)])
                    k = 0
                    for (c0, n) in tiles:
                        s = 1 if c0 < NCTX else 0
                        for o in range(0, n, 256):
                            a0 = c0 - hc0 + o
                            emit_norm(nb, lambda c, a0=a0: acc[:, c, a0:a0 + 256], 256,
                                      lambda c, a0=a0: h2[:, c, a0:a0 + 256], l, 1, s, k % 2, acct(a0, 256), [f"h2:{a0}"])
                            k += 1
                    cnt = {"u": 0, "y": 0, "ut": 0, "rl": 0}
                    for fg in range(NF // G):
                        slots = []
                        for fi in range(G):
                            f = fg * G + fi
                            sl = f % 6
                            slots.append(sl)
                            P.dma("pool", f"w1_{sl}", w1[sl][:], wff1_(l)[f], writes=[f"w1_{sl}"])
                            P.dma("pool", f"w2_{sl}", w2[sl][:], wff2_(l)[f], writes=[f"w2_{sl}"])
                        for (c0, n) in tiles:
                            s = 1 if c0 < NCTX else 0
                            a0 = c0 - hc0
                            h2toks = [f"h2:{a0 + o}" for o in range(0, n, 256)]
                            ui = cnt["ut"] % 2
                            cnt["ut"] += 1
                            for fi in range(G):
                                sl = slots[fi]
                                ub = cnt["u"] % 4
                                cnt["u"] += 1

                                def fn(e, sl=sl, ub=ub, a0=a0, n=n):
                                    last_ = None
                                    for kc in range(KC):
                                        last_ = e.matmul(ps[:, ub, :n], lhsT=w1[sl][:, kc, :], rhs=h2[:, kc, a0:a0 + n], start=(kc == 0), stop=(kc == KC - 1))
                                    return last_
                                P.add("pe", fn, reads=[f"w1_{sl}"] + h2toks, writes=[f"ps{ub}"])
                                ri = cnt["rl"] % 2
                                cnt["rl"] += 1
                                P.add("act", lambda e, ub=ub, ri=ri, n=n: e.activation(out=rl[ri][:, :n], in_=ps[:, ub, :n], func=AF.Relu),
                                      reads=[f"ps{ub}"], writes=[f"rl{ri}"])
                                P.add("dve", lambda e, ri=ri, ui=ui, fi=fi, n=n: e.tensor_tensor(out=ut[ui][:, fi, :n], in0=rl[ri][:, :n], in1=rl[ri][:, :n], op=ALU.mult),
                                      reads=[f"rl{ri}"], writes=[f"ut{ui}"])
                            for m in range(KC):
                                yb = 4 + cnt["y"] % 4
                                cnt["y"] += 1

                                def fn2(e, yb=yb, m=m, ui=ui, n=n, slots=tuple(slots)):
                                    last_ = None
                                    for fi in range(G):
                                        last_ = e.matmul(ps[:, yb, :n], lhsT=w2[slots[fi]][:, m * 128:(m + 1) * 128], rhs=ut[ui][:, fi, :n],
                                                         start=(fi == 0), stop=(fi == G - 1))
                                    return last_
                                P.add("pe", fn2, reads=[f"w2_{sl}" for sl in slots] + [f"ut{ui}"], writes=[f"ps{yb}"])
                                P.add("dve", lambda e, yb=yb, m=m, a0=a0, n=n, s=s: e.scalar_tensor_tensor(
                                    out=acc[:, m, a0:a0 + n], in0=ps[:, yb, :n], scalar=mv(l, 5, m, s), in1=acc[:, m, a0:a0 + n],
                                    op0=ALU.mult, op1=ALU.add), reads=[f"ps{yb}"] + accwt(m, a0, n), writes=accwt(m, a0, n))
                    for ti, (c0_, n_) in enumerate(tiles):
                        a_ = c0_ - hc0
                        tt = [t_ for m in range(KC) for t_ in accwt(m, a_, n_)] + acct(a_, n_)
                        dst = outT[:, :, c0_ - NCTX:c0_ - NCTX + n_] if last else xres[:, :, c0_:c0_ + n_]
                        P.dma("sp", f"stout{ti}", dst, acc[:, :, a_:a_ + n_], reads=tt, writes=[f"accst{ti}"])
                    if not last:
                        ed = edge_in.rearrange("p (c t) -> p c t", t=16)
                        if hi == 0:
                            a = NCTX - hc0
                            P.dma("sp", "edge0", ed[:, :, 0:8], acc[:, :, a:a + 8],
                                  reads=[t_ for m in range(KC) for t_ in accwt(m, a, 256)] + acct(a, 256), writes=["edge0"])
                        else:
                            a = T - 8 - hc0
                            a_t = (a // 256) * 256
                            P.dma("sp", "edge1", ed[:, :, 8:16], acc[:, :, a:a + 8],
                                  reads=[t_ for m in range(KC) for t_ in accwt(m, a_t, 256)] + acct(a_t, 256), writes=["edge1"])
                if not last:
                    P.add("pool", lambda e: e.collective_compute("AllGather", ALU.bypass, replica_groups=RG, ins=[edge_in], outs=[edge_all]),
                          reads=["edge0", "edge1"], writes=["edge_all"], final_wait=True)
                P.emit(nc)
            return stop_after == f"D_{l}"

        phase_M()
        done = stop_after == "M"
        for l in range(nl):
            if done:
                break
            done = phase_A(l)
            if done:
                break
            done = phase_B(l)
            if done:
                break
            done = phase_C(l)
            if done:
                break
            done = phase_D(l)
        if done or True:
            pass
    return nc


def _perm():
    i = np.arange(128)
    return np.where((i // 32) % 2 == 0, i + 32, i - 32)


def _rope_tables(pos):
    n = pos.shape[0]
    row = (pos // 64).astype(np.float32)
    col = (pos % 64).astype(np.float32)
    inv = (np.float32(10000.0) ** (-(np.arange(32, dtype=np.float32)) / np.float32(32))).astype(np.float32)
    ar = row[:, None] * inv
    ac = col[:, None] * inv
    ang = np.concatenate([ar, ar, ac, ac], -1)
    return np.cos(ang).astype(np.float32), np.sin(ang).astype(np.float32)


def prep_shared(inp):
    f = np.float32
    sh = {}
    for l in range(DEPTH):
        sh[f"wmod{l}"] = np.ascontiguousarray(inp["w_mod"][l].reshape(16, 128, 96, 128).transpose(2, 1, 0, 3))
        sh[f"win{l}"] = np.ascontiguousarray(inp["w_in"][l].reshape(16, 128, 28, 128).transpose(2, 1, 0, 3))
        sh[f"wout{l}"] = np.ascontiguousarray(inp["w_out"][l].reshape(16, 128, 16, 128).transpose(2, 1, 0, 3))
        sh[f"wff1{l}"] = np.ascontiguousarray(inp["w_ff1"][l].reshape(16, 128, 64, 128).transpose(2, 1, 0, 3))
        sh[f"wff2{l}"] = np.ascontiguousarray(inp["w_ff2"][l].reshape(64, 128, 2048))
    bm = inp["b_mod"].reshape(2, 96, 128).transpose(2, 0, 1)
    sh["bmod"] = np.ascontiguousarray(np.repeat(bm[..., None], 2, -1)).astype(f)
    g = np.stack([inp["norm1_g"], inp["norm2_g"]], 1)
    sh["g12"] = np.ascontiguousarray(g.reshape(2, 2, 16, 128).transpose(3, 0, 1, 2)).astype(f)
    pm = _perm()
    qg, kg = inp["q_norm_g"], inp["k_norm_g"]
    sh["qkg"] = np.ascontiguousarray(np.stack([qg, qg[:, pm], kg, kg[:, pm]], -1).transpose(1, 0, 2)).astype(f)
    rm = np.zeros((128, 128), np.float32)
    rm[pm, np.arange(128)] = 1.0
    sh["rmat"] = rm.astype(ml_dtypes.bfloat16)
    sh["wpool"] = np.ascontiguousarray(inp["w_pool"].transpose(2, 0, 1, 3)).astype(f)
    sh["pscale"] = np.ascontiguousarray(inp["pool_scale"].reshape(2, 4, 128).transpose(2, 0, 1)).astype(f)
    sh["convw"] = np.ascontiguousarray(inp["conv_w"].reshape(2, 3, 4, 128).transpose(3, 0, 2, 1)).astype(f)
    return sh


def prep_core(inp, r):
    f = np.float32
    b, j = r // 4, r % 4
    x = inp["x"]
    lo, hi = j * NLAT, (j + 1) * NLAT
    cat = np.concatenate([inp["ctx"][b], x[b, lo:hi]], 0)
    m = {}
    m["xT"] = np.ascontiguousarray(cat.reshape(T, 16, 128).transpose(2, 1, 0))
    hl = x[b, lo - 8:lo] if j > 0 else np.zeros((8, D), f)
    hr = x[b, hi:hi + 8] if j < 3 else np.zeros((8, D), f)
    m["xh"] = np.ascontiguousarray(np.concatenate([hl, hr], 0).reshape(16, 16, 128).transpose(2, 1, 0)).astype(f)
    cv = np.stack([inp["c"][b], inp["c_ctx"]], -1)
    m["cv"] = np.ascontiguousarray(cv.reshape(16, 128, 2).transpose(1, 0, 2)).astype(f)
    cos, sin = _rope_tables(np.arange(lo, hi))
    sign = np.where((np.arange(128) // 32) % 2 == 0, -1.0, 1.0).astype(f)
    cs = np.zeros((128, 2, T), f)
    cs[:, 0, :NCTX] = 1.0
    cs[:, 0, NCTX:] = cos.T
    cs[:, 1, NCTX:] = (sin * sign[None, :]).T
    m["cs"] = cs
    pt = np.zeros((4, 4, 8), f)

    def inv_cnt(tpos, n, w):
        lo_ = w // 2
        hi_ = w - 1 - lo_
        st = np.clip(tpos - lo_, 0, n)
        en = np.clip(tpos + hi_ + 1, 0, n)
        return (1.0 / (en - st)).astype(f)
    for g, w in enumerate((2, 4, 8, 16)):
        pt[g, 0] = inv_cnt(np.arange(0, 8), NCTX, w)
        pt[g, 1] = inv_cnt(np.arange(NCTX - 8, NCTX), NCTX, w)
        pt[g, 2] = inv_cnt(np.arange(lo, lo + 8), 4 * NLAT, w)
        pt[g, 3] = inv_cnt(np.arange(hi - 8, hi), 4 * NLAT, w)
    m["ptab"] = np.ascontiguousarray(np.broadcast_to(pt[None], (128, 4, 4, 8))).astype(f)
    hm = np.zeros((16,), f)
    hm[0:8] = 1.0 if j > 0 else 0.0
    hm[8:16] = 1.0 if j < 3 else 0.0
    m["hmask"] = np.ascontiguousarray(np.broadcast_to(hm[None], (128, 16))).astype(f)
    hs = np.zeros((8,), f)
    if j > 0:
        hs[j - 1] = 1.0
    if j < 3:
        hs[4 + j + 1] = 1.0
    m["hsel"] = np.ascontiguousarray(np.broadcast_to(hs[None], (128, 8))).astype(f)
    return m


def make_in_maps(inp):
    inp = {k: np.asarray(v) for k, v in inp.items()}
    sh = prep_shared(inp)
    maps = []
    for r in range(8):
        m = prep_core(inp, r)
        m.update(sh)
        maps.append(m)
    return maps


def assemble(results):
    out = np.empty((2, 4 * NLAT, D), np.float32)
    for r in range(8):
        b, j = r // 4, r % 4
        o = np.asarray(results[r]["outT"])
        out[b, j * NLAT:(j + 1) * NLAT] = o.transpose(2, 1, 0).reshape(NLAT, D)
    return out


def kernel(**inputs):
    maps = make_in_maps(inputs)
    nc = build()
    res = run_bass_kernel_spmd(nc, maps, core_ids=list(range(8)))
    return assemble(res.results)
```

```python
import contextlib
import math
import numpy as np
import ml_dtypes
import concourse.bass as bass
import concourse.mybir as mybir
from concourse.bass_utils import run_bass_kernel_spmd

F32 = mybir.dt.float32
BF16 = mybir.dt.bfloat16
ALU = mybir.AluOpType
AF = mybir.ActivationFunctionType

D = 2048
KC = 16
NCTX = 256
NLAT = 2048
T = NCTX + NLAT
DEPTH = 2
NF = 64
EPS = 1e-6
SCALE = 1.0 / math.sqrt(128.0)
TL = [(0, 256), (256, 512), (768, 512), (1280, 512), (1792, 512)]
WN = 2336
ENGS = ("pe", "act", "dve", "pool", "sp")
SUM_OFFLOAD = True


class _Op:
    __slots__ = ("eng", "fn", "reads", "writes", "dma_key", "deps", "idx", "need_inc", "cnt", "dma_val", "final_wait")


class Prog:
    def __init__(self):
        self.ops = []
        self.last_w = {}
        self.readers = {}
        self.nblk = 0

    def add(self, eng, fn, reads=(), writes=(), dma_key=None, final_wait=False):
        op = _Op()
        op.final_wait = final_wait
        op.eng, op.fn, op.reads, op.writes, op.dma_key = eng, fn, tuple(reads), tuple(writes), dma_key
        op.deps = set()
        op.need_inc = False
        op.cnt = None
        op.dma_val = None
        op.idx = len(self.ops)
        for r in op.reads:
            w = self.last_w.get(r)
            if w is not None:
                op.deps.add(w)
        for t in op.writes:
            w = self.last_w.get(t)
            if w is not None:
                op.deps.add(w)
            for rd in self.readers.get(t, {}).values():
                if rd is not op:
                    op.deps.add(rd)
        for r in op.reads:
            d = self.readers.setdefault(r, {})
            k = ("dma", op.idx) if dma_key is not None else op.eng
            d[k] = op
        for t in op.writes:
            self.last_w[t] = op
            self.readers[t] = {}
        if dma_key is None:
            drop = set()
            for d in op.deps:
                if d.dma_key is None and d.eng == op.eng and op.eng == "pe":
                    if not any(r in d.writes for r in op.reads):
                        drop.add(d)
            op.deps -= drop
        self.ops.append(op)
        return op

    def dma(self, eng, key, out, in_, reads=(), writes=(), **kw):
        def fn(e):
            return e.dma_start(out=out, in_=in_, **kw)
        return self.add(eng, fn, reads, writes, dma_key=key)

    def setup(self, nc, stack):
        self.nc = nc
        self.stack = stack
        self.esem = {e: stack.enter_context(nc.semaphore(f"s_{e}")) for e in ENGS}
        self.dsem = {}
        self.cnt = {e: 0 for e in ENGS}
        self.dcnt = {}
        self.waited = {e: {} for e in ENGS}

    def emit(self, nc):
        ops = self.ops
        self.nblk += 1
        for op in ops:
            if op.final_wait:
                op.need_inc = True
            for d in op.deps:
                d.need_inc = True
        cnt, dcnt, esem, dsem = self.cnt, self.dcnt, self.esem, self.dsem
        for op in ops:
            if op.dma_key is not None:
                if op.dma_key not in dsem:
                    dsem[op.dma_key] = self.stack.enter_context(nc.semaphore(f"d_{len(dsem)}"))
                dcnt[op.dma_key] = dcnt.get(op.dma_key, 0) + 16
                op.dma_val = dcnt[op.dma_key]
            elif op.need_inc:
                cnt[op.eng] += 1
                op.cnt = cnt[op.eng]
        with nc.Block() as block:
            by_eng = {e: [op for op in ops if op.eng == e] for e in ENGS}

            def run(engname, engobj):
                waited = self.waited[engname]
                lst = by_eng[engname]
                for oi, op in enumerate(lst):
                    need = {}
                    deps = list(op.deps)
                    if engname == "pe" and oi + 1 < len(lst):
                        deps += [d for d in lst[oi + 1].deps if d.idx < op.idx]
                    for d in deps:
                        if d.dma_key is not None:
                            s, v = dsem[d.dma_key], d.dma_val
                        else:
                            s, v = esem[d.eng], d.cnt
                        k = id(s)
                        if need.get(k, (None, 0))[1] < v:
                            need[k] = (s, v)
                    for k, (s, v) in need.items():
                        if waited.get(k, 0) < v:
                            engobj.wait_ge(s, v)
                            waited[k] = v
                    ins = op.fn(engobj)
                    if op.dma_key is not None:
                        ins.then_inc(dsem[op.dma_key], 16)
                    elif op.need_inc:
                        ins.then_inc(esem[op.eng], 1)
                fw = [op.cnt for op in by_eng[engname] if op.final_wait]
                if fw and waited.get(id(esem[engname]), 0) < max(fw):
                    engobj.wait_ge(esem[engname], max(fw))
                    waited[id(esem[engname])] = max(fw)
                last = {}
                for op in by_eng[engname]:
                    if op.dma_key is not None:
                        last[op.dma_key] = op.dma_val
                for k, v in last.items():
                    if waited.get(id(dsem[k]), 0) < v:
                        engobj.wait_ge(dsem[k], v)
                        waited[id(dsem[k])] = v

            @block.tensor
            def _(e):
                run("pe", e)

            @block.scalar
            def _(e):
                run("act", e)

            @block.vector
            def _(e):
                run("dve", e)

            @block.gpsimd
            def _(e):
                run("pool", e)

            @block.sync
            def _(e):
                run("sp", e)
        self.ops = []
        self.last_w = {}
        self.readers = {}


def wcol(c0):
    return 8 + c0 if c0 < NCTX else c0 + 24


def build(stop_after=None, debug=False, ncores=8, nl=DEPTH):
    nc = bass.Bass("TRN2", target_bir_lowering=False)

    def din(name, shape, dt=F32):
        return nc.dram_tensor(name, list(shape), dt, kind="ExternalInput").ap()

    def dint(name, shape, dt, kind="Internal"):
        return nc.dram_tensor(name, list(shape), dt, kind=kind).ap()

    xT = din("xT", [128, KC, T])
    xh = din("xh", [128, KC, 16])
    cvd = din("cv", [128, KC, 2])
    _wc = {}

    def wl(name, l, shape):
        key = f"{name}{l}"
        if key not in _wc:
            _wc[key] = din(key, shape)
        return _wc[key]

    def wmod_(l):
        return wl("wmod", l, [96, 128, KC, 128])

    def win_(l):
        return wl("win", l, [28, 128, KC, 128])

    def wout_(l):
        return wl("wout", l, [KC, 128, KC, 128])

    def wff1_(l):
        return wl("wff1", l, [NF, 128, KC, 128])

    def wff2_(l):
        return wl("wff2", l, [NF, 128, D])
    bmod = din("bmod", [128, DEPTH, 96, 2])
    g12 = din("g12", [128, DEPTH, 2, KC])
    qkg = din("qkg", [128, DEPTH, 4])
    csd = din("cs", [128, 2, T])
    rmatd = din("rmat", [128, 128], BF16)
    wpoold = din("wpool", [128, DEPTH, 4, 128])
    pscaled = din("pscale", [128, DEPTH, 4])
    convwd = din("convw", [128, DEPTH, 4, 3])
    ptabd = din("ptab", [128, 4, 4, 8])
    hmaskd = din("hmask", [128, 16])
    hseld = din("hsel", [128, 8])
    outT = nc.dram_tensor("outT", [128, KC, NLAT], F32, kind="ExternalOutput").ap()

    dk = "ExternalOutput" if debug else "Internal"
    xres = dint("xres", [128, KC, T], F32, dk)
    qbuf = dint("qbuf", [128, 8, T], BF16, dk)
    mixbuf = dint("mixbuf", [128, KC, T], BF16, dk)
    kvc = dint("kvc", [128, 2, 2, 256], BF16, dk)
    hdbg = dint("hdbg", [128, KC, T + 16], BF16, dk) if debug else None
    moddbg = dint("moddbg", [128, DEPTH, 96, 2], F32, dk) if debug else None
    kv_in = dint("kv_in", [256, 4096], BF16)
    kv_all = dint("kv_all", [2, 512, 4096], BF16)
    kvdbg = dint("kvdbg", [2, 512, 4096], BF16, dk) if debug else None
    edge_in = dint("edge_in", [128, 256], F32)
    edge_all = dint("edge_all", [512, 256], F32)
    RG = [[0, 1, 2, 3], [4, 5, 6, 7]] if ncores == 8 else [[0, 1, 2, 3]]

    P = Prog()
    es = contextlib.ExitStack()
    _nm = [0]
    with es:
        P.setup(nc, es)

        def sb(name, shape, dt=F32, stack=es):
            _nm[0] += 1
            return stack.enter_context(nc.sbuf_tensor(f"{name}_{_nm[0]}", list(shape), dt))

        ps = es.enter_context(nc.psum_tensor("ps", [128, 8, 512], F32))
        modsb = sb("modsb", [128, DEPTH, 96, 2])
        A12 = sb("A12", [128, DEPTH, 2, KC, 2])
        g12s = sb("g12s", [128, DEPTH, 2, KC])
        qkgs = sb("qkgs", [128, DEPTH, 4])
        pscs = sb("pscs", [128, DEPTH, 4])
        cws = sb("cws", [128, DEPTH, 4, 3])
        ptabs = sb("ptabs", [128, 4, 4, 8])
        hmasks = sb("hmasks", [128, 16])
        hsels = sb("hsels", [128, 8])
        ones = sb("ones", [128, 128], BF16)
        rmats = sb("rmats", [128, 128], BF16)
        wpools = sb("wpools", [128, DEPTH, 4, 128], BF16)

        def mv(l, k, c, s):
            return modsb[:, l, k * 16 + c, s:s + 1]

        def phase_M():
            with contextlib.ExitStack() as st:
                cvs = sb("cvs", [128, KC, 2], F32, st)
                cact = sb("cact", [128, KC, 2], BF16, st)
                bms = sb("bms", [128, DEPTH, 96, 2], F32, st)
                tmpA = sb("tmpA", [128, KC, 2], F32, st)
                wm = [sb(f"wm{i}", [128, 4, KC, 128], BF16, st) for i in range(2)]
                P.dma("sp", "ld_cv", cvs[:], cvd, writes=["cvs"])
                P.dma("sp", "ld_bm", bms[:], bmod, writes=["bms"])
                P.dma("sp", "ld_g12", g12s[:], g12, writes=["g12s"])
                P.dma("sp", "ld_qkg", qkgs[:], qkg, writes=["qkgs"])
                P.dma("sp", "ld_psc", pscs[:], pscaled, writes=["pscs"])
                P.dma("sp", "ld_cw", cws[:], convwd, writes=["cws"])
                P.dma("sp", "ld_ptab", ptabs[:], ptabd, writes=["ptabs"])
                P.dma("sp", "ld_hm", hmasks[:], hmaskd, writes=["hmasks"])
                P.dma("sp", "ld_hs", hsels[:], hseld, writes=["hsels"])
                P.dma("sp", "ld_rm", rmats[:], rmatd, writes=["rmats"])
                P.dma("pool", "ld_wp", wpools[:], wpoold, writes=["wpools"])
                P.add("dve", lambda e: e.memset(ones[:], 1.0), writes=["ones"])
                P.add("act", lambda e: e.activation(out=cact[:], in_=cvs[:], func=AF.Silu),
                      reads=["cvs"], writes=["cact"])
                for l in range(nl):
                    psm = ps[:, l, 0:192].rearrange("p (e s) -> p e s", s=2)
                    for eg in range(24):
                        slot = (l * 24 + eg) % 2
                        P.dma("pool", f"wm{slot}", wm[slot][:],
                              wmod_(l)[eg * 4:(eg + 1) * 4].rearrange("e p k m -> p e k m"),
                              writes=[f"wm{slot}"])

                        def mm(e, slot=slot, eg=eg, psm=psm):
                            last = None
                            for e4 in range(4):
                                for kc in range(KC):
                                    last = e.matmul(psm[:, eg * 4 + e4, :], lhsT=wm[slot][:, e4, kc, :],
                                                    rhs=cact[:, kc, :], start=(kc == 0), stop=(kc == KC - 1))
                            return last
                        P.add("pe", mm, reads=[f"wm{slot}", "cact"], writes=[f"psmod{l}"])
                    P.add("dve", lambda e, l=l, psm=psm: e.tensor_tensor(out=modsb[:, l], in0=psm, in1=bms[:, l], op=ALU.add),
                          reads=[f"psmod{l}", "bms"], writes=["modsb"])
                    for w, ksc in enumerate((1, 4)):
                        P.add("dve", lambda e, l=l, ksc=ksc: e.tensor_scalar(
                            out=tmpA[:], in0=modsb[:, l, ksc * 16:(ksc + 1) * 16, :], scalar1=1.0, scalar2=None, op0=ALU.add),
                            reads=["modsb"], writes=["tmpA"])
                        P.add("dve", lambda e, l=l, w=w: e.tensor_tensor(
                            out=A12[:, l, w], in0=tmpA[:],
                            in1=g12s[:, l, w, :].unsqueeze(2).broadcast_to([128, KC, 2]), op=ALU.mult),
                            reads=["tmpA", "g12s"], writes=["A12"])
                if debug:
                    P.dma("sp", "dbg_mod", moddbg[:, 0:nl], modsb[:, 0:nl], reads=["modsb"])
                P.emit(nc)

        class NormBufs:
            def __init__(self, st, tag, n):
                self.nr = 6
                self.sq = [sb(f"nsq{tag}{i}", [128, n], BF16, st) for i in range(self.nr)]
                self.tc = [sb(f"ntc{tag}{i}", [128, n], F32, st) for i in range(self.nr)]
                self.rt = sb(f"nrt{tag}", [128, n], F32, st)
                self.rstd = sb(f"nrstd{tag}", [128, n], F32, st)
                self.cnt = 0
                self.tag = tag

        def emit_norm(nb, xc_fn, n, out_fn, l, w, s, psb, rtok, wtok):
            tg = nb.tag
            for c in range(KC):
                i = nb.cnt % nb.nr
                nb.cnt += 1
                P.add("act", lambda e, c=c, i=i: e.activation(out=nb.sq[i][:, :n], in_=xc_fn(c), func=AF.Square),
                      reads=rtok, writes=[f"nsq{tg}{i}"])
                P.add("pe", lambda e, c=c, i=i: e.matmul(ps[:, psb, :n], lhsT=ones[:], rhs=nb.sq[i][:, :n],
                                                         start=(c == 0), stop=(c == KC - 1)),
                      reads=[f"nsq{tg}{i}", "ones"], writes=[f"ps{psb}"])
            P.add("act", lambda e: e.activation(out=nb.rt[:, :n], in_=ps[:, psb, :n], func=AF.Sqrt, bias=EPS, scale=1.0 / D),
                  reads=[f"ps{psb}"], writes=[f"nrt{tg}"])
            P.add("dve", lambda e: e.reciprocal(out=nb.rstd[:, :n], in_=nb.rt[:, :n]),
                  reads=[f"nrt{tg}"], writes=[f"nrstd{tg}"])
            ksh = 0 if w == 0 else 3
            for c in range(KC):
                i = nb.cnt % nb.nr
                nb.cnt += 1
                P.add("dve", lambda e, c=c, i=i: e.tensor_tensor(out=nb.tc[i][:, :n], in0=xc_fn(c), in1=nb.rstd[:, :n], op=ALU.mult),
                      reads=list(rtok) + [f"nrstd{tg}"], writes=[f"ntc{tg}{i}"])
                P.add("act", lambda e, c=c, i=i: e.activation(out=out_fn(c), in_=nb.tc[i][:, :n], func=AF.Identity,
                                                              bias=mv(l, ksh, c, s), scale=A12[:, l, w, c, s:s + 1]),
                      reads=[f"ntc{tg}{i}"], writes=wtok)

        def phase_A(l):
            last = (l == DEPTH - 1)
            with contextlib.ExitStack() as stA:
                hT = sb("hT", [128, KC, T + 16], BF16, stA)
                with contextlib.ExitStack() as st:
                    xs = [sb(f"xs{i}", [128, KC, 512], F32, st) for i in range(2)]
                    xhs = sb("xhs", [128, KC, 16], F32, st)
                    nb = NormBufs(st, "a", 512)
                    src = xT if l == 0 else xres
                    for i, (c0, n) in enumerate(TL):
                        slot = i % 2
                        s = 1 if c0 < NCTX else 0
                        P.dma("sp", f"xs{slot}", xs[slot][:, :, :n], src[:, :, c0:c0 + n], writes=[f"xs{slot}"])
                        emit_norm(nb, lambda c, slot=slot, n=n: xs[slot][:, c, :n], n,
                                  lambda c, c0=c0, n=n: hT[:, c, c0:c0 + n], l, 0, s, i % 2,
                                  [f"xs{slot}"], [f"hT{c0}"])
                    if l == 0:
                        P.dma("sp", "xhs", xhs[:], xh, writes=["xhs"])
                    else:
                        egs = sb("egs", [128, 4, KC, 16], F32, st)
                        P.dma("sp", "egs", egs[:], edge_all.rearrange("(r p) (c t) -> p r c t", p=128, t=16), writes=["egs"])
                        for side in range(2):
                            dst = xhs[:, :, side * 8:(side + 1) * 8]
                            for r in range(4):
                                srcc = egs[:, r, :, (1 - side) * 8:(2 - side) * 8]
                                sc = hsels[:, side * 4 + r:side * 4 + r + 1]
                                if r == 0:
                                    P.add("dve", lambda e, dst=dst, srcc=srcc, sc=sc: e.tensor_scalar(
                                        out=dst, in0=srcc, scalar1=sc, scalar2=None, op0=ALU.mult),
                                        reads=["egs"], writes=["xhs"])
                                else:
                                    P.add("dve", lambda e, dst=dst, srcc=srcc, sc=sc: e.scalar_tensor_tensor(
                                        out=dst, in0=srcc, scalar=sc, in1=dst, op0=ALU.mult, op1=ALU.add),
                                        reads=["egs", "xhs"], writes=["xhs"])
                    emit_norm(nb, lambda c: xhs[:, c, :], 16, lambda c: hT[:, c, T:T + 16], l, 0, 0, 2, ["xhs"], ["hTh"])
                    if debug and l == 0:
                        P.dma("sp", "dbg_h", hdbg, hT[:], reads=[f"hT{c0}" for c0, _ in TL] + ["hTh"])
                    P.emit(nc)
                if stop_after == f"A0_{l}":
                    return True
                with contextlib.ExitStack() as st:
                    wr = [sb(f"wr{i}", [128, KC, 128], BF16, st) for i in range(3)]
                    wv = sb("wv", [128, KC, 256], BF16, st)
                    css = sb("css", [128, 2, T], F32, st)
                    W = [sb(f"W{i}", [128, WN], F32, st) for i in range(4)]
                    Dbf = sb("Dbf", [128, WN], BF16, st)
                    tmp8 = sb("tmp8", [128, 8], F32, st)
                    sqb = [sb(f"sqb{i}", [128, 512], BF16, st) for i in range(2)]
                    qbb = [sb(f"qbb{i}", [128, 512], BF16, st) for i in range(2)]
                    rt = [sb(f"rt{i}", [128, 512], F32, st) for i in range(2)]
                    rstd = [sb(f"rstd{i}", [128, 512], F32, st) for i in range(2)]
                    u1 = [sb(f"u1{i}", [128, 512], F32, st) for i in range(2)]
                    u2 = [sb(f"u2{i}", [128, 512], F32, st) for i in range(2)]
                    ost = [sb(f"ost{i}", [128, 512], BF16, st) for i in range(3)]
                    P.dma("sp", "ld_cs", css[:], csd, writes=["css"])
                    for i in range(4):
                        P.add("dve", lambda e, i=i: e.memset(W[i][:], 0.0), writes=[f"W{i}"])
                    cnt = {"w": 0, "mb": 0, "ep": 0, "ost": 0, "aux": 0}

                    def load_w(chunk):
                        slot = cnt["w"] % 3
                        cnt["w"] += 1
                        P.dma("pool", f"wr{slot}", wr[slot][:], win_(l)[chunk], writes=[f"wr{slot}"])
                        return slot

                    def mm_tile(wslot, c0, n):
                        psb = cnt["mb"] % 4
                        cnt["mb"] += 1

                        def fn(e):
                            last = None
                            for kc in range(KC):
                                last = e.matmul(ps[:, psb, :n], lhsT=wr[wslot][:, kc, :], rhs=hT[:, kc, c0:c0 + n],
                                                start=(kc == 0), stop=(kc == KC - 1))
                            return last
                        P.add("pe", fn, reads=[f"wr{wslot}"], writes=[f"ps{psb}"])
                        return psb

                    def new_ost():
                        o = cnt["ost"] % 3
                        cnt["ost"] += 1
                        return o

                    def qk_chunk(chunk, kind, hidx):
                        wslot = load_w(chunk)
                        gcol = 0 if kind == "q" else 2
                        tiles = TL if (kind == "k" or not last) else TL[1:]
                        for (c0, n) in tiles:
                            psb = mm_tile(wslot, c0, n)
                            i = cnt["ep"] % 2
                            cnt["ep"] += 1
                            ssb, rtb = 4 + i, 6 + i
                            P.add("act", lambda e, psb=psb, i=i, n=n: e.activation(out=sqb[i][:, :n], in_=ps[:, psb, :n], func=AF.Square),
                                  reads=[f"ps{psb}"], writes=[f"sqb{i}"])
                            P.add("act", lambda e, psb=psb, i=i, n=n: e.activation(out=qbb[i][:, :n], in_=ps[:, psb, :n], func=AF.Copy),
                                  reads=[f"ps{psb}"], writes=[f"qbb{i}"])
                            P.add("pe", lambda e, ssb=ssb, i=i, n=n: e.matmul(ps[:, ssb, :n], lhsT=ones[:], rhs=sqb[i][:, :n], start=True, stop=True),
                                  reads=[f"sqb{i}"], writes=[f"ps{ssb}"])
                            P.add("pe", lambda e, rtb=rtb, i=i, n=n: e.matmul(ps[:, rtb, :n], lhsT=rmats[:], rhs=qbb[i][:, :n], start=True, stop=True),
                                  reads=[f"qbb{i}"], writes=[f"ps{rtb}"])
                            P.add("act", lambda e, ssb=ssb, i=i, n=n: e.activation(out=rt[i][:, :n], in_=ps[:, ssb, :n], func=AF.Sqrt,
                                                                                   bias=EPS, scale=1.0 / 128.0),
                                  reads=[f"ps{ssb}"], writes=[f"rt{i}"])
                            P.add("dve", lambda e, i=i, n=n: e.reciprocal(out=rstd[i][:, :n], in_=rt[i][:, :n]),
                                  reads=[f"rt{i}"], writes=[f"rstd{i}"])
                            P.add("dve", lambda e, psb=psb, i=i, n=n, c0=c0: e.scalar_tensor_tensor(
                                out=u1[i][:, :n], in0=ps[:, psb, :n], scalar=qkgs[:, l, gcol:gcol + 1], in1=css[:, 0, c0:c0 + n],
                                op0=ALU.mult, op1=ALU.mult), reads=[f"ps{psb}", "css"], writes=[f"u1{i}"])
                            P.add("dve", lambda e, rtb=rtb, i=i, n=n, c0=c0: e.scalar_tensor_tensor(
                                out=u2[i][:, :n], in0=ps[:, rtb, :n], scalar=qkgs[:, l, gcol + 1:gcol + 2], in1=css[:, 1, c0:c0 + n],
                                op0=ALU.mult, op1=ALU.mult), reads=[f"ps{rtb}", "css"], writes=[f"u2{i}"])
                            P.add("dve", lambda e, i=i, n=n: e.tensor_tensor(out=u1[i][:, :n], in0=u1[i][:, :n], in1=u2[i][:, :n], op=ALU.add),
                                  reads=[f"u1{i}", f"u2{i}"], writes=[f"u1{i}"])
                            o = new_ost()
                            P.add("dve", lambda e, i=i, n=n, o=o: e.tensor_tensor(out=ost[o][:, :n], in0=u1[i][:, :n], in1=rstd[i][:, :n], op=ALU.mult),
                                  reads=[f"u1{i}", f"rstd{i}"], writes=[f"ost{o}"])
                            if kind == "q":
                                dst = qbuf[:, hidx, c0:c0 + n]
                                wt = []
                            elif c0 < NCTX:
                                dst = kvc[:, 0, hidx, :]
                                wt = []
                            else:
                                dst = kv_in[0:128, hidx * NLAT + c0 - NCTX: hidx * NLAT + c0 - NCTX + n]
                                wt = [f"kv_in:k{hidx}:{c0}"]
                                kvtoks.append(wt[0])
                            P.dma("sp", f"ost{o}", dst, ost[o][:, :n], reads=[f"ost{o}"], writes=wt)

                    kvtoks = []

                    def v_chunk():
                        for j in range(2):
                            P.dma("pool", f"wv{j}", wv[:, :, j * 128:(j + 1) * 128], win_(l)[10 + j], writes=["wv"])
                        for tt in range(18):
                            psb = cnt["mb"] % 4
                            cnt["mb"] += 1

                            def fn(e, tt=tt, psb=psb):
                                last = None
                                for kc in range(KC):
                                    last = e.matmul(ps[:, psb, :256], lhsT=hT[:, kc, tt * 128:(tt + 1) * 128], rhs=wv[:, kc, :],
                                                    start=(kc == 0), stop=(kc == KC - 1))
                                return last
                            P.add("pe", fn, reads=["wv"], writes=[f"ps{psb}"])
                            o = new_ost()
                            P.add("act", lambda e, psb=psb, o=o: e.activation(out=ost[o][:, :256], in_=ps[:, psb, :256], func=AF.Copy),
                                  reads=[f"ps{psb}"], writes=[f"ost{o}"])
                            if tt < 2:
                                dst = kvc[:, 1, tt, :]
                                wt = []
                            else:
                                dst = kv_in[128:256, (tt - 2) * 256:(tt - 1) * 256]
                                wt = [f"kv_in:v{tt}"]
                                kvtoks.append(wt[0])
                            P.dma("sp", f"ost{o}", dst, ost[o][:, :256], reads=[f"ost{o}"], writes=wt)

                    mtiles = TL[1:] if last else TL
                    HL = (T, 16)

                    def halo_pair(fn):
                        fn(272, 0)
                        fn(2328, 8)

                    def conv_triple(j):
                        Wa, Wb = W[2 * (j % 2)], W[2 * (j % 2) + 1]
                        ta, tb = f"W{2 * (j % 2)}", f"W{2 * (j % 2) + 1}"
                        wslot = load_w(24 + j)
                        for (c0, n) in mtiles:
                            psb = mm_tile(wslot, c0, n)
                            P.add("act", lambda e, psb=psb, c0=c0, n=n: e.activation(out=Wa[:, wcol(c0):wcol(c0) + n], in_=ps[:, psb, :n], func=AF.Copy),
                                  reads=[f"ps{psb}"], writes=[ta])
                        psb = mm_tile(wslot, *HL)
                        halo_pair(lambda wc, hc, psb=psb: P.add("dve", lambda e: e.tensor_tensor(
                            out=Wa[:, wc:wc + 8], in0=ps[:, psb, hc:hc + 8], in1=hmasks[:, hc:hc + 8], op=ALU.mult),
                            reads=[f"ps{psb}"], writes=[ta]))
                        wslot = load_w(20 + j)
                        for (c0, n) in mtiles:
                            psb = mm_tile(wslot, c0, n)
                            P.add("dve", lambda e, psb=psb, c0=c0, n=n: e.tensor_tensor(
                                out=Wa[:, wcol(c0):wcol(c0) + n], in0=ps[:, psb, :n], in1=Wa[:, wcol(c0):wcol(c0) + n], op=ALU.mult),
                                reads=[f"ps{psb}", ta], writes=[ta])
                        psb = mm_tile(wslot, *HL)
                        halo_pair(lambda wc, hc, psb=psb: P.add("dve", lambda e: e.tensor_tensor(
                            out=Wa[:, wc:wc + 8], in0=ps[:, psb, hc:hc + 8], in1=Wa[:, wc:wc + 8], op=ALU.mult),
                            reads=[f"ps{psb}", ta], writes=[ta]))
                        P.add("dve", lambda e: e.tensor_scalar(out=Wb[:, 1:WN - 1], in0=Wa[:, 1:WN - 1], scalar1=cws[:, l, j, 1:2], scalar2=None, op0=ALU.mult),
                              reads=[ta], writes=[tb])
                        P.add("dve", lambda e: e.scalar_tensor_tensor(out=Wb[:, 1:WN - 1], in0=Wa[:, 0:WN - 2], scalar=cws[:, l, j, 0:1],
                                                                      in1=Wb[:, 1:WN - 1], op0=ALU.mult, op1=ALU.add),
                              reads=[ta, tb], writes=[tb])
                        P.add("dve", lambda e: e.scalar_tensor_tensor(out=Wb[:, 1:WN - 1], in0=Wa[:, 2:WN], scalar=cws[:, l, j, 2:3],
                                                                      in1=Wb[:, 1:WN - 1], op0=ALU.mult, op1=ALU.add),
                              reads=[ta, tb], writes=[tb])
                        wslot = load_w(16 + j)
                        for (c0, n) in mtiles:
                            psb = mm_tile(wslot, c0, n)
                            o = new_ost()
                            P.add("dve", lambda e, psb=psb, c0=c0, n=n, o=o: e.tensor_tensor(
                                out=ost[o][:, :n], in0=ps[:, psb, :n], in1=Wb[:, wcol(c0):wcol(c0) + n], op=ALU.mult),
                                reads=[f"ps{psb}", tb], writes=[f"ost{o}"])
                            P.dma("sp", f"ost{o}", mixbuf[:, 12 + j, c0:c0 + n], ost[o][:, :n], reads=[f"ost{o}"])

                    def pool_chunk(g):
                        U, Bb, Cb = W[0], W[1], W[3]
                        w = (2, 4, 8, 16)[g]
                        wslot = load_w(12 + g)
                        for (c0, n) in mtiles:
                            psb = mm_tile(wslot, c0, n)
                            P.add("act", lambda e, psb=psb, c0=c0, n=n: e.activation(out=U[:, wcol(c0):wcol(c0) + n], in_=ps[:, psb, :n], func=AF.Copy),
                                  reads=[f"ps{psb}"], writes=["W0"])
                        psb = mm_tile(wslot, *HL)
                        halo_pair(lambda wc, hc, psb=psb: P.add("dve", lambda e: e.tensor_tensor(
                            out=U[:, wc:wc + 8], in0=ps[:, psb, hc:hc + 8], in1=hmasks[:, hc:hc + 8], op=ALU.mult),
                            reads=[f"ps{psb}"], writes=["W0"]))
                        N = WN

                        def add(dst, dlo, a, alo, b, blo, ln, rd, wt):
                            P.add("dve", lambda e: e.tensor_tensor(out=dst[:, dlo:dlo + ln], in0=a[:, alo:alo + ln], in1=b[:, blo:blo + ln], op=ALU.add),
                                  reads=rd, writes=wt)
                        if g == 0:
                            add(Bb, 1, U, 0, U, 1, N - 1, ["W0"], ["W1"])
                            S, ts = Bb, "W1"
                        else:
                            add(Bb, 0, U, 0, U, 1, N - 1, ["W0"], ["W1"])
                            if g == 1:
                                add(Cb, 2, Bb, 0, Bb, 2, N - 3, ["W1"], ["W3"])
                                S, ts = Cb, "W3"
                            else:
                                add(Cb, 0, Bb, 0, Bb, 2, N - 3, ["W1"], ["W3"])
                                if g == 2:
                                    add(Bb, 4, Cb, 0, Cb, 4, N - 7, ["W3"], ["W1"])
                                    S, ts = Bb, "W1"
                                else:
                                    add(Bb, 0, Cb, 0, Cb, 4, N - 7, ["W3"], ["W1"])
                                    add(Cb, 8, Bb, 0, Bb, 8, N - 15, ["W1"], ["W3"])
                                    S, ts = Cb, "W3"
                        P.add("dve", lambda e: e.scalar_tensor_tensor(out=Dbf[:, 8:N - 8], in0=S[:, 8:N - 8], scalar=1.0 / w, in1=U[:, 8:N - 8],
                                                                      op0=ALU.mult, op1=ALU.subtract),
                              reads=[ts, "W0"], writes=["Dbf"])
                        for ei, ec in enumerate((8, 256, 280, 2320)):
                            P.add("dve", lambda e, ei=ei, ec=ec: e.tensor_tensor(out=tmp8[:], in0=S[:, ec:ec + 8], in1=ptabs[:, g, ei, :], op=ALU.mult),
                                  reads=[ts], writes=["tmp8"])
                            P.add("dve", lambda e, ec=ec: e.tensor_tensor(out=Dbf[:, ec:ec + 8], in0=tmp8[:], in1=U[:, ec:ec + 8], op=ALU.subtract),
                                  reads=["tmp8", "W0", "Dbf"], writes=["Dbf"])
                        for (c0, n) in mtiles:
                            ab = 4 + cnt["aux"] % 4
                            cnt["aux"] += 1
                            P.add("pe", lambda e, ab=ab, c0=c0, n=n: e.matmul(ps[:, ab, :n], lhsT=wpools[:, l, g, :], rhs=Dbf[:, wcol(c0):wcol(c0) + n],
                                                                              start=True, stop=True),
                                  reads=["Dbf"], writes=[f"ps{ab}"])
                            o = new_ost()
                            P.add("act", lambda e, ab=ab, n=n, o=o: e.activation(out=ost[o][:, :n], in_=ps[:, ab, :n], func=AF.Copy, scale=pscs[:, l, g:g + 1]),
                                  reads=[f"ps{ab}"], writes=[f"ost{o}"])
                            P.dma("sp", f"ost{o}", mixbuf[:, 8 + g, c0:c0 + n], ost[o][:, :n], reads=[f"ost{o}"])

                    sub = stop_after.split(":")[1] if (stop_after and ":" in stop_after and stop_after.startswith(f"A1_{l}")) else "all"
                    qk_chunk(8, "k", 0)
                    qk_chunk(9, "k", 1)
                    if sub != "k":
                        v_chunk()
                    if sub not in ("k", "v"):
                        for j in range(4):
                            conv_triple(j)
                    if sub not in ("k", "v", "conv"):
                        for g in range(4):
                            pool_chunk(g)
                    if sub not in ("k", "v", "conv", "pool"):
                        for h in range(8):
                            qk_chunk(h, "q", h)
                    if sub not in ("k", "v", "conv", "pool", "q"):
                        for kvi in range(2):
                            P.add("pool", lambda e, kvi=kvi: e.collective_compute(
                                "AllGather", ALU.bypass, replica_groups=RG, ins=[kv_in[kvi * 128:(kvi + 1) * 128, :]], outs=[kv_all[kvi]]),
                                reads=kvtoks + ["kv_all"], writes=["kv_all"], final_wait=True)
                        if debug and l == 0:
                            P.dma("sp", "dbg_kv", kvdbg, kv_all, reads=["kv_all"])
                    P.emit(nc)
                    if sub != "all":
                        return True
            return stop_after == f"A1_{l}"

        def phase_B(l):
            last = (l == DEPTH - 1)
            with contextlib.ExitStack() as st:
                Kf = sb("Kf", [128, 2, NCTX + 4 * NLAT], BF16, st)
                Vf = sb("Vf", [128, 66, 256], BF16, st)
                qa = sb("qa", [128, 8, T], BF16, st)
                Pt = [sb(f"Pt{i}", [128, 2, 512], BF16, st) for i in range(4)]
                rec = [sb(f"rec{i}", [128, 512], F32, st) for i in range(2)]
                sacc = [sb(f"sacc{i}", [128, 2, 512], F32, st) for i in range(2)]
                ssb = sb("ssb", [128, 512], BF16, st)
                kva = kv_all.rearrange("k (r p) c -> k p r c", r=4, p=128)
                for h in range(2):
                    P.dma("sp", f"ldk{h}", Kf[:, h, NCTX:].rearrange("p (r t) -> p r t", r=4), kva[0][:, :, h * NLAT:(h + 1) * NLAT],
                          writes=["Kf"])
                P.dma("sp", "ldkc", Kf[:, :, 0:NCTX], kvc[:, 0], writes=["Kf"])
                P.dma("sp", "ldv", Vf[:, 2:66, :].rearrange("p (r i) d -> p r (i d)", r=4), kva[1], writes=["Vf"])
                P.dma("sp", "ldvc", Vf[:, 0:2, :], kvc[:, 1], writes=["Vf"])
                P.dma("sp", "ldq", qa[:], qbuf, writes=["qa"])
                iters = []
                if not last:
                    for h in range(8):
                        iters.append((h, 0, 256, [0, 1]))
                for h in range(8):
                    for (c0, n) in TL[1:]:
                        iters.append((h, c0, n, list(range(66))))
                import os as _os
                if _os.environ.get("KDBG_BITERS"):
                    _n = int(_os.environ["KDBG_BITERS"])
                    _keep = iters[:_n] + iters[8:8 + _n]
                    for (h_, c0_, n_, _k) in iters:
                        if (h_, c0_, n_, _k) not in _keep:
                            P.dma("sp", f"dbgq{h_ % 2}", mixbuf[:, h_, c0_:c0_ + n_], qa[:, h_, c0_:c0_ + n_], reads=["qa"])
                    iters = _keep
                cnt = {"s": 0, "p": 0}

                def do_iter(it, h, c0, n, kts):
                    ab = 0
                    kvh = h // 4
                    bo, bs = 6, 7
                    qtok = f"q:{h}:{c0}"
                    nk = len(kts)
                    npair = nk // 2
                    offl = SUM_OFFLOAD and npair >= 8

                    def mode(j):
                        if not offl:
                            return "pe"
                        return ("pe", "dve", "pool", "dve", "dve", "pe", "dve", "pool", "dve", "dve", "pe", "dve", "pool", "dve", "pool")[j % 15]
                    pe_js = [j for j in range(npair) if mode(j) == "pe"]
                    sbank = {}
                    pslot = {}
                    first = {"dve": True, "pool": True}

                    def S2(j):
                        b0 = 2 * (cnt["s"] % 3)
                        cnt["s"] += 1
                        sbank[j] = b0
                        k0, k1 = kts[2 * j], kts[2 * j + 1]

                        def fn(e, b0=b0, k0=k0, k1=k1):
                            e.matmul(ps[:, b0, :n], lhsT=Kf[:, kvh, k0 * 128:(k0 + 1) * 128], rhs=qa[:, h, c0:c0 + n], start=True, stop=True)
                            return e.matmul(ps[:, b0 + 1, :n], lhsT=Kf[:, kvh, k1 * 128:(k1 + 1) * 128], rhs=qa[:, h, c0:c0 + n], start=True, stop=True)
                        P.add("pe", fn, reads=["Kf", "qa", qtok], writes=[f"ps{b0}", f"ps{b0 + 1}"])

                    def E2(j):
                        b0 = sbank[j]
                        p = cnt["p"] % 4
                        cnt["p"] += 1
                        pslot[j] = p
                        P.add("act", lambda e, b0=b0, p=p: e.activation(out=Pt[p][:, :, :n], in_=ps[:, b0:b0 + 2, :n], func=AF.Exp, scale=SCALE),
                              reads=[f"ps{b0}", f"ps{b0 + 1}"], writes=[f"Pt{p}"])

                    def PV2(j):
                        p = pslot[j]
                        k0, k1 = kts[2 * j], kts[2 * j + 1]
                        md = mode(j)

                        def fn(e, p=p, k0=k0, k1=k1, j=j, md=md):
                            e.matmul(ps[:, bo, :n], lhsT=Vf[:, k0, kvh * 128:(kvh + 1) * 128], rhs=Pt[p][:, 0, :n], start=(j == 0), stop=False)
                            r = e.matmul(ps[:, bo, :n], lhsT=Vf[:, k1, kvh * 128:(kvh + 1) * 128], rhs=Pt[p][:, 1, :n], start=False, stop=(j == npair - 1))
                            if md == "pe":
                                lastpe = (j == pe_js[-1]) and not offl
                                e.matmul(ps[:, bs, :n], lhsT=ones[:], rhs=Pt[p][:, 0, :n], start=(j == 0), stop=False)
                                r = e.matmul(ps[:, bs, :n], lhsT=ones[:], rhs=Pt[p][:, 1, :n], start=False, stop=lastpe)
                            return r
                        P.add("pe", fn, reads=[f"Pt{p}", "Vf"], writes=[f"acc{ab}"])
                        if md != "pe":
                            a = 0 if md == "dve" else 1
                            if first[md]:
                                first[md] = False
                                P.add(md, lambda e, a=a, p=p: e.tensor_copy(out=sacc[a][:, :, :n], in_=Pt[p][:, :, :n]),
                                      reads=[f"Pt{p}"], writes=[f"sacc{a}"])
                            else:
                                P.add(md, lambda e, a=a, p=p: e.tensor_tensor(out=sacc[a][:, :, :n], in0=sacc[a][:, :, :n], in1=Pt[p][:, :, :n], op=ALU.add),
                                      reads=[f"Pt{p}", f"sacc{a}"], writes=[f"sacc{a}"])
                    LA = 2
                    for j in range(min(LA, npair)):
                        S2(j)
                    for j in range(npair):
                        if j + LA < npair:
                            S2(j + LA)
                        E2(j)
                        PV2(j)
                    if offl:
                        P.add("dve", lambda e: e.tensor_tensor(out=sacc[0][:, 0, :n], in0=sacc[0][:, 0, :n], in1=sacc[0][:, 1, :n], op=ALU.add),
                              reads=["sacc0"], writes=["sacc0"])
                        P.add("pool", lambda e: e.tensor_tensor(out=sacc[1][:, 0, :n], in0=sacc[1][:, 0, :n], in1=sacc[1][:, 1, :n], op=ALU.add),
                              reads=["sacc1"], writes=["sacc1"])
                        P.add("dve", lambda e: e.tensor_tensor(out=ssb[:, :n], in0=sacc[0][:, 0, :n], in1=sacc[1][:, 0, :n], op=ALU.add),
                              reads=["sacc0", "sacc1"], writes=["ssb"])
                        P.add("pe", lambda e: e.matmul(ps[:, bs, :n], lhsT=ones[:], rhs=ssb[:, :n], start=False, stop=True),
                              reads=["ssb", f"acc{ab}"], writes=[f"acc{ab}"])
                    P.add("dve", lambda e: e.reciprocal(out=rec[ab][:, :n], in_=ps[:, bs, :n]),
                          reads=[f"acc{ab}"], writes=[f"rec{ab}"])
                    P.add("dve", lambda e: e.tensor_tensor(out=qa[:, h, c0:c0 + n], in0=ps[:, bo, :n], in1=rec[ab][:, :n], op=ALU.mult),
                          reads=[f"acc{ab}", f"rec{ab}"], writes=[qtok, f"acc{ab}"])
                    P.dma("sp", f"sta{it % 4}", mixbuf[:, h, c0:c0 + n], qa[:, h, c0:c0 + n], reads=[qtok])
                for it, (h, c0, n, kts) in enumerate(iters):
                    do_iter(it, h, c0, n, kts)
                P.emit(nc)
            return stop_after == f"B_{l}"

        def phase_C(l):
            last = (l == DEPTH - 1)
            with contextlib.ExitStack() as st:
                mix = sb("mix", [128, KC, T], BF16, st)
                wr = [sb(f"wo{i}", [128, KC, 128], BF16, st) for i in range(3)]
                xs = [sb(f"xc{i}", [128, 512], F32, st) for i in range(4)]
                src = xT if l == 0 else xres
                tiles = TL[1:] if last else TL
                P.dma("sp", "ldmix", mix[:], mixbuf, writes=["mix"])
                k = 0
                for m in range(KC):
                    ws = m % 3
                    P.dma("pool", f"wo{ws}", wr[ws][:], wout_(l)[m], writes=[f"wo{ws}"])
                    for (c0, n) in tiles:
                        s = 1 if c0 < NCTX else 0
                        xi = k % 4
                        psb = k % 4
                        k += 1
                        P.dma("sp", f"xc{xi}", xs[xi][:, :n], src[:, m, c0:c0 + n], writes=[f"xc{xi}"])

                        def fn(e, ws=ws, psb=psb, c0=c0, n=n):
                            last_ = None
                            for kc in range(KC):
                                last_ = e.matmul(ps[:, psb, :n], lhsT=wr[ws][:, kc, :], rhs=mix[:, kc, c0:c0 + n], start=(kc == 0), stop=(kc == KC - 1))
                            return last_
                        P.add("pe", fn, reads=[f"wo{ws}", "mix"], writes=[f"ps{psb}"])
                        P.add("dve", lambda e, xi=xi, psb=psb, n=n, m=m, s=s: e.scalar_tensor_tensor(
                            out=xs[xi][:, :n], in0=ps[:, psb, :n], scalar=mv(l, 2, m, s), in1=xs[xi][:, :n], op0=ALU.mult, op1=ALU.add),
                            reads=[f"ps{psb}", f"xc{xi}"], writes=[f"xc{xi}"])
                        P.dma("sp", f"xo{xi}", xres[:, m, c0:c0 + n], xs[xi][:, :n], reads=[f"xc{xi}"])
                P.emit(nc)
            return stop_after == f"C_{l}"

        def phase_D(l):
            last = (l == DEPTH - 1)
            halves = [TL[1:3], TL[3:5]] if last else [TL[0:3], TL[3:5]]
            G = 4

            def blk(a0, n):
                return list(range(a0 // 256, (a0 + n) // 256))

            def acct(a0, n):
                return [f"acc:{k}" for k in blk(a0, n)]

            def accwt(m, a0, n):
                return [f"accw:{m}:{k}" for k in blk(a0, n)]
            with contextlib.ExitStack() as st:
                acc = sb("acc", [128, KC, 1280], F32, st)
                h2 = sb("h2", [128, KC, 1280], BF16, st)
                w1 = [sb(f"w1_{i}", [128, KC, 128], BF16, st) for i in range(6)]
                w2 = [sb(f"w2_{i}", [128, D], BF16, st) for i in range(6)]
                ut = [sb(f"ut{i}", [128, G, 512], BF16, st) for i in range(2)]
                rl = [sb(f"rl{i}", [128, 512], F32, st) for i in range(2)]
                nb = NormBufs(st, "d", 256)
                for hi, tiles in enumerate(halves):
                    hc0 = tiles[0][0]
                    hn = sum(n for _, n in tiles)
                    acctoks = [f"accw:{m}:{c0 - hc0}" for m in range(KC) for (c0, n) in tiles]
                    for ti, (c0_, n_) in enumerate(tiles):
                        P.dma("sp", f"ldacc{ti}", acc[:, :, c0_ - hc0:c0_ - hc0 + n_], xres[:, :, c0_:c0_ + n_],
                              writes=acct(c0_ - hc0, n_) + [t_ for m in range(KC) for t_ in accwt(m, c0_ - hc0, n_)])
                    k = 0
                    for (c0, n) in tiles:
                        s = 1 if c0 < NCTX else 0
                        for o in range(0, n, 256):
                            a0 = c0 - hc0 + o
                            emit_norm(nb, lambda c, a0=a0: acc[:, c, a0:a0 + 256], 256,
                                      lambda c, a0=a0: h2[:, c, a0:a0 + 256], l, 1, s, k % 2, acct(a0, 256), [f"h2:{a0}"])
                            k += 1
                    cnt = {"u": 0, "y": 0, "ut": 0, "rl": 0}
                    for fg in range(NF // G):
                        slots = []
                        for fi in range(G):
                            f = fg * G + fi
                            sl = f % 6
                            slots.append(sl)
                            P.dma("pool", f"w1_{sl}", w1[sl][:], wff1_(l)[f], writes=[f"w1_{sl}"])
                            P.dma("pool", f"w2_{sl}", w2[sl][:], wff2_(l)[f], writes=[f"w2_{sl}"])
                        for (c0, n) in tiles:
                            s = 1 if c0 < NCTX else 0
                            a0 = c0 - hc0
                            h2toks = [f"h2:{a0 + o}" for o in range(0, n, 256)]
                            ui = cnt["ut"] % 2
                            cnt["ut"] += 1
                            for fi in range(G):
                                sl = slots[fi]
                                ub = cnt["u"] % 4
                                cnt["u"] += 1

                                def fn(e, sl=sl, ub=ub, a0=a0, n=n):
                                    last_ = None
                                    for kc in range(KC):
                                        last_ = e.matmul(ps[:, ub, :n], lhsT=w1[sl][:, kc, :], rhs=h2[:, kc, a0:a0 + n], start=(kc == 0), stop=(kc == KC - 1))
                                    return last_
                                P.add("pe", fn, reads=[f"w1_{sl}"] + h2toks, writes=[f"ps{ub}"])
                                ri = cnt["rl"] % 2
                                cnt["rl"] += 1
                                P.add("act", lambda e, ub=ub, ri=ri, n=n: e.activation(out=rl[ri][:, :n], in_=ps[:, ub, :n], func=AF.Relu),
                                      reads=[f"ps{ub}"], writes=[f"rl{ri}"])
                                P.add("dve", lambda e, ri=ri, ui=ui, fi=fi, n=n: e.tensor_tensor(out=ut[ui][:, fi, :n], in0=rl[ri][:, :n], in1=rl[ri][:, :n], op=ALU.mult),
                                      reads=[f"rl{ri}"], writes=[f"ut{ui}"])
                            for m in range(KC):
                                yb = 4 + cnt["y"] % 4
                                cnt["y"] += 1

                                def fn2(e, yb=yb, m=m, ui=ui, n=n, slots=tuple(slots)):
                                    last_ = None
                                    for fi in range(G):
                                        last_ = e.matmul(ps[:, yb, :n], lhsT=w2[slots[fi]][:, m * 128:(m + 1) * 128], rhs=ut[ui][:, fi, :n],
                                                         start=(fi == 0), stop=(fi == G - 1))
                                    return last_
                                P.add("pe", fn2, reads=[f"w2_{sl}" for sl in slots] + [f"ut{ui}"], writes=[f"ps{yb}"])
                                P.add("dve", lambda e, yb=yb, m=m, a0=a0, n=n, s=s: e.scalar_tensor_tensor(
                                    out=acc[:, m, a0:a0 + n], in0=ps[:, yb, :n], scalar=mv(l, 5, m, s), in1=acc[:, m, a0:a0 + n],
                                    op0=ALU.mult, op1=ALU.add), reads=[f"ps{yb}"] + accwt(m, a0, n), writes=accwt(m, a0, n))
                    for ti, (c0_, n_) in enumerate(tiles):
                        a_ = c0_ - hc0
                        tt = [t_ for m in range(KC) for t_ in accwt(m, a_, n_)] + acct(a_, n_)
                        dst = outT[:, :, c0_ - NCTX:c0_ - NCTX + n_] if last else xres[:, :, c0_:c0_ + n_]
                        P.dma("sp", f"stout{ti}", dst, acc[:, :, a_:a_ + n_], reads=tt, writes=[f"accst{ti}"])
                    if not last:
                        ed = edge_in.rearrange("p (c t) -> p c t", t=16)
                        if hi == 0:
                            a = NCTX - hc0
                            P.dma("sp", "edge0", ed[:, :, 0:8], acc[:, :, a:a + 8],
                                  reads=[t_ for m in range(KC) for t_ in accwt(m, a, 256)] + acct(a, 256), writes=["edge0"])
                        else:
                            a = T - 8 - hc0
                            a_t = (a // 256) * 256
                            P.dma("sp", "edge1", ed[:, :, 8:16], acc[:, :, a:a + 8],
                                  reads=[t_ for m in range(KC) for t_ in accwt(m, a_t, 256)] + acct(a_t, 256), writes=["edge1"])
                if not last:
                    P.add("pool", lambda e: e.collective_compute("AllGather", ALU.bypass, replica_groups=RG, ins=[edge_in], outs=[edge_all]),
                          reads=["edge0", "edge1"], writes=["edge_all"], final_wait=True)
                P.emit(nc)
            return stop_after == f"D_{l}"

        phase_M()
        done = stop_after == "M"
        for l in range(nl):
            if done:
                break
            done = phase_A(l)
            if done:
                break
            done = phase_B(l)
            if done:
                break
            done = phase_C(l)
            if done:
                break
            done = phase_D(l)
        if done or True:
            pass
    return nc


def _perm():
    i = np.arange(128)
    return np.where((i // 32) % 2 == 0, i + 32, i - 32)


def _rope_tables(pos):
    n = pos.shape[0]
    row = (pos // 64).astype(np.float32)
    col = (pos % 64).astype(np.float32)
    inv = (np.float32(10000.0) ** (-(np.arange(32, dtype=np.float32)) / np.float32(32))).astype(np.float32)
    ar = row[:, None] * inv
    ac = col[:, None] * inv
    ang = np.concatenate([ar, ar, ac, ac], -1)
    return np.cos(ang).astype(np.float32), np.sin(ang).astype(np.float32)


def prep_shared(inp):
    f = np.float32
    sh = {}
    for l in range(DEPTH):
        sh[f"wmod{l}"] = np.ascontiguousarray(inp["w_mod"][l].reshape(16, 128, 96, 128).transpose(2, 1, 0, 3))
        sh[f"win{l}"] = np.ascontiguousarray(inp["w_in"][l].reshape(16, 128, 28, 128).transpose(2, 1, 0, 3))
        sh[f"wout{l}"] = np.ascontiguousarray(inp["w_out"][l].reshape(16, 128, 16, 128).transpose(2, 1, 0, 3))
        sh[f"wff1{l}"] = np.ascontiguousarray(inp["w_ff1"][l].reshape(16, 128, 64, 128).transpose(2, 1, 0, 3))
        sh[f"wff2{l}"] = np.ascontiguousarray(inp["w_ff2"][l].reshape(64, 128, 2048))
    bm = inp["b_mod"].reshape(2, 96, 128).transpose(2, 0, 1)
    sh["bmod"] = np.ascontiguousarray(np.repeat(bm[..., None], 2, -1)).astype(f)
    g = np.stack([inp["norm1_g"], inp["norm2_g"]], 1)
    sh["g12"] = np.ascontiguousarray(g.reshape(2, 2, 16, 128).transpose(3, 0, 1, 2)).astype(f)
    pm = _perm()
    qg, kg = inp["q_norm_g"], inp["k_norm_g"]
    sh["qkg"] = np.ascontiguousarray(np.stack([qg, qg[:, pm], kg, kg[:, pm]], -1).transpose(1, 0, 2)).astype(f)
    rm = np.zeros((128, 128), np.float32)
    rm[pm, np.arange(128)] = 1.0
    sh["rmat"] = rm.astype(ml_dtypes.bfloat16)
    sh["wpool"] = np.ascontiguousarray(inp["w_pool"].transpose(2, 0, 1, 3)).astype(f)
    sh["pscale"] = np.ascontiguousarray(inp["pool_scale"].reshape(2, 4, 128).transpose(2, 0, 1)).astype(f)
    sh["convw"] = np.ascontiguousarray(inp["conv_w"].reshape(2, 3, 4, 128).transpose(3, 0, 2, 1)).astype(f)
    return sh


def prep_core(inp, r):
    f = np.float32
    b, j = r // 4, r % 4
    x = inp["x"]
    lo, hi = j * NLAT, (j + 1) * NLAT
    cat = np.concatenate([inp["ctx"][b], x[b, lo:hi]], 0)
    m = {}
    m["xT"] = np.ascontiguousarray(cat.reshape(T, 16, 128).transpose(2, 1, 0))
    hl = x[b, lo - 8:lo] if j > 0 else np.zeros((8, D), f)
    hr = x[b, hi:hi + 8] if j < 3 else np.zeros((8, D), f)
    m["xh"] = np.ascontiguousarray(np.concatenate([hl, hr], 0).reshape(16, 16, 128).transpose(2, 1, 0)).astype(f)
    cv = np.stack([inp["c"][b], inp["c_ctx"]], -1)
    m["cv"] = np.ascontiguousarray(cv.reshape(16, 128, 2).transpose(1, 0, 2)).astype(f)
    cos, sin = _rope_tables(np.arange(lo, hi))
    sign = np.where((np.arange(128) // 32) % 2 == 0, -1.0, 1.0).astype(f)
    cs = np.zeros((128, 2, T), f)
    cs[:, 0, :NCTX] = 1.0
    cs[:, 0, NCTX:] = cos.T
    cs[:, 1, NCTX:] = (sin * sign[None, :]).T
    m["cs"] = cs
    pt = np.zeros((4, 4, 8), f)

    def inv_cnt(tpos, n, w):
        lo_ = w // 2
        hi_ = w - 1 - lo_
        st = np.clip(tpos - lo_, 0, n)
        en = np.clip(tpos + hi_ + 1, 0, n)
        return (1.0 / (en - st)).astype(f)
    for g, w in enumerate((2, 4, 8, 16)):
        pt[g, 0] = inv_cnt(np.arange(0, 8), NCTX, w)
        pt[g, 1] = inv_cnt(np.arange(NCTX - 8, NCTX), NCTX, w)
        pt[g, 2] = inv_cnt(np.arange(lo, lo + 8), 4 * NLAT, w)
        pt[g, 3] = inv_cnt(np.arange(hi - 8, hi), 4 * NLAT, w)
    m["ptab"] = np.ascontiguousarray(np.broadcast_to(pt[None], (128, 4, 4, 8))).astype(f)
    hm = np.zeros((16,), f)
    hm[0:8] = 1.0 if j > 0 else 0.0
    hm[8:16] = 1.0 if j < 3 else 0.0
    m["hmask"] = np.ascontiguousarray(np.broadcast_to(hm[None], (128, 16))).astype(f)
    hs = np.zeros((8,), f)
    if j > 0:
        hs[j - 1] = 1.0
    if j < 3:
        hs[4 + j + 1] = 1.0
    m["hsel"] = np.ascontiguousarray(np.broadcast_to(hs[None], (128, 8))).astype(f)
    return m


def make_in_maps(inp):
    inp = {k: np.asarray(v) for k, v in inp.items()}
    sh = prep_shared(inp)
    maps = []
    for r in range(8):
        m = prep_core(inp, r)
        m.update(sh)
        maps.append(m)
    return maps


def assemble(results):
    out = np.empty((2, 4 * NLAT, D), np.float32)
    for r in range(8):
        b, j = r // 4, r % 4
        o = np.asarray(results[r]["outT"])
        out[b, j * NLAT:(j + 1) * NLAT] = o.transpose(2, 1, 0).reshape(NLAT, D)
    return out


def kernel(**inputs):
    maps = make_in_maps(inputs)
    nc = build()
    res = run_bass_kernel_spmd(nc, maps, core_ids=list(range(8)))
    return assemble(res.results)
```

```python
import contextlib
import math
import numpy as np
import ml_dtypes
import concourse.bass as bass
import concourse.mybir as mybir
from concourse.bass_utils import run_bass_kernel_spmd

F32 = mybir.dt.float32
BF16 = mybir.dt.bfloat16
ALU = mybir.AluOpType
AF = mybir.ActivationFunctionType

D = 2048
KC = 16
NCTX = 256
NLAT = 2048
T = NCTX + NLAT
DEPTH = 2
NF = 64
EPS = 1e-6
SCALE = 1.0 / math.sqrt(128.0)
TL = [(0, 256), (256, 512), (768, 512), (1280, 512), (1792, 512)]
WN = 2336
ENGS = ("pe", "act", "dve", "pool", "sp")
SUM_OFFLOAD = True


class _Op:
    __slots__ = ("eng", "fn", "reads", "writes", "dma_key", "deps", "idx", "need_inc", "cnt", "dma_val", "final_wait")


class Prog:
    def __init__(self):
        self.ops = []
        self.last_w = {}
        self.readers = {}
        self.nblk = 0

    def add(self, eng, fn, reads=(), writes=(), dma_key=None, final_wait=False):
        op = _Op()
        op.final_wait = final_wait
        op.eng, op.fn, op.reads, op.writes, op.dma_key = eng, fn, tuple(reads), tuple(writes), dma_key
        op.deps = set()
        op.need_inc = False
        op.cnt = None
        op.dma_val = None
        op.idx = len(self.ops)
        for r in op.reads:
            w = self.last_w.get(r)
            if w is not None:
                op.deps.add(w)
        for t in op.writes:
            w = self.last_w.get(t)
            if w is not None:
                op.deps.add(w)
            for rd in self.readers.get(t, {}).values():
                if rd is not op:
                    op.deps.add(rd)
        for r in op.reads:
            d = self.readers.setdefault(r, {})
            k = ("dma", op.idx) if dma_key is not None else op.eng
            d[k] = op
        for t in op.writes:
            self.last_w[t] = op
            self.readers[t] = {}
        if dma_key is None:
            drop = set()
            for d in op.deps:
                if d.dma_key is None and d.eng == op.eng and op.eng == "pe":
                    if not any(r in d.writes for r in op.reads):
                        drop.add(d)
            op.deps -= drop
        self.ops.append(op)
        return op

    def dma(self, eng, key, out, in_, reads=(), writes=(), **kw):
        def fn(e):
            return e.dma_start(out=out, in_=in_, **kw)
        return self.add(eng, fn, reads, writes, dma_key=key)

    def setup(self, nc, stack):
        self.nc = nc
        self.stack = stack
        self.esem = {e: stack.enter_context(nc.semaphore(f"s_{e}")) for e in ENGS}
        self.dsem = {}
        self.cnt = {e: 0 for e in ENGS}
        self.dcnt = {}
        self.waited = {e: {} for e in ENGS}

    def emit(self, nc):
        ops = self.ops
        self.nblk += 1
        for op in ops:
            if op.final_wait:
                op.need_inc = True
            for d in op.deps:
                d.need_inc = True
        cnt, dcnt, esem, dsem = self.cnt, self.dcnt, self.esem, self.dsem
        for op in ops:
            if op.dma_key is not None:
                if op.dma_key not in dsem:
                    dsem[op.dma_key] = self.stack.enter_context(nc.semaphore(f"d_{len(dsem)}"))
                dcnt[op.dma_key] = dcnt.get(op.dma_key, 0) + 16
                op.dma_val = dcnt[op.dma_key]
            elif op.need_inc:
                cnt[op.eng] += 1
                op.cnt = cnt[op.eng]
        with nc.Block() as block:
            by_eng = {e: [op for op in ops if op.eng == e] for e in ENGS}

            def run(engname, engobj):
                waited = self.waited[engname]
                for op in by_eng[engname]:
                    need = {}
                    for d in op.deps:
                        if d.dma_key is not None:
                            s, v = dsem[d.dma_key], d.dma_val
                        else:
                            s, v = esem[d.eng], d.cnt
                        k = id(s)
                        if need.get(k, (None, 0))[1] < v:
                            need[k] = (s, v)
                    for k, (s, v) in need.items():
                        if waited.get(k, 0) < v:
                            engobj.wait_ge(s, v)
                            waited[k] = v
                    ins = op.fn(engobj)
                    if op.dma_key is not None:
                        ins.then_inc(dsem[op.dma_key], 16)
                    elif op.need_inc:
                        ins.then_inc(esem[op.eng], 1)
                fw = [op.cnt for op in by_eng[engname] if op.final_wait]
                if fw and waited.get(id(esem[engname]), 0) < max(fw):
                    engobj.wait_ge(esem[engname], max(fw))
                    waited[id(esem[engname])] = max(fw)
                last = {}
                for op in by_eng[engname]:
                    if op.dma_key is not None:
                        last[op.dma_key] = op.dma_val
                for k, v in last.items():
                    if waited.get(id(dsem[k]), 0) < v:
                        engobj.wait_ge(dsem[k], v)
                        waited[id(dsem[k])] = v

            @block.tensor
            def _(e):
                run("pe", e)

            @block.scalar
            def _(e):
                run("act", e)

            @block.vector
            def _(e):
                run("dve", e)

            @block.gpsimd
            def _(e):
                run("pool", e)

            @block.sync
            def _(e):
                run("sp", e)
        self.ops = []
        self.last_w = {}
        self.readers = {}


def wcol(c0):
    return 8 + c0 if c0 < NCTX else c0 + 24


def build(stop_after=None, debug=False, ncores=8, nl=DEPTH):
    nc = bass.Bass("TRN2", target_bir_lowering=False)

    def din(name, shape, dt=F32):
        return nc.dram_tensor(name, list(shape), dt, kind="ExternalInput").ap()

    def dint(name, shape, dt, kind="Internal"):
        return nc.dram_tensor(name, list(shape), dt, kind=kind).ap()

    xT = din("xT", [128, KC, T])
    xh = din("xh", [128, KC, 16])
    cvd = din("cv", [128, KC, 2])
    _wc = {}

    def wl(name, l, shape):
        key = f"{name}{l}"
        if key not in _wc:
            _wc[key] = din(key, shape)
        return _wc[key]

    def wmod_(l):
        return wl("wmod", l, [96, 128, KC, 128])

    def win_(l):
        return wl("win", l, [28, 128, KC, 128])

    def wout_(l):
        return wl("wout", l, [KC, 128, KC, 128])

    def wff1_(l):
        return wl("wff1", l, [NF, 128, KC, 128])

    def wff2_(l):
        return wl("wff2", l, [NF, 128, D])
    bmod = din("bmod", [128, DEPTH, 96, 2])
    g12 = din("g12", [128, DEPTH, 2, KC])
    qkg = din("qkg", [128, DEPTH, 4])
    csd = din("cs", [128, 2, T])
    rmatd = din("rmat", [128, 128], BF16)
    wpoold = din("wpool", [128, DEPTH, 4, 128])
    pscaled = din("pscale", [128, DEPTH, 4])
    convwd = din("convw", [128, DEPTH, 4, 3])
    ptabd = din("ptab", [128, 4, 4, 8])
    hmaskd = din("hmask", [128, 16])
    hseld = din("hsel", [128, 8])
    outT = nc.dram_tensor("outT", [128, KC, NLAT], F32, kind="ExternalOutput").ap()

    dk = "ExternalOutput" if debug else "Internal"
    xres = dint("xres", [128, KC, T], F32, dk)
    qbuf = dint("qbuf", [128, 8, T], BF16, dk)
    mixbuf = dint("mixbuf", [128, KC, T], BF16, dk)
    kvc = dint("kvc", [128, 2, 2, 256], BF16, dk)
    hdbg = dint("hdbg", [128, KC, T + 16], BF16, dk) if debug else None
    moddbg = dint("moddbg", [128, DEPTH, 96, 2], F32, dk) if debug else None
    kv_in = dint("kv_in", [256, 4096], BF16)
    kv_all = dint("kv_all", [2, 512, 4096], BF16)
    kvdbg = dint("kvdbg", [2, 512, 4096], BF16, dk) if debug else None
    edge_in = dint("edge_in", [128, 256], F32)
    edge_all = dint("edge_all", [512, 256], F32)
    RG = [[0, 1, 2, 3], [4, 5, 6, 7]] if ncores == 8 else [[0, 1, 2, 3]]

    P = Prog()
    es = contextlib.ExitStack()
    _nm = [0]
    with es:
        P.setup(nc, es)

        def sb(name, shape, dt=F32, stack=es):
            _nm[0] += 1
            return stack.enter_context(nc.sbuf_tensor(f"{name}_{_nm[0]}", list(shape), dt))

        ps = es.enter_context(nc.psum_tensor("ps", [128, 8, 512], F32))
        modsb = sb("modsb", [128, DEPTH, 96, 2])
        A12 = sb("A12", [128, DEPTH, 2, KC, 2])
        g12s = sb("g12s", [128, DEPTH, 2, KC])
        qkgs = sb("qkgs", [128, DEPTH, 4])
        pscs = sb("pscs", [128, DEPTH, 4])
        cws = sb("cws", [128, DEPTH, 4, 3])
        ptabs = sb("ptabs", [128, 4, 4, 8])
        hmasks = sb("hmasks", [128, 16])
        hsels = sb("hsels", [128, 8])
        ones = sb("ones", [128, 128], BF16)
        rmats = sb("rmats", [128, 128], BF16)
        wpools = sb("wpools", [128, DEPTH, 4, 128], BF16)

        def mv(l, k, c, s):
            return modsb[:, l, k * 16 + c, s:s + 1]

        cact = sb("cact", [128, KC, 2], BF16)
        bms = sb("bms", [128, DEPTH, 96, 2], F32)
        tmpA = sb("tmpA", [128, KC, 2], F32)
        modcnt = {"g": 0}

        def mod_groups(l, groups, wm, psbank):
            psm = ps[:, psbank, 0:192].rearrange("p (e s) -> p e s", s=2)
            for (e0, ne) in groups:
                slot = modcnt["g"] % 2
                modcnt["g"] += 1
                P.dma("pool", f"wm{slot}", wm[slot][:, 0:ne],
                      wmod_(l)[e0:e0 + ne].rearrange("e p k m -> p e k m"), writes=[f"wm{slot}"])

                def mm(e, slot=slot, e0=e0, ne=ne, psm=psm):
                    last = None
                    for e4 in range(ne):
                        for kc in range(KC):
                            last = e.matmul(psm[:, e0 + e4, :], lhsT=wm[slot][:, e4, kc, :],
                                            rhs=cact[:, kc, :], start=(kc == 0), stop=(kc == KC - 1))
                    return last
                P.add("pe", mm, reads=[f"wm{slot}", "cact"], writes=[f"psmod{psbank}"])

        def mod_finish(l, e0, e1, psbank):
            psm = ps[:, psbank, 0:192].rearrange("p (e s) -> p e s", s=2)
            P.add("dve", lambda e: e.tensor_tensor(out=modsb[:, l, e0:e1], in0=psm[:, e0:e1], in1=bms[:, l, e0:e1], op=ALU.add),
                  reads=[f"psmod{psbank}", "bms"], writes=["modsb"])
            for w, ksc in enumerate((1, 4)):
                if e0 <= ksc * 16 and (ksc + 1) * 16 <= e1:
                    P.add("dve", lambda e, ksc=ksc: e.tensor_scalar(
                        out=tmpA[:], in0=modsb[:, l, ksc * 16:(ksc + 1) * 16, :], scalar1=1.0, scalar2=None, op0=ALU.add),
                        reads=["modsb"], writes=["tmpA"])
                    P.add("dve", lambda e, w=w: e.tensor_tensor(
                        out=A12[:, l, w], in0=tmpA[:],
                        in1=g12s[:, l, w, :].unsqueeze(2).broadcast_to([128, KC, 2]), op=ALU.mult),
                        reads=["tmpA", "g12s"], writes=["A12"])

        def phase_M():
            with contextlib.ExitStack() as st:
                cvs = sb("cvs", [128, KC, 2], F32, st)
                wm = [sb(f"wm{i}", [128, 4, KC, 128], BF16, st) for i in range(2)]
                P.dma("sp", "ld_cv", cvs[:], cvd, writes=["cvs"])
                P.dma("sp", "ld_bm", bms[:], bmod, writes=["bms"])
                P.dma("sp", "ld_g12", g12s[:], g12, writes=["g12s"])
                P.dma("sp", "ld_qkg", qkgs[:], qkg, writes=["qkgs"])
                P.dma("sp", "ld_psc", pscs[:], pscaled, writes=["pscs"])
                P.dma("sp", "ld_cw", cws[:], convwd, writes=["cws"])
                P.dma("sp", "ld_ptab", ptabs[:], ptabd, writes=["ptabs"])
                P.dma("sp", "ld_hm", hmasks[:], hmaskd, writes=["hmasks"])
                P.dma("sp", "ld_hs", hsels[:], hseld, writes=["hsels"])
                P.dma("sp", "ld_rm", rmats[:], rmatd, writes=["rmats"])
                P.dma("pool", "ld_wp", wpools[:], wpoold, writes=["wpools"])
                P.add("dve", lambda e: e.memset(ones[:], 1.0), writes=["ones"])
                P.add("act", lambda e: e.activation(out=cact[:], in_=cvs[:], func=AF.Silu),
                      reads=["cvs"], writes=["cact"])
                for l in range(nl):
                    mod_groups(l, [(e0, 4) for e0 in range(0, 32, 4)], wm, l)
                    mod_finish(l, 0, 32, l)
                P.emit(nc)

        class NormBufs:
            def __init__(self, st, tag, n):
                self.nr = 6
                self.sq = [sb(f"nsq{tag}{i}", [128, n], BF16, st) for i in range(self.nr)]
                self.tc = [sb(f"ntc{tag}{i}", [128, n], F32, st) for i in range(self.nr)]
                self.rt = sb(f"nrt{tag}", [128, n], F32, st)
                self.rstd = sb(f"nrstd{tag}", [128, n], F32, st)
                self.cnt = 0
                self.tag = tag

        def emit_norm(nb, xc_fn, n, out_fn, l, w, s, psb, rtok, wtok):
            tg = nb.tag
            for c in range(KC):
                i = nb.cnt % nb.nr
                nb.cnt += 1
                P.add("act", lambda e, c=c, i=i: e.activation(out=nb.sq[i][:, :n], in_=xc_fn(c), func=AF.Square),
                      reads=rtok, writes=[f"nsq{tg}{i}"])
                P.add("pe", lambda e, c=c, i=i: e.matmul(ps[:, psb, :n], lhsT=ones[:], rhs=nb.sq[i][:, :n],
                                                         start=(c == 0), stop=(c == KC - 1)),
                      reads=[f"nsq{tg}{i}", "ones"], writes=[f"ps{psb}"])
            P.add("act", lambda e: e.activation(out=nb.rt[:, :n], in_=ps[:, psb, :n], func=AF.Sqrt, bias=EPS, scale=1.0 / D),
                  reads=[f"ps{psb}"], writes=[f"nrt{tg}"])
            P.add("dve", lambda e: e.reciprocal(out=nb.rstd[:, :n], in_=nb.rt[:, :n]),
                  reads=[f"nrt{tg}"], writes=[f"nrstd{tg}"])
            ksh = 0 if w == 0 else 3
            for c in range(KC):
                i = nb.cnt % nb.nr
                nb.cnt += 1
                P.add("dve", lambda e, c=c, i=i: e.tensor_tensor(out=nb.tc[i][:, :n], in0=xc_fn(c), in1=nb.rstd[:, :n], op=ALU.mult),
                      reads=list(rtok) + [f"nrstd{tg}"], writes=[f"ntc{tg}{i}"])
                P.add("act", lambda e, c=c, i=i: e.activation(out=out_fn(c), in_=nb.tc[i][:, :n], func=AF.Identity,
                                                              bias=mv(l, ksh, c, s), scale=A12[:, l, w, c, s:s + 1]),
                      reads=[f"ntc{tg}{i}"], writes=wtok)

        def phase_A(l):
            last = (l == DEPTH - 1)
            with contextlib.ExitStack() as stA:
                hT = sb("hT", [128, KC, T + 16], BF16, stA)
                with contextlib.ExitStack() as st:
                    xs = [sb(f"xs{i}", [128, KC, 512], F32, st) for i in range(2)]
                    xhs = sb("xhs", [128, KC, 16], F32, st)
                    nb = NormBufs(st, "a", 512)
                    wma = [sb(f"wma{i}", [128, 2, KC, 128], BF16, st) for i in range(2)]
                    mgroups = [(e0, 2) for e0 in range(32, 96, 2)]
                    src = xT if l == 0 else xres
                    for i, (c0, n) in enumerate(TL):
                        slot = i % 2
                        s = 1 if c0 < NCTX else 0
                        P.dma("sp", f"xs{slot}", xs[slot][:, :, :n], src[:, :, c0:c0 + n], writes=[f"xs{slot}"])
                        emit_norm(nb, lambda c, slot=slot, n=n: xs[slot][:, c, :n], n,
                                  lambda c, c0=c0, n=n: hT[:, c, c0:c0 + n], l, 0, s, i % 2,
                                  [f"xs{slot}"], [f"hT{c0}"])
                        mod_groups(l, mgroups[i * 6:(i + 1) * 6], wma, 4)
                    mod_groups(l, mgroups[30:], wma, 4)
                    mod_finish(l, 32, 96, 4)
                    if l == 0:
                        P.dma("sp", "xhs", xhs[:], xh, writes=["xhs"])
                    else:
                        egs = sb("egs", [128, 4, KC, 16], F32, st)
                        P.dma("sp", "egs", egs[:], edge_all.rearrange("(r p) (c t) -> p r c t", p=128, t=16), writes=["egs"])
                        for side in range(2):
                            dst = xhs[:, :, side * 8:(side + 1) * 8]
                            for r in range(4):
                                srcc = egs[:, r, :, (1 - side) * 8:(2 - side) * 8]
                                sc = hsels[:, side * 4 + r:side * 4 + r + 1]
                                if r == 0:
                                    P.add("dve", lambda e, dst=dst, srcc=srcc, sc=sc: e.tensor_scalar(
                                        out=dst, in0=srcc, scalar1=sc, scalar2=None, op0=ALU.mult),
                                        reads=["egs"], writes=["xhs"])
                                else:
                                    P.add("dve", lambda e, dst=dst, srcc=srcc, sc=sc: e.scalar_tensor_tensor(
                                        out=dst, in0=srcc, scalar=sc, in1=dst, op0=ALU.mult, op1=ALU.add),
                                        reads=["egs", "xhs"], writes=["xhs"])
                    emit_norm(nb, lambda c: xhs[:, c, :], 16, lambda c: hT[:, c, T:T + 16], l, 0, 0, 2, ["xhs"], ["hTh"])
                    if debug and l == 0:
                        P.dma("sp", "dbg_h", hdbg, hT[:], reads=[f"hT{c0}" for c0, _ in TL] + ["hTh"])
                    P.emit(nc)
                if stop_after == f"A0_{l}":
                    return True
                with contextlib.ExitStack() as st:
                    wr = [sb(f"wr{i}", [128, KC, 128], BF16, st) for i in range(3)]
                    wv = sb("wv", [128, KC, 256], BF16, st)
                    css = sb("css", [128, 2, T], F32, st)
                    W = [sb(f"W{i}", [128, WN], F32, st) for i in range(4)]
                    Dbf = sb("Dbf", [128, WN], BF16, st)
                    tmp8 = sb("tmp8", [128, 8], F32, st)
                    sqb = [sb(f"sqb{i}", [128, 512], BF16, st) for i in range(2)]
                    qbb = [sb(f"qbb{i}", [128, 512], BF16, st) for i in range(2)]
                    rt = [sb(f"rt{i}", [128, 512], F32, st) for i in range(2)]
                    rstd = [sb(f"rstd{i}", [128, 512], F32, st) for i in range(2)]
                    u1 = [sb(f"u1{i}", [128, 512], F32, st) for i in range(2)]
                    u2 = [sb(f"u2{i}", [128, 512], F32, st) for i in range(2)]
                    ost = [sb(f"ost{i}", [128, 512], BF16, st) for i in range(3)]
                    P.dma("sp", "ld_cs", css[:], csd, writes=["css"])
                    for i in range(4):
                        P.add("dve", lambda e, i=i: e.memset(W[i][:], 0.0), writes=[f"W{i}"])
                    cnt = {"w": 0, "mb": 0, "ep": 0, "ost": 0, "aux": 0}

                    def load_w(chunk):
                        slot = cnt["w"] % 3
                        cnt["w"] += 1
                        P.dma("pool", f"wr{slot}", wr[slot][:], win_(l)[chunk], writes=[f"wr{slot}"])
                        return slot

                    def mm_tile(wslot, c0, n):
                        psb = cnt["mb"] % 4
                        cnt["mb"] += 1

                        def fn(e):
                            last = None
                            for kc in range(KC):
                                last = e.matmul(ps[:, psb, :n], lhsT=wr[wslot][:, kc, :], rhs=hT[:, kc, c0:c0 + n],
                                                start=(kc == 0), stop=(kc == KC - 1))
                            return last
                        P.add("pe", fn, reads=[f"wr{wslot}"], writes=[f"ps{psb}"])
                        return psb

                    def new_ost():
                        o = cnt["ost"] % 3
                        cnt["ost"] += 1
                        return o

                    def qk_chunk(chunk, kind, hidx):
                        wslot = load_w(chunk)
                        gcol = 0 if kind == "q" else 2
                        tiles = TL if (kind == "k" or not last) else TL[1:]
                        for (c0, n) in tiles:
                            psb = mm_tile(wslot, c0, n)
                            i = cnt["ep"] % 2
                            cnt["ep"] += 1
                            ssb, rtb = 4 + i, 6 + i
                            P.add("act", lambda e, psb=psb, i=i, n=n: e.activation(out=sqb[i][:, :n], in_=ps[:, psb, :n], func=AF.Square),
                                  reads=[f"ps{psb}"], writes=[f"sqb{i}"])
                            P.add("act", lambda e, psb=psb, i=i, n=n: e.activation(out=qbb[i][:, :n], in_=ps[:, psb, :n], func=AF.Copy),
                                  reads=[f"ps{psb}"], writes=[f"qbb{i}"])
                            P.add("pe", lambda e, ssb=ssb, i=i, n=n: e.matmul(ps[:, ssb, :n], lhsT=ones[:], rhs=sqb[i][:, :n], start=True, stop=True),
                                  reads=[f"sqb{i}"], writes=[f"ps{ssb}"])
                            P.add("pe", lambda e, rtb=rtb, i=i, n=n: e.matmul(ps[:, rtb, :n], lhsT=rmats[:], rhs=qbb[i][:, :n], start=True, stop=True),
                                  reads=[f"qbb{i}"], writes=[f"ps{rtb}"])
                            P.add("act", lambda e, ssb=ssb, i=i, n=n: e.activation(out=rt[i][:, :n], in_=ps[:, ssb, :n], func=AF.Sqrt,
                                                                                   bias=EPS, scale=1.0 / 128.0),
                                  reads=[f"ps{ssb}"], writes=[f"rt{i}"])
                            P.add("dve", lambda e, i=i, n=n: e.reciprocal(out=rstd[i][:, :n], in_=rt[i][:, :n]),
                                  reads=[f"rt{i}"], writes=[f"rstd{i}"])
                            P.add("dve", lambda e, psb=psb, i=i, n=n, c0=c0: e.scalar_tensor_tensor(
                                out=u1[i][:, :n], in0=ps[:, psb, :n], scalar=qkgs[:, l, gcol:gcol + 1], in1=css[:, 0, c0:c0 + n],
                                op0=ALU.mult, op1=ALU.mult), reads=[f"ps{psb}", "css"], writes=[f"u1{i}"])
                            P.add("dve", lambda e, rtb=rtb, i=i, n=n, c0=c0: e.scalar_tensor_tensor(
                                out=u2[i][:, :n], in0=ps[:, rtb, :n], scalar=qkgs[:, l, gcol + 1:gcol + 2], in1=css[:, 1, c0:c0 + n],
                                op0=ALU.mult, op1=ALU.mult), reads=[f"ps{rtb}", "css"], writes=[f"u2{i}"])
                            P.add("dve", lambda e, i=i, n=n: e.tensor_tensor(out=u1[i][:, :n], in0=u1[i][:, :n], in1=u2[i][:, :n], op=ALU.add),
                                  reads=[f"u1{i}", f"u2{i}"], writes=[f"u1{i}"])
                            o = new_ost()
                            P.add("dve", lambda e, i=i, n=n, o=o: e.tensor_tensor(out=ost[o][:, :n], in0=u1[i][:, :n], in1=rstd[i][:, :n], op=ALU.mult),
                                  reads=[f"u1{i}", f"rstd{i}"], writes=[f"ost{o}"])
                            if kind == "q":
                                dst = qbuf[:, hidx, c0:c0 + n]
                                wt = []
                            elif c0 < NCTX:
                                dst = kvc[:, 0, hidx, :]
                                wt = []
                            else:
                                dst = kv_in[0:128, hidx * NLAT + c0 - NCTX: hidx * NLAT + c0 - NCTX + n]
                                wt = [f"kv_in:k{hidx}:{c0}"]
                                kvtoks.append(wt[0])
                            P.dma("sp", f"ost{o}", dst, ost[o][:, :n], reads=[f"ost{o}"], writes=wt)

                    kvtoks = []

                    def v_chunk():
                        for j in range(2):
                            P.dma("pool", f"wv{j}", wv[:, :, j * 128:(j + 1) * 128], win_(l)[10 + j], writes=["wv"])
                        for tt in range(18):
                            psb = cnt["mb"] % 4
                            cnt["mb"] += 1

                            def fn(e, tt=tt, psb=psb):
                                last = None
                                for kc in range(KC):
                                    last = e.matmul(ps[:, psb, :256], lhsT=hT[:, kc, tt * 128:(tt + 1) * 128], rhs=wv[:, kc, :],
                                                    start=(kc == 0), stop=(kc == KC - 1))
                                return last
                            P.add("pe", fn, reads=["wv"], writes=[f"ps{psb}"])
                            o = new_ost()
                            P.add("act", lambda e, psb=psb, o=o: e.activation(out=ost[o][:, :256], in_=ps[:, psb, :256], func=AF.Copy),
                                  reads=[f"ps{psb}"], writes=[f"ost{o}"])
                            if tt < 2:
                                dst = kvc[:, 1, tt, :]
                                wt = []
                            else:
                                dst = kv_in[128:256, (tt - 2) * 256:(tt - 1) * 256]
                                wt = [f"kv_in:v{tt}"]
                                kvtoks.append(wt[0])
                            P.dma("sp", f"ost{o}", dst, ost[o][:, :256], reads=[f"ost{o}"], writes=wt)

                    mtiles = TL[1:] if last else TL
                    HL = (T, 16)

                    def halo_pair(fn):
                        fn(272, 0)
                        fn(2328, 8)

                    def conv_triple(j):
                        Wa, Wb = W[2 * (j % 2)], W[2 * (j % 2) + 1]
                        ta, tb = f"W{2 * (j % 2)}", f"W{2 * (j % 2) + 1}"
                        wslot = load_w(24 + j)
                        for (c0, n) in mtiles:
                            psb = mm_tile(wslot, c0, n)
                            P.add("act", lambda e, psb=psb, c0=c0, n=n: e.activation(out=Wa[:, wcol(c0):wcol(c0) + n], in_=ps[:, psb, :n], func=AF.Copy),
                                  reads=[f"ps{psb}"], writes=[ta])
                        psb = mm_tile(wslot, *HL)
                        halo_pair(lambda wc, hc, psb=psb: P.add("dve", lambda e: e.tensor_tensor(
                            out=Wa[:, wc:wc + 8], in0=ps[:, psb, hc:hc + 8], in1=hmasks[:, hc:hc + 8], op=ALU.mult),
                            reads=[f"ps{psb}"], writes=[ta]))
                        wslot = load_w(20 + j)
                        for (c0, n) in mtiles:
                            psb = mm_tile(wslot, c0, n)
                            P.add("dve", lambda e, psb=psb, c0=c0, n=n: e.tensor_tensor(
                                out=Wa[:, wcol(c0):wcol(c0) + n], in0=ps[:, psb, :n], in1=Wa[:, wcol(c0):wcol(c0) + n], op=ALU.mult),
                                reads=[f"ps{psb}", ta], writes=[ta])
                        psb = mm_tile(wslot, *HL)
                        halo_pair(lambda wc, hc, psb=psb: P.add("dve", lambda e: e.tensor_tensor(
                            out=Wa[:, wc:wc + 8], in0=ps[:, psb, hc:hc + 8], in1=Wa[:, wc:wc + 8], op=ALU.mult),
                            reads=[f"ps{psb}", ta], writes=[ta]))
                        P.add("dve", lambda e: e.tensor_scalar(out=Wb[:, 1:WN - 1], in0=Wa[:, 1:WN - 1], scalar1=cws[:, l, j, 1:2], scalar2=None, op0=ALU.mult),
                              reads=[ta], writes=[tb])
                        P.add("dve", lambda e: e.scalar_tensor_tensor(out=Wb[:, 1:WN - 1], in0=Wa[:, 0:WN - 2], scalar=cws[:, l, j, 0:1],
                                                                      in1=Wb[:, 1:WN - 1], op0=ALU.mult, op1=ALU.add),
                              reads=[ta, tb], writes=[tb])
                        P.add("dve", lambda e: e.scalar_tensor_tensor(out=Wb[:, 1:WN - 1], in0=Wa[:, 2:WN], scalar=cws[:, l, j, 2:3],
                                                                      in1=Wb[:, 1:WN - 1], op0=ALU.mult, op1=ALU.add),
                              reads=[ta, tb], writes=[tb])
                        wslot = load_w(16 + j)
                        for (c0, n) in mtiles:
                            psb = mm_tile(wslot, c0, n)
                            o = new_ost()
                            P.add("dve", lambda e, psb=psb, c0=c0, n=n, o=o: e.tensor_tensor(
                                out=ost[o][:, :n], in0=ps[:, psb, :n], in1=Wb[:, wcol(c0):wcol(c0) + n], op=ALU.mult),
                                reads=[f"ps{psb}", tb], writes=[f"ost{o}"])
                            P.dma("sp", f"ost{o}", mixbuf[:, 12 + j, c0:c0 + n], ost[o][:, :n], reads=[f"ost{o}"])

                    def pool_chunk(g):
                        U, Bb, Cb = W[0], W[1], W[3]
                        w = (2, 4, 8, 16)[g]
                        wslot = load_w(12 + g)
                        for (c0, n) in mtiles:
                            psb = mm_tile(wslot, c0, n)
                            P.add("act", lambda e, psb=psb, c0=c0, n=n: e.activation(out=U[:, wcol(c0):wcol(c0) + n], in_=ps[:, psb, :n], func=AF.Copy),
                                  reads=[f"ps{psb}"], writes=["W0"])
                        psb = mm_tile(wslot, *HL)
                        halo_pair(lambda wc, hc, psb=psb: P.add("dve", lambda e: e.tensor_tensor(
                            out=U[:, wc:wc + 8], in0=ps[:, psb, hc:hc + 8], in1=hmasks[:, hc:hc + 8], op=ALU.mult),
                            reads=[f"ps{psb}"], writes=["W0"]))
                        N = WN

                        def add(dst, dlo, a, alo, b, blo, ln, rd, wt):
                            P.add("dve", lambda e: e.tensor_tensor(out=dst[:, dlo:dlo + ln], in0=a[:, alo:alo + ln], in1=b[:, blo:blo + ln], op=ALU.add),
                                  reads=rd, writes=wt)
                        if g == 0:
                            add(Bb, 1, U, 0, U, 1, N - 1, ["W0"], ["W1"])
                            S, ts = Bb, "W1"
                        else:
                            add(Bb, 0, U, 0, U, 1, N - 1, ["W0"], ["W1"])
                            if g == 1:
                                add(Cb, 2, Bb, 0, Bb, 2, N - 3, ["W1"], ["W3"])
                                S, ts = Cb, "W3"
                            else:
                                add(Cb, 0, Bb, 0, Bb, 2, N - 3, ["W1"], ["W3"])
                                if g == 2:
                                    add(Bb, 4, Cb, 0, Cb, 4, N - 7, ["W3"], ["W1"])
                                    S, ts = Bb, "W1"
                                else:
                                    add(Bb, 0, Cb, 0, Cb, 4, N - 7, ["W3"], ["W1"])
                                    add(Cb, 8, Bb, 0, Bb, 8, N - 15, ["W1"], ["W3"])
                                    S, ts = Cb, "W3"
                        P.add("dve", lambda e: e.scalar_tensor_tensor(out=Dbf[:, 8:N - 8], in0=S[:, 8:N - 8], scalar=1.0 / w, in1=U[:, 8:N - 8],
                                                                      op0=ALU.mult, op1=ALU.subtract),
                              reads=[ts, "W0"], writes=["Dbf"])
                        for ei, ec in enumerate((8, 256, 280, 2320)):
                            P.add("dve", lambda e, ei=ei, ec=ec: e.tensor_tensor(out=tmp8[:], in0=S[:, ec:ec + 8], in1=ptabs[:, g, ei, :], op=ALU.mult),
                                  reads=[ts], writes=["tmp8"])
                            P.add("dve", lambda e, ec=ec: e.tensor_tensor(out=Dbf[:, ec:ec + 8], in0=tmp8[:], in1=U[:, ec:ec + 8], op=ALU.subtract),
                                  reads=["tmp8", "W0", "Dbf"], writes=["Dbf"])
                        for (c0, n) in mtiles:
                            ab = 4 + cnt["aux"] % 4
                            cnt["aux"] += 1
                            P.add("pe", lambda e, ab=ab, c0=c0, n=n: e.matmul(ps[:, ab, :n], lhsT=wpools[:, l, g, :], rhs=Dbf[:, wcol(c0):wcol(c0) + n],
                                                                              start=True, stop=True),
                                  reads=["Dbf"], writes=[f"ps{ab}"])
                            o = new_ost()
                            P.add("act", lambda e, ab=ab, n=n, o=o: e.activation(out=ost[o][:, :n], in_=ps[:, ab, :n], func=AF.Copy, scale=pscs[:, l, g:g + 1]),
                                  reads=[f"ps{ab}"], writes=[f"ost{o}"])
                            P.dma("sp", f"ost{o}", mixbuf[:, 8 + g, c0:c0 + n], ost[o][:, :n], reads=[f"ost{o}"])

                    sub = stop_after.split(":")[1] if (stop_after and ":" in stop_after and stop_after.startswith(f"A1_{l}")) else "all"
                    qk_chunk(8, "k", 0)
                    qk_chunk(9, "k", 1)
                    if sub != "k":
                        v_chunk()
                    if sub not in ("k", "v"):
                        for j in range(4):
                            conv_triple(j)
                    if sub not in ("k", "v", "conv"):
                        for g in range(4):
                            pool_chunk(g)
                    if sub not in ("k", "v", "conv", "pool"):
                        for h in range(8):
                            qk_chunk(h, "q", h)
                    if sub not in ("k", "v", "conv", "pool", "q"):
                        for kvi in range(2):
                            P.add("pool", lambda e, kvi=kvi: e.collective_compute(
                                "AllGather", ALU.bypass, replica_groups=RG, ins=[kv_in[kvi * 128:(kvi + 1) * 128, :]], outs=[kv_all[kvi]]),
                                reads=kvtoks + ["kv_all"], writes=["kv_all"], final_wait=True)
                        if debug and l == 0:
                            P.dma("sp", "dbg_kv", kvdbg, kv_all, reads=["kv_all"])
                    P.emit(nc)
                    if sub != "all":
                        return True
            return stop_after == f"A1_{l}"

        def phase_B(l):
            last = (l == DEPTH - 1)
            with contextlib.ExitStack() as st:
                Kf = sb("Kf", [128, 2, NCTX + 4 * NLAT], BF16, st)
                Vf = sb("Vf", [128, 66, 256], BF16, st)
                qa = sb("qa", [128, 8, T], BF16, st)
                Pt = [sb(f"Pt{i}", [128, 2, 512], BF16, st) for i in range(4)]
                rec = [sb(f"rec{i}", [128, 512], F32, st) for i in range(2)]
                sacc = [sb(f"sacc{i}", [128, 2, 512], F32, st) for i in range(2)]
                ssb = sb("ssb", [128, 512], BF16, st)
                kva = kv_all.rearrange("k (r p) c -> k p r c", r=4, p=128)
                for h in range(2):
                    P.dma("sp", f"ldk{h}", Kf[:, h, NCTX:].rearrange("p (r t) -> p r t", r=4), kva[0][:, :, h * NLAT:(h + 1) * NLAT],
                          writes=["Kf"])
                P.dma("sp", "ldkc", Kf[:, :, 0:NCTX], kvc[:, 0], writes=["Kf"])
                P.dma("sp", "ldv", Vf[:, 2:66, :].rearrange("p (r i) d -> p r (i d)", r=4), kva[1], writes=["Vf"])
                P.dma("sp", "ldvc", Vf[:, 0:2, :], kvc[:, 1], writes=["Vf"])
                P.dma("sp", "ldq", qa[:], qbuf, writes=["qa"])
                iters = []
                if not last:
                    for h in range(8):
                        iters.append((h, 0, 256, [0, 1]))
                for h in range(8):
                    for (c0, n) in TL[1:]:
                        iters.append((h, c0, n, list(range(66))))
                import os as _os
                if _os.environ.get("KDBG_BITERS"):
                    _n = int(_os.environ["KDBG_BITERS"])
                    _keep = iters[:_n] + iters[8:8 + _n]
                    for (h_, c0_, n_, _k) in iters:
                        if (h_, c0_, n_, _k) not in _keep:
                            P.dma("sp", f"dbgq{h_ % 2}", mixbuf[:, h_, c0_:c0_ + n_], qa[:, h_, c0_:c0_ + n_], reads=["qa"])
                    iters = _keep
                cnt = {"s": 0, "p": 0}

                def do_iter(it, h, c0, n, kts):
                    ab = 0
                    kvh = h // 4
                    bo, bs = 6, 7
                    qtok = f"q:{h}:{c0}"
                    nk = len(kts)
                    npair = nk // 2
                    offl = SUM_OFFLOAD and npair >= 8

                    def mode(j):
                        if not offl:
                            return "pe"
                        return ("pe", "dve", "pool", "dve", "dve", "pe", "dve", "pool", "dve", "dve", "pe", "dve", "pool", "dve", "pool")[j % 15]
                    pe_js = [j for j in range(npair) if mode(j) == "pe"]
                    sbank = {}
                    pslot = {}
                    first = {"dve": True, "pool": True}

                    def S2(j):
                        b0 = 2 * (cnt["s"] % 3)
                        cnt["s"] += 1
                        sbank[j] = b0
                        k0, k1 = kts[2 * j], kts[2 * j + 1]

                        def fn(e, b0=b0, k0=k0, k1=k1):
                            e.matmul(ps[:, b0, :n], lhsT=Kf[:, kvh, k0 * 128:(k0 + 1) * 128], rhs=qa[:, h, c0:c0 + n], start=True, stop=True)
                            return e.matmul(ps[:, b0 + 1, :n], lhsT=Kf[:, kvh, k1 * 128:(k1 + 1) * 128], rhs=qa[:, h, c0:c0 + n], start=True, stop=True)
                        P.add("pe", fn, reads=["Kf", "qa", qtok], writes=[f"ps{b0}", f"ps{b0 + 1}"])

                    def E2(j):
                        b0 = sbank[j]
                        p = cnt["p"] % 4
                        cnt["p"] += 1
                        pslot[j] = p
                        P.add("act", lambda e, b0=b0, p=p: e.activation(out=Pt[p][:, :, :n], in_=ps[:, b0:b0 + 2, :n], func=AF.Exp, scale=SCALE),
                              reads=[f"ps{b0}", f"ps{b0 + 1}"], writes=[f"Pt{p}"])

                    def PV2(j):
                        p = pslot[j]
                        k0, k1 = kts[2 * j], kts[2 * j + 1]
                        md = mode(j)

                        def fn(e, p=p, k0=k0, k1=k1, j=j, md=md):
                            e.matmul(ps[:, bo, :n], lhsT=Vf[:, k0, kvh * 128:(kvh + 1) * 128], rhs=Pt[p][:, 0, :n], start=(j == 0), stop=False)
                            r = e.matmul(ps[:, bo, :n], lhsT=Vf[:, k1, kvh * 128:(kvh + 1) * 128], rhs=Pt[p][:, 1, :n], start=False, stop=(j == npair - 1))
                            if md == "pe":
                                lastpe = (j == pe_js[-1]) and not offl
                                e.matmul(ps[:, bs, :n], lhsT=ones[:], rhs=Pt[p][:, 0, :n], start=(j == 0), stop=False)
                                r = e.matmul(ps[:, bs, :n], lhsT=ones[:], rhs=Pt[p][:, 1, :n], start=False, stop=lastpe)
                            return r
                        P.add("pe", fn, reads=[f"Pt{p}", "Vf"], writes=[f"acc{ab}"])
                        if md != "pe":
                            a = 0 if md == "dve" else 1
                            if first[md]:
                                first[md] = False
                                P.add(md, lambda e, a=a, p=p: e.tensor_copy(out=sacc[a][:, :, :n], in_=Pt[p][:, :, :n]),
                                      reads=[f"Pt{p}"], writes=[f"sacc{a}"])
                            else:
                                P.add(md, lambda e, a=a, p=p: e.tensor_tensor(out=sacc[a][:, :, :n], in0=sacc[a][:, :, :n], in1=Pt[p][:, :, :n], op=ALU.add),
                                      reads=[f"Pt{p}", f"sacc{a}"], writes=[f"sacc{a}"])
                    LA = 2
                    for j in range(min(LA, npair)):
                        S2(j)
                    for j in range(npair):
                        if j + LA < npair:
                            S2(j + LA)
                        E2(j)
                        PV2(j)
                    if offl:
                        P.add("dve", lambda e: e.tensor_tensor(out=sacc[0][:, 0, :n], in0=sacc[0][:, 0, :n], in1=sacc[0][:, 1, :n], op=ALU.add),
                              reads=["sacc0"], writes=["sacc0"])
                        P.add("pool", lambda e: e.tensor_tensor(out=sacc[1][:, 0, :n], in0=sacc[1][:, 0, :n], in1=sacc[1][:, 1, :n], op=ALU.add),
                              reads=["sacc1"], writes=["sacc1"])
                        P.add("dve", lambda e: e.tensor_tensor(out=ssb[:, :n], in0=sacc[0][:, 0, :n], in1=sacc[1][:, 0, :n], op=ALU.add),
                              reads=["sacc0", "sacc1"], writes=["ssb"])
                        P.add("pe", lambda e: e.matmul(ps[:, bs, :n], lhsT=ones[:], rhs=ssb[:, :n], start=False, stop=True),
                              reads=["ssb", f"acc{ab}"], writes=[f"acc{ab}"])
                    P.add("dve", lambda e: e.reciprocal(out=rec[ab][:, :n], in_=ps[:, bs, :n]),
                          reads=[f"acc{ab}"], writes=[f"rec{ab}"])
                    P.add("dve", lambda e: e.tensor_tensor(out=qa[:, h, c0:c0 + n], in0=ps[:, bo, :n], in1=rec[ab][:, :n], op=ALU.mult),
                          reads=[f"acc{ab}", f"rec{ab}"], writes=[qtok, f"acc{ab}"])
                    P.dma("sp", f"sta{it % 4}", mixbuf[:, h, c0:c0 + n], qa[:, h, c0:c0 + n], reads=[qtok])
                for it, (h, c0, n, kts) in enumerate(iters):
                    do_iter(it, h, c0, n, kts)
                P.emit(nc)
            return stop_after == f"B_{l}"

        def phase_C(l):
            last = (l == DEPTH - 1)
            with contextlib.ExitStack() as st:
                mix = sb("mix", [128, KC, T], BF16, st)
                wr = [sb(f"wo{i}", [128, KC, 128], BF16, st) for i in range(3)]
                xs = [sb(f"xc{i}", [128, 512], F32, st) for i in range(4)]
                src = xT if l == 0 else xres
                tiles = TL[1:] if last else TL
                P.dma("sp", "ldmix", mix[:], mixbuf, writes=["mix"])
                k = 0
                for m in range(KC):
                    ws = m % 3
                    P.dma("pool", f"wo{ws}", wr[ws][:], wout_(l)[m], writes=[f"wo{ws}"])
                    for (c0, n) in tiles:
                        s = 1 if c0 < NCTX else 0
                        xi = k % 4
                        psb = k % 4
                        k += 1
                        P.dma("sp", f"xc{xi}", xs[xi][:, :n], src[:, m, c0:c0 + n], writes=[f"xc{xi}"])

                        def fn(e, ws=ws, psb=psb, c0=c0, n=n):
                            last_ = None
                            for kc in range(KC):
                                last_ = e.matmul(ps[:, psb, :n], lhsT=wr[ws][:, kc, :], rhs=mix[:, kc, c0:c0 + n], start=(kc == 0), stop=(kc == KC - 1))
                            return last_
                        P.add("pe", fn, reads=[f"wo{ws}", "mix"], writes=[f"ps{psb}"])
                        P.add("dve", lambda e, xi=xi, psb=psb, n=n, m=m, s=s: e.scalar_tensor_tensor(
                            out=xs[xi][:, :n], in0=ps[:, psb, :n], scalar=mv(l, 2, m, s), in1=xs[xi][:, :n], op0=ALU.mult, op1=ALU.add),
                            reads=[f"ps{psb}", f"xc{xi}"], writes=[f"xc{xi}"])
                        P.dma("sp", f"xo{xi}", xres[:, m, c0:c0 + n], xs[xi][:, :n], reads=[f"xc{xi}"])
                P.emit(nc)
            return stop_after == f"C_{l}"

        def phase_D(l):
            last = (l == DEPTH - 1)
            halves = [TL[1:3], TL[3:5]] if last else [TL[0:3], TL[3:5]]
            G = 4

            def blk(a0, n):
                return list(range(a0 // 256, (a0 + n) // 256))

            def acct(a0, n):
                return [f"acc:{k}" for k in blk(a0, n)]

            def accwt(m, a0, n):
                return [f"accw:{m}:{k}" for k in blk(a0, n)]
            with contextlib.ExitStack() as st:
                acc = sb("acc", [128, KC, 1280], F32, st)
                h2 = sb("h2", [128, KC, 1280], BF16, st)
                w1 = [sb(f"w1_{i}", [128, KC, 128], BF16, st) for i in range(6)]
                w2 = [sb(f"w2_{i}", [128, D], BF16, st) for i in range(6)]
                ut = [sb(f"ut{i}", [128, G, 512], BF16, st) for i in range(2)]
                rl = [sb(f"rl{i}", [128, 512], F32, st) for i in range(2)]
                nb = NormBufs(st, "d", 256)
                for hi, tiles in enumerate(halves):
                    hc0 = tiles[0][0]
                    hn = sum(n for _, n in tiles)
                    acctoks = [f"accw:{m}:{c0 - hc0}" for m in range(KC) for (c0, n) in tiles]
                    for ti, (c0_, n_) in enumerate(tiles):
                        P.dma("sp", f"ldacc{ti}", acc[:, :, c0_ - hc0:c0_ - hc0 + n_], xres[:, :, c0_:c0_ + n_],
                              writes=acct(c0_ - hc0, n_) + [t_ for m in range(KC) for t_ in accwt(m, c0_ - hc0, n_)])
                    k = 0
                    for (c0, n) in tiles:
                        s = 1 if c0 < NCTX else 0
                        for o in range(0, n, 256):
                            a0 = c0 - hc0 + o
                            emit_norm(nb, lambda c, a0=a0: acc[:, c, a0:a0 + 256], 256,
                                      lambda c, a0=a0: h2[:, c, a0:a0 + 256], l, 1, s, k % 2, acct(a0, 256), [f"h2:{a0}"])
                            k += 1
                    cnt = {"u": 0, "y": 0, "ut": 0, "rl": 0}
                    for fg in range(NF // G):
                        slots = []
                        for fi in range(G):
                            f = fg * G + fi
                            sl = f % 6
                            slots.append(sl)
                            P.dma("pool", f"w1_{sl}", w1[sl][:], wff1_(l)[f], writes=[f"w1_{sl}"])
                            P.dma("pool", f"w2_{sl}", w2[sl][:], wff2_(l)[f], writes=[f"w2_{sl}"])
                        for (c0, n) in tiles:
                            s = 1 if c0 < NCTX else 0
                            a0 = c0 - hc0
                            h2toks = [f"h2:{a0 + o}" for o in range(0, n, 256)]
                            ui = cnt["ut"] % 2
                            cnt["ut"] += 1
                            for fi in range(G):
                                sl = slots[fi]
                                ub = cnt["u"] % 4
                                cnt["u"] += 1

                                def fn(e, sl=sl, ub=ub, a0=a0, n=n):
                                    last_ = None
                                    for kc in range(KC):
                                        last_ = e.matmul(ps[:, ub, :n], lhsT=w1[sl][:, kc, :], rhs=h2[:, kc, a0:a0 + n], start=(kc == 0), stop=(kc == KC - 1))
                                    return last_
                                P.add("pe", fn, reads=[f"w1_{sl}"] + h2toks, writes=[f"ps{ub}"])
                                ri = cnt["rl"] % 2
                                cnt["rl"] += 1
                                P.add("act", lambda e, ub=ub, ri=ri, n=n: e.activation(out=rl[ri][:, :n], in_=ps[:, ub, :n], func=AF.Relu),
                                      reads=[f"ps{ub}"], writes=[f"rl{ri}"])
                                P.add("dve", lambda e, ri=ri, ui=ui, fi=fi, n=n: e.tensor_tensor(out=ut[ui][:, fi, :n], in0=rl[ri][:, :n], in1=rl[ri][:, :n], op=ALU.mult),
                                      reads=[f"rl{ri}"], writes=[f"ut{ui}"])
                            for m in range(KC):
                                yb = 4 + cnt["y"] % 4
                                cnt["y"] += 1

                                def fn2(e, yb=yb, m=m, ui=ui, n=n, slots=tuple(slots)):
                                    last_ = None
                                    for fi in range(G):
                                        last_ = e.matmul(ps[:, yb, :n], lhsT=w2[slots[fi]][:, m * 128:(m + 1) * 128], rhs=ut[ui][:, fi, :n],
                                                         start=(fi == 0), stop=(fi == G - 1))
                                    return last_
                                P.add("pe", fn2, reads=[f"w2_{sl}" for sl in slots] + [f"ut{ui}"], writes=[f"ps{yb}"])
                                P.add("dve", lambda e, yb=yb, m=m, a0=a0, n=n, s=s: e.scalar_tensor_tensor(
                                    out=acc[:, m, a0:a0 + n], in0=ps[:, yb, :n], scalar=mv(l, 5, m, s), in1=acc[:, m, a0:a0 + n],
                                    op0=ALU.mult, op1=ALU.add), reads=[f"ps{yb}"] + accwt(m, a0, n), writes=accwt(m, a0, n))
                    for ti, (c0_, n_) in enumerate(tiles):
                        a_ = c0_ - hc0
                        tt = [t_ for m in range(KC) for t_ in accwt(m, a_, n_)] + acct(a_, n_)
                        dst = outT[:, :, c0_ - NCTX:c0_ - NCTX + n_] if last else xres[:, :, c0_:c0_ + n_]
                        P.dma("sp", f"stout{ti}", dst, acc[:, :, a_:a_ + n_], reads=tt, writes=[f"accst{ti}"])
                    if not last:
                        ed = edge_in.rearrange("p (c t) -> p c t", t=16)
                        if hi == 0:
                            a = NCTX - hc0
                            P.dma("sp", "edge0", ed[:, :, 0:8], acc[:, :, a:a + 8],
                                  reads=[t_ for m in range(KC) for t_ in accwt(m, a, 256)] + acct(a, 256), writes=["edge0"])
                        else:
                            a = T - 8 - hc0
                            a_t = (a // 256) * 256
                            P.dma("sp", "edge1", ed[:, :, 8:16], acc[:, :, a:a + 8],
                                  reads=[t_ for m in range(KC) for t_ in accwt(m, a_t, 256)] + acct(a_t, 256), writes=["edge1"])
                if not last:
                    P.add("pool", lambda e: e.collective_compute("AllGather", ALU.bypass, replica_groups=RG, ins=[edge_in], outs=[edge_all]),
                          reads=["edge0", "edge1"], writes=["edge_all"], final_wait=True)
                P.emit(nc)
            return stop_after == f"D_{l}"

        phase_M()
        done = stop_after == "M"
        for l in range(nl):
            if done:
                break
            done = phase_A(l)
            if done:
                break
            done = phase_B(l)
            if done:
                break
            done = phase_C(l)
            if done:
                break
            done = phase_D(l)
        if done or True:
            pass
    return nc


def _perm():
    i = np.arange(128)
    return np.where((i // 32) % 2 == 0, i + 32, i - 32)


def _rope_tables(pos):
    n = pos.shape[0]
    row = (pos // 64).astype(np.float32)
    col = (pos % 64).astype(np.float32)
    inv = (np.float32(10000.0) ** (-(np.arange(32, dtype=np.float32)) / np.float32(32))).astype(np.float32)
    ar = row[:, None] * inv
    ac = col[:, None] * inv
    ang = np.concatenate([ar, ar, ac, ac], -1)
    return np.cos(ang).astype(np.float32), np.sin(ang).astype(np.float32)


def prep_shared(inp):
    f = np.float32
    sh = {}
    for l in range(DEPTH):
        sh[f"wmod{l}"] = np.ascontiguousarray(inp["w_mod"][l].reshape(16, 128, 96, 128).transpose(2, 1, 0, 3))
        sh[f"win{l}"] = np.ascontiguousarray(inp["w_in"][l].reshape(16, 128, 28, 128).transpose(2, 1, 0, 3))
        sh[f"wout{l}"] = np.ascontiguousarray(inp["w_out"][l].reshape(16, 128, 16, 128).transpose(2, 1, 0, 3))
        sh[f"wff1{l}"] = np.ascontiguousarray(inp["w_ff1"][l].reshape(16, 128, 64, 128).transpose(2, 1, 0, 3))
        sh[f"wff2{l}"] = np.ascontiguousarray(inp["w_ff2"][l].reshape(64, 128, 2048))
    bm = inp["b_mod"].reshape(2, 96, 128).transpose(2, 0, 1)
    sh["bmod"] = np.ascontiguousarray(np.repeat(bm[..., None], 2, -1)).astype(f)
    g = np.stack([inp["norm1_g"], inp["norm2_g"]], 1)
    sh["g12"] = np.ascontiguousarray(g.reshape(2, 2, 16, 128).transpose(3, 0, 1, 2)).astype(f)
    pm = _perm()
    qg, kg = inp["q_norm_g"], inp["k_norm_g"]
    sh["qkg"] = np.ascontiguousarray(np.stack([qg, qg[:, pm], kg, kg[:, pm]], -1).transpose(1, 0, 2)).astype(f)
    rm = np.zeros((128, 128), np.float32)
    rm[pm, np.arange(128)] = 1.0
    sh["rmat"] = rm.astype(ml_dtypes.bfloat16)
    sh["wpool"] = np.ascontiguousarray(inp["w_pool"].transpose(2, 0, 1, 3)).astype(f)
    sh["pscale"] = np.ascontiguousarray(inp["pool_scale"].reshape(2, 4, 128).transpose(2, 0, 1)).astype(f)
    sh["convw"] = np.ascontiguousarray(inp["conv_w"].reshape(2, 3, 4, 128).transpose(3, 0, 2, 1)).astype(f)
    return sh


def prep_core(inp, r):
    f = np.float32
    b, j = r // 4, r % 4
    x = inp["x"]
    lo, hi = j * NLAT, (j + 1) * NLAT
    cat = np.concatenate([inp["ctx"][b], x[b, lo:hi]], 0)
    m = {}
    m["xT"] = np.ascontiguousarray(cat.reshape(T, 16, 128).transpose(2, 1, 0))
    hl = x[b, lo - 8:lo] if j > 0 else np.zeros((8, D), f)
    hr = x[b, hi:hi + 8] if j < 3 else np.zeros((8, D), f)
    m["xh"] = np.ascontiguousarray(np.concatenate([hl, hr], 0).reshape(16, 16, 128).transpose(2, 1, 0)).astype(f)
    cv = np.stack([inp["c"][b], inp["c_ctx"]], -1)
    m["cv"] = np.ascontiguousarray(cv.reshape(16, 128, 2).transpose(1, 0, 2)).astype(f)
    cos, sin = _rope_tables(np.arange(lo, hi))
    sign = np.where((np.arange(128) // 32) % 2 == 0, -1.0, 1.0).astype(f)
    cs = np.zeros((128, 2, T), f)
    cs[:, 0, :NCTX] = 1.0
    cs[:, 0, NCTX:] = cos.T
    cs[:, 1, NCTX:] = (sin * sign[None, :]).T
    m["cs"] = cs
    pt = np.zeros((4, 4, 8), f)

    def inv_cnt(tpos, n, w):
        lo_ = w // 2
        hi_ = w - 1 - lo_
        st = np.clip(tpos - lo_, 0, n)
        en = np.clip(tpos + hi_ + 1, 0, n)
        return (1.0 / (en - st)).astype(f)
    for g, w in enumerate((2, 4, 8, 16)):
        pt[g, 0] = inv_cnt(np.arange(0, 8), NCTX, w)
        pt[g, 1] = inv_cnt(np.arange(NCTX - 8, NCTX), NCTX, w)
        pt[g, 2] = inv_cnt(np.arange(lo, lo + 8), 4 * NLAT, w)
        pt[g, 3] = inv_cnt(np.arange(hi - 8, hi), 4 * NLAT, w)
    m["ptab"] = np.ascontiguousarray(np.broadcast_to(pt[None], (128, 4, 4, 8))).astype(f)
    hm = np.zeros((16,), f)
    hm[0:8] = 1.0 if j > 0 else 0.0
    hm[8:16] = 1.0 if j < 3 else 0.0
    m["hmask"] = np.ascontiguousarray(np.broadcast_to(hm[None], (128, 16))).astype(f)
    hs = np.zeros((8,), f)
    if j > 0:
        hs[j - 1] = 1.0
    if j < 3:
        hs[4 + j + 1] = 1.0
    m["hsel"] = np.ascontiguousarray(np.broadcast_to(hs[None], (128, 8))).astype(f)
    return m


def make_in_maps(inp):
    inp = {k: np.asarray(v) for k, v in inp.items()}
    sh = prep_shared(inp)
    maps = []
    for r in range(8):
        m = prep_core(inp, r)
        m.update(sh)
        maps.append(m)
    return maps


def assemble(results):
    out = np.empty((2, 4 * NLAT, D), np.float32)
    for r in range(8):
        b, j = r // 4, r % 4
        o = np.asarray(results[r]["outT"])
        out[b, j * NLAT:(j + 1) * NLAT] = o.transpose(2, 1, 0).reshape(NLAT, D)
    return out


def kernel(**inputs):
    maps = make_in_maps(inputs)
    nc = build()
    res = run_bass_kernel_spmd(nc, maps, core_ids=list(range(8)))
    return assemble(res.results)
```

```python
import contextlib
import math
import numpy as np
import ml_dtypes
import concourse.bass as bass
import concourse.mybir as mybir
from concourse.bass_utils import run_bass_kernel_spmd

F32 = mybir.dt.float32
BF16 = mybir.dt.bfloat16
ALU = mybir.AluOpType
AF = mybir.ActivationFunctionType

D = 2048
KC = 16
NCTX = 256
NLAT = 2048
T = NCTX + NLAT
DEPTH = 2
NF = 64
EPS = 1e-6
SCALE = 1.0 / math.sqrt(128.0)
TL = [(0, 256), (256, 512), (768, 512), (1280, 512), (1792, 512)]
WN = 2336
ENGS = ("pe", "act", "dve", "pool", "sp")
SUM_OFFLOAD = True


class _Op:
    __slots__ = ("eng", "fn", "reads", "writes", "dma_key", "deps", "idx", "need_inc", "cnt", "dma_val", "final_wait")


class Prog:
    def __init__(self):
        self.ops = []
        self.last_w = {}
        self.readers = {}
        self.nblk = 0

    def add(self, eng, fn, reads=(), writes=(), dma_key=None, final_wait=False):
        op = _Op()
        op.final_wait = final_wait
        op.eng, op.fn, op.reads, op.writes, op.dma_key = eng, fn, tuple(reads), tuple(writes), dma_key
        op.deps = set()
        op.need_inc = False
        op.cnt = None
        op.dma_val = None
        op.idx = len(self.ops)
        for r in op.reads:
            w = self.last_w.get(r)
            if w is not None:
                op.deps.add(w)
        for t in op.writes:
            w = self.last_w.get(t)
            if w is not None:
                op.deps.add(w)
            for rd in self.readers.get(t, {}).values():
                if rd is not op:
                    op.deps.add(rd)
        for r in op.reads:
            d = self.readers.setdefault(r, {})
            k = ("dma", op.idx) if dma_key is not None else op.eng
            d[k] = op
        for t in op.writes:
            self.last_w[t] = op
            self.readers[t] = {}
        if dma_key is None:
            drop = set()
            for d in op.deps:
                if d.dma_key is None and d.eng == op.eng and op.eng == "pe":
                    if not any(r in d.writes for r in op.reads):
                        drop.add(d)
            op.deps -= drop
        self.ops.append(op)
        return op

    def dma(self, eng, key, out, in_, reads=(), writes=(), **kw):
        def fn(e):
            return e.dma_start(out=out, in_=in_, **kw)
        return self.add(eng, fn, reads, writes, dma_key=key)

    def setup(self, nc, stack):
        self.nc = nc
        self.stack = stack
        self.esem = {e: stack.enter_context(nc.semaphore(f"s_{e}")) for e in ENGS}
        self.dsem = {}
        self.cnt = {e: 0 for e in ENGS}
        self.dcnt = {}
        self.waited = {e: {} for e in ENGS}

    def emit(self, nc):
        ops = self.ops
        self.nblk += 1
        for op in ops:
            if op.final_wait:
                op.need_inc = True
            for d in op.deps:
                d.need_inc = True
        cnt, dcnt, esem, dsem = self.cnt, self.dcnt, self.esem, self.dsem
        for op in ops:
            if op.dma_key is not None:
                if op.dma_key not in dsem:
                    dsem[op.dma_key] = self.stack.enter_context(nc.semaphore(f"d_{len(dsem)}"))
                dcnt[op.dma_key] = dcnt.get(op.dma_key, 0) + 16
                op.dma_val = dcnt[op.dma_key]
            elif op.need_inc:
                cnt[op.eng] += 1
                op.cnt = cnt[op.eng]
        with nc.Block() as block:
            by_eng = {e: [op for op in ops if op.eng == e] for e in ENGS}

            def run(engname, engobj):
                waited = self.waited[engname]
                for op in by_eng[engname]:
                    need = {}
                    for d in op.deps:
                        if d.dma_key is not None:
                            s, v = dsem[d.dma_key], d.dma_val
                        else:
                            s, v = esem[d.eng], d.cnt
                        k = id(s)
                        if need.get(k, (None, 0))[1] < v:
                            need[k] = (s, v)
                    for k, (s, v) in need.items():
                        if waited.get(k, 0) < v:
                            engobj.wait_ge(s, v)
                            waited[k] = v
                    ins = op.fn(engobj)
                    if op.dma_key is not None:
                        ins.then_inc(dsem[op.dma_key], 16)
                    elif op.need_inc:
                        ins.then_inc(esem[op.eng], 1)
                fw = [op.cnt for op in by_eng[engname] if op.final_wait]
                if fw and waited.get(id(esem[engname]), 0) < max(fw):
                    engobj.wait_ge(esem[engname], max(fw))
                    waited[id(esem[engname])] = max(fw)
                last = {}
                for op in by_eng[engname]:
                    if op.dma_key is not None:
                        last[op.dma_key] = op.dma_val
                for k, v in last.items():
                    if waited.get(id(dsem[k]), 0) < v:
                        engobj.wait_ge(dsem[k], v)
                        waited[id(dsem[k])] = v

            @block.tensor
            def _(e):
                run("pe", e)

            @block.scalar
            def _(e):
                run("act", e)

            @block.vector
            def _(e):
                run("dve", e)

            @block.gpsimd
            def _(e):
                run("pool", e)

            @block.sync
            def _(e):
                run("sp", e)
        self.ops = []
        self.last_w = {}
        self.readers = {}


def wcol(c0):
    return 8 + c0 if c0 < NCTX else c0 + 24


def build(stop_after=None, debug=False, ncores=8, nl=DEPTH):
    nc = bass.Bass("TRN2", target_bir_lowering=False)

    def din(name, shape, dt=F32):
        return nc.dram_tensor(name, list(shape), dt, kind="ExternalInput").ap()

    def dint(name, shape, dt, kind="Internal"):
        return nc.dram_tensor(name, list(shape), dt, kind=kind).ap()

    xT = din("xT", [128, KC, T])
    xh = din("xh", [128, KC, 16])
    cvd = din("cv", [128, KC, 2])
    _wc = {}

    def wl(name, l, shape):
        key = f"{name}{l}"
        if key not in _wc:
            _wc[key] = din(key, shape)
        return _wc[key]

    def wmod_(l):
        return wl("wmod", l, [96, 128, KC, 128])

    def win_(l):
        return wl("win", l, [28, 128, KC, 128])

    def wout_(l):
        return wl("wout", l, [KC, 128, KC, 128])

    def wff1_(l):
        return wl("wff1", l, [NF, 128, KC, 128])

    def wff2_(l):
        return wl("wff2", l, [NF, 128, D])
    bmod = din("bmod", [128, DEPTH, 96, 2])
    g12 = din("g12", [128, DEPTH, 2, KC])
    qkg = din("qkg", [128, DEPTH, 4])
    csd = din("cs", [128, 2, T])
    rmatd = din("rmat", [128, 128], BF16)
    wpoold = din("wpool", [128, DEPTH, 4, 128])
    pscaled = din("pscale", [128, DEPTH, 4])
    convwd = din("convw", [128, DEPTH, 4, 3])
    ptabd = din("ptab", [128, 4, 4, 8])
    hmaskd = din("hmask", [128, 16])
    hseld = din("hsel", [128, 8])
    outT = nc.dram_tensor("outT", [128, KC, NLAT], F32, kind="ExternalOutput").ap()

    dk = "ExternalOutput" if debug else "Internal"
    xres = dint("xres", [128, KC, T], F32, dk)
    qbuf = dint("qbuf", [128, 8, T], BF16, dk)
    mixbuf = dint("mixbuf", [128, KC, T], BF16, dk)
    kvc = dint("kvc", [128, 2, 2, 256], BF16, dk)
    hdbg = dint("hdbg", [128, KC, T + 16], BF16, dk) if debug else None
    moddbg = dint("moddbg", [128, DEPTH, 96, 2], F32, dk) if debug else None
    kv_in = dint("kv_in", [256, 4096], BF16)
    kv_all = dint("kv_all", [2, 512, 4096], BF16)
    kvdbg = dint("kvdbg", [2, 512, 4096], BF16, dk) if debug else None
    edge_in = dint("edge_in", [128, 256], F32)
    edge_all = dint("edge_all", [512, 256], F32)
    RG = [[0, 1, 2, 3], [4, 5, 6, 7]] if ncores == 8 else [[0, 1, 2, 3]]

    P = Prog()
    es = contextlib.ExitStack()
    _nm = [0]
    with es:
        P.setup(nc, es)

        def sb(name, shape, dt=F32, stack=es):
            _nm[0] += 1
            return stack.enter_context(nc.sbuf_tensor(f"{name}_{_nm[0]}", list(shape), dt))

        ps = es.enter_context(nc.psum_tensor("ps", [128, 8, 512], F32))
        modsb = sb("modsb", [128, DEPTH, 96, 2])
        A12 = sb("A12", [128, DEPTH, 2, KC, 2])
        g12s = sb("g12s", [128, DEPTH, 2, KC])
        qkgs = sb("qkgs", [128, DEPTH, 4])
        pscs = sb("pscs", [128, DEPTH, 4])
        cws = sb("cws", [128, DEPTH, 4, 3])
        ptabs = sb("ptabs", [128, 4, 4, 8])
        hmasks = sb("hmasks", [128, 16])
        hsels = sb("hsels", [128, 8])
        ones = sb("ones", [128, 128], BF16)
        rmats = sb("rmats", [128, 128], BF16)
        wpools = sb("wpools", [128, DEPTH, 4, 128], BF16)

        def mv(l, k, c, s):
            return modsb[:, l, k * 16 + c, s:s + 1]

        cact = sb("cact", [128, KC, 2], BF16)
        bms = sb("bms", [128, DEPTH, 96, 2], F32)
        tmpA = sb("tmpA", [128, KC, 2], F32)
        modcnt = {"g": 0}

        def mod_groups(l, groups, wm, psbank):
            psm = ps[:, psbank, 0:192].rearrange("p (e s) -> p e s", s=2)
            for (e0, ne) in groups:
                slot = modcnt["g"] % 2
                modcnt["g"] += 1
                P.dma("pool", f"wm{slot}", wm[slot][:, 0:ne],
                      wmod_(l)[e0:e0 + ne].rearrange("e p k m -> p e k m"), writes=[f"wm{slot}"])

                def mm(e, slot=slot, e0=e0, ne=ne, psm=psm):
                    last = None
                    for e4 in range(ne):
                        for kc in range(KC):
                            last = e.matmul(psm[:, e0 + e4, :], lhsT=wm[slot][:, e4, kc, :],
                                            rhs=cact[:, kc, :], start=(kc == 0), stop=(kc == KC - 1))
                    return last
                P.add("pe", mm, reads=[f"wm{slot}", "cact"], writes=[f"psmod{psbank}"])

        def mod_finish(l, e0, e1, psbank):
            psm = ps[:, psbank, 0:192].rearrange("p (e s) -> p e s", s=2)
            P.add("dve", lambda e: e.tensor_tensor(out=modsb[:, l, e0:e1], in0=psm[:, e0:e1], in1=bms[:, l, e0:e1], op=ALU.add),
                  reads=[f"psmod{psbank}", "bms"], writes=["modsb"])
            for w, ksc in enumerate((1, 4)):
                if e0 <= ksc * 16 and (ksc + 1) * 16 <= e1:
                    P.add("dve", lambda e, ksc=ksc: e.tensor_scalar(
                        out=tmpA[:], in0=modsb[:, l, ksc * 16:(ksc + 1) * 16, :], scalar1=1.0, scalar2=None, op0=ALU.add),
                        reads=["modsb"], writes=["tmpA"])
                    P.add("dve", lambda e, w=w: e.tensor_tensor(
                        out=A12[:, l, w], in0=tmpA[:],
                        in1=g12s[:, l, w, :].unsqueeze(2).broadcast_to([128, KC, 2]), op=ALU.mult),
                        reads=["tmpA", "g12s"], writes=["A12"])

        def phase_M():
            with contextlib.ExitStack() as st:
                cvs = sb("cvs", [128, KC, 2], F32, st)
                wm = [sb(f"wm{i}", [128, 4, KC, 128], BF16, st) for i in range(2)]
                P.dma("sp", "ld_cv", cvs[:], cvd, writes=["cvs"])
                P.dma("sp", "ld_bm", bms[:], bmod, writes=["bms"])
                P.dma("sp", "ld_g12", g12s[:], g12, writes=["g12s"])
                P.dma("sp", "ld_qkg", qkgs[:], qkg, writes=["qkgs"])
                P.dma("sp", "ld_psc", pscs[:], pscaled, writes=["pscs"])
                P.dma("sp", "ld_cw", cws[:], convwd, writes=["cws"])
                P.dma("sp", "ld_ptab", ptabs[:], ptabd, writes=["ptabs"])
                P.dma("sp", "ld_hm", hmasks[:], hmaskd, writes=["hmasks"])
                P.dma("sp", "ld_hs", hsels[:], hseld, writes=["hsels"])
                P.dma("sp", "ld_rm", rmats[:], rmatd, writes=["rmats"])
                P.dma("pool", "ld_wp", wpools[:], wpoold, writes=["wpools"])
                P.add("dve", lambda e: e.memset(ones[:], 1.0), writes=["ones"])
                P.add("act", lambda e: e.activation(out=cact[:], in_=cvs[:], func=AF.Silu),
                      reads=["cvs"], writes=["cact"])
                for l in range(nl):
                    mod_groups(l, [(e0, 4) for e0 in range(0, 32, 4)], wm, l)
                    mod_finish(l, 0, 32, l)
                P.emit(nc)

        class NormBufs:
            def __init__(self, st, tag, n):
                self.nr = 6
                self.sq = [sb(f"nsq{tag}{i}", [128, n], BF16, st) for i in range(self.nr)]
                self.tc = [sb(f"ntc{tag}{i}", [128, n], F32, st) for i in range(self.nr)]
                self.rt = sb(f"nrt{tag}", [128, n], F32, st)
                self.rstd = sb(f"nrstd{tag}", [128, n], F32, st)
                self.cnt = 0
                self.tag = tag

        def emit_norm(nb, xc_fn, n, out_fn, l, w, s, psb, rtok, wtok):
            tg = nb.tag
            for c in range(KC):
                i = nb.cnt % nb.nr
                nb.cnt += 1
                P.add("act", lambda e, c=c, i=i: e.activation(out=nb.sq[i][:, :n], in_=xc_fn(c), func=AF.Square),
                      reads=rtok, writes=[f"nsq{tg}{i}"])
                P.add("pe", lambda e, c=c, i=i: e.matmul(ps[:, psb, :n], lhsT=ones[:], rhs=nb.sq[i][:, :n],
                                                         start=(c == 0), stop=(c == KC - 1)),
                      reads=[f"nsq{tg}{i}", "ones"], writes=[f"ps{psb}"])
            P.add("act", lambda e: e.activation(out=nb.rt[:, :n], in_=ps[:, psb, :n], func=AF.Sqrt, bias=EPS, scale=1.0 / D),
                  reads=[f"ps{psb}"], writes=[f"nrt{tg}"])
            P.add("dve", lambda e: e.reciprocal(out=nb.rstd[:, :n], in_=nb.rt[:, :n]),
                  reads=[f"nrt{tg}"], writes=[f"nrstd{tg}"])
            ksh = 0 if w == 0 else 3
            for c in range(KC):
                i = nb.cnt % nb.nr
                nb.cnt += 1
                P.add("dve", lambda e, c=c, i=i: e.tensor_tensor(out=nb.tc[i][:, :n], in0=xc_fn(c), in1=nb.rstd[:, :n], op=ALU.mult),
                      reads=list(rtok) + [f"nrstd{tg}"], writes=[f"ntc{tg}{i}"])
                P.add("act", lambda e, c=c, i=i: e.activation(out=out_fn(c), in_=nb.tc[i][:, :n], func=AF.Identity,
                                                              bias=mv(l, ksh, c, s), scale=A12[:, l, w, c, s:s + 1]),
                      reads=[f"ntc{tg}{i}"], writes=wtok)

        def phase_A(l):
            last = (l == DEPTH - 1)
            with contextlib.ExitStack() as stA:
                hT = sb("hT", [128, KC, T + 16], BF16, stA)
                with contextlib.ExitStack() as st:
                    xs = [sb(f"xs{i}", [128, KC, 512], F32, st) for i in range(2)]
                    xhs = sb("xhs", [128, KC, 16], F32, st)
                    nb = NormBufs(st, "a", 512)
                    wma = [sb(f"wma{i}", [128, 2, KC, 128], BF16, st) for i in range(2)]
                    mgroups = [(e0, 2) for e0 in range(32, 96, 2)]
                    src = xT if l == 0 else xres
                    mod_groups(l, mgroups[0:8], wma, 4)
                    for i, (c0, n) in enumerate(TL):
                        slot = i % 2
                        s = 1 if c0 < NCTX else 0
                        P.dma("sp", f"xs{slot}", xs[slot][:, :, :n], src[:, :, c0:c0 + n], writes=[f"xs{slot}"])
                        emit_norm(nb, lambda c, slot=slot, n=n: xs[slot][:, c, :n], n,
                                  lambda c, c0=c0, n=n: hT[:, c, c0:c0 + n], l, 0, s, i % 2,
                                  [f"xs{slot}"], [f"hT{c0}"])
                        mod_groups(l, mgroups[8 + i * 5:8 + (i + 1) * 5], wma, 4)
                    mod_finish(l, 32, 96, 4)
                    if l == 0:
                        P.dma("sp", "xhs", xhs[:], xh, writes=["xhs"])
                    else:
                        egs = sb("egs", [128, 4, KC, 16], F32, st)
                        P.dma("sp", "egs", egs[:], edge_all.rearrange("(r p) (c t) -> p r c t", p=128, t=16), writes=["egs"])
                        for side in range(2):
                            dst = xhs[:, :, side * 8:(side + 1) * 8]
                            for r in range(4):
                                srcc = egs[:, r, :, (1 - side) * 8:(2 - side) * 8]
                                sc = hsels[:, side * 4 + r:side * 4 + r + 1]
                                if r == 0:
                                    P.add("dve", lambda e, dst=dst, srcc=srcc, sc=sc: e.tensor_scalar(
                                        out=dst, in0=srcc, scalar1=sc, scalar2=None, op0=ALU.mult),
                                        reads=["egs"], writes=["xhs"])
                                else:
                                    P.add("dve", lambda e, dst=dst, srcc=srcc, sc=sc: e.scalar_tensor_tensor(
                                        out=dst, in0=srcc, scalar=sc, in1=dst, op0=ALU.mult, op1=ALU.add),
                                        reads=["egs", "xhs"], writes=["xhs"])
                    emit_norm(nb, lambda c: xhs[:, c, :], 16, lambda c: hT[:, c, T:T + 16], l, 0, 0, 2, ["xhs"], ["hTh"])
                    if debug and l == 0:
                        P.dma("sp", "dbg_h", hdbg, hT[:], reads=[f"hT{c0}" for c0, _ in TL] + ["hTh"])
                    P.emit(nc)
                if stop_after == f"A0_{l}":
                    return True
                with contextlib.ExitStack() as st:
                    wr = [sb(f"wr{i}", [128, KC, 128], BF16, st) for i in range(3)]
                    wv = sb("wv", [128, KC, 256], BF16, st)
                    css = sb("css", [128, 2, T], F32, st)
                    W = [sb(f"W{i}", [128, WN], F32, st) for i in range(4)]
                    Dbf = sb("Dbf", [128, WN], BF16, st)
                    tmp8 = sb("tmp8", [128, 8], F32, st)
                    sqb = [sb(f"sqb{i}", [128, 512], BF16, st) for i in range(2)]
                    qbb = [sb(f"qbb{i}", [128, 512], BF16, st) for i in range(2)]
                    rt = [sb(f"rt{i}", [128, 512], F32, st) for i in range(2)]
                    rstd = [sb(f"rstd{i}", [128, 512], F32, st) for i in range(2)]
                    u1 = [sb(f"u1{i}", [128, 512], F32, st) for i in range(2)]
                    u2 = [sb(f"u2{i}", [128, 512], F32, st) for i in range(2)]
                    ost = [sb(f"ost{i}", [128, 512], BF16, st) for i in range(3)]
                    P.dma("sp", "ld_cs", css[:], csd, writes=["css"])
                    for i in range(4):
                        P.add("dve", lambda e, i=i: e.memset(W[i][:], 0.0), writes=[f"W{i}"])
                    cnt = {"w": 0, "mb": 0, "ep": 0, "ost": 0, "aux": 0}

                    def load_w(chunk):
                        slot = cnt["w"] % 3
                        cnt["w"] += 1
                        P.dma("pool", f"wr{slot}", wr[slot][:], win_(l)[chunk], writes=[f"wr{slot}"])
                        return slot

                    def mm_tile(wslot, c0, n):
                        psb = cnt["mb"] % 4
                        cnt["mb"] += 1

                        def fn(e):
                            last = None
                            for kc in range(KC):
                                last = e.matmul(ps[:, psb, :n], lhsT=wr[wslot][:, kc, :], rhs=hT[:, kc, c0:c0 + n],
                                                start=(kc == 0), stop=(kc == KC - 1))
                            return last
                        P.add("pe", fn, reads=[f"wr{wslot}"], writes=[f"ps{psb}"])
                        return psb

                    def new_ost():
                        o = cnt["ost"] % 3
                        cnt["ost"] += 1
                        return o

                    def qk_chunk(chunk, kind, hidx):
                        wslot = load_w(chunk)
                        gcol = 0 if kind == "q" else 2
                        tiles = TL if (kind == "k" or not last) else TL[1:]
                        for (c0, n) in tiles:
                            psb = mm_tile(wslot, c0, n)
                            i = cnt["ep"] % 2
                            cnt["ep"] += 1
                            ssb, rtb = 4 + i, 6 + i
                            P.add("act", lambda e, psb=psb, i=i, n=n: e.activation(out=sqb[i][:, :n], in_=ps[:, psb, :n], func=AF.Square),
                                  reads=[f"ps{psb}"], writes=[f"sqb{i}"])
                            P.add("act", lambda e, psb=psb, i=i, n=n: e.activation(out=qbb[i][:, :n], in_=ps[:, psb, :n], func=AF.Copy),
                                  reads=[f"ps{psb}"], writes=[f"qbb{i}"])
                            P.add("pe", lambda e, ssb=ssb, i=i, n=n: e.matmul(ps[:, ssb, :n], lhsT=ones[:], rhs=sqb[i][:, :n], start=True, stop=True),
                                  reads=[f"sqb{i}"], writes=[f"ps{ssb}"])
                            P.add("pe", lambda e, rtb=rtb, i=i, n=n: e.matmul(ps[:, rtb, :n], lhsT=rmats[:], rhs=qbb[i][:, :n], start=True, stop=True),
                                  reads=[f"qbb{i}"], writes=[f"ps{rtb}"])
                            P.add("act", lambda e, ssb=ssb, i=i, n=n: e.activation(out=rt[i][:, :n], in_=ps[:, ssb, :n], func=AF.Sqrt,
                                                                                   bias=EPS, scale=1.0 / 128.0),
                                  reads=[f"ps{ssb}"], writes=[f"rt{i}"])
                            P.add("dve", lambda e, i=i, n=n: e.reciprocal(out=rstd[i][:, :n], in_=rt[i][:, :n]),
                                  reads=[f"rt{i}"], writes=[f"rstd{i}"])
                            P.add("dve", lambda e, psb=psb, i=i, n=n, c0=c0: e.scalar_tensor_tensor(
                                out=u1[i][:, :n], in0=ps[:, psb, :n], scalar=qkgs[:, l, gcol:gcol + 1], in1=css[:, 0, c0:c0 + n],
                                op0=ALU.mult, op1=ALU.mult), reads=[f"ps{psb}", "css"], writes=[f"u1{i}"])
                            P.add("dve", lambda e, rtb=rtb, i=i, n=n, c0=c0: e.scalar_tensor_tensor(
                                out=u2[i][:, :n], in0=ps[:, rtb, :n], scalar=qkgs[:, l, gcol + 1:gcol + 2], in1=css[:, 1, c0:c0 + n],
                                op0=ALU.mult, op1=ALU.mult), reads=[f"ps{rtb}", "css"], writes=[f"u2{i}"])
                            P.add("dve", lambda e, i=i, n=n: e.tensor_tensor(out=u1[i][:, :n], in0=u1[i][:, :n], in1=u2[i][:, :n], op=ALU.add),
                                  reads=[f"u1{i}", f"u2{i}"], writes=[f"u1{i}"])
                            o = new_ost()
                            P.add("dve", lambda e, i=i, n=n, o=o: e.tensor_tensor(out=ost[o][:, :n], in0=u1[i][:, :n], in1=rstd[i][:, :n], op=ALU.mult),
                                  reads=[f"u1{i}", f"rstd{i}"], writes=[f"ost{o}"])
                            if kind == "q":
                                dst = qbuf[:, hidx, c0:c0 + n]
                                wt = []
                            elif c0 < NCTX:
                                dst = kvc[:, 0, hidx, :]
                                wt = []
                            else:
                                dst = kv_in[0:128, hidx * NLAT + c0 - NCTX: hidx * NLAT + c0 - NCTX + n]
                                wt = [f"kv_in:k{hidx}:{c0}"]
                                kvtoks.append(wt[0])
                            P.dma("sp", f"ost{o}", dst, ost[o][:, :n], reads=[f"ost{o}"], writes=wt)

                    kvtoks = []

                    def v_chunk():
                        for j in range(2):
                            P.dma("pool", f"wv{j}", wv[:, :, j * 128:(j + 1) * 128], win_(l)[10 + j], writes=["wv"])
                        for tt in range(18):
                            psb = cnt["mb"] % 4
                            cnt["mb"] += 1

                            def fn(e, tt=tt, psb=psb):
                                last = None
                                for kc in range(KC):
                                    last = e.matmul(ps[:, psb, :256], lhsT=hT[:, kc, tt * 128:(tt + 1) * 128], rhs=wv[:, kc, :],
                                                    start=(kc == 0), stop=(kc == KC - 1))
                                return last
                            P.add("pe", fn, reads=["wv"], writes=[f"ps{psb}"])
                            o = new_ost()
                            P.add("act", lambda e, psb=psb, o=o: e.activation(out=ost[o][:, :256], in_=ps[:, psb, :256], func=AF.Copy),
                                  reads=[f"ps{psb}"], writes=[f"ost{o}"])
                            if tt < 2:
                                dst = kvc[:, 1, tt, :]
                                wt = []
                            else:
                                dst = kv_in[128:256, (tt - 2) * 256:(tt - 1) * 256]
                                wt = [f"kv_in:v{tt}"]
                                kvtoks.append(wt[0])
                            P.dma("sp", f"ost{o}", dst, ost[o][:, :256], reads=[f"ost{o}"], writes=wt)

                    mtiles = TL[1:] if last else TL
                    HL = (T, 16)

                    def halo_pair(fn):
                        fn(272, 0)
                        fn(2328, 8)

                    def conv_triple(j):
                        Wa, Wb = W[2 * (j % 2)], W[2 * (j % 2) + 1]
                        ta, tb = f"W{2 * (j % 2)}", f"W{2 * (j % 2) + 1}"
                        wslot = load_w(24 + j)
                        for (c0, n) in mtiles:
                            psb = mm_tile(wslot, c0, n)
                            P.add("act", lambda e, psb=psb, c0=c0, n=n: e.activation(out=Wa[:, wcol(c0):wcol(c0) + n], in_=ps[:, psb, :n], func=AF.Copy),
                                  reads=[f"ps{psb}"], writes=[ta])
                        psb = mm_tile(wslot, *HL)
                        halo_pair(lambda wc, hc, psb=psb: P.add("dve", lambda e: e.tensor_tensor(
                            out=Wa[:, wc:wc + 8], in0=ps[:, psb, hc:hc + 8], in1=hmasks[:, hc:hc + 8], op=ALU.mult),
                            reads=[f"ps{psb}"], writes=[ta]))
                        wslot = load_w(20 + j)
                        for (c0, n) in mtiles:
                            psb = mm_tile(wslot, c0, n)
                            P.add("dve", lambda e, psb=psb, c0=c0, n=n: e.tensor_tensor(
                                out=Wa[:, wcol(c0):wcol(c0) + n], in0=ps[:, psb, :n], in1=Wa[:, wcol(c0):wcol(c0) + n], op=ALU.mult),
                                reads=[f"ps{psb}", ta], writes=[ta])
                        psb = mm_tile(wslot, *HL)
                        halo_pair(lambda wc, hc, psb=psb: P.add("dve", lambda e: e.tensor_tensor(
                            out=Wa[:, wc:wc + 8], in0=ps[:, psb, hc:hc + 8], in1=Wa[:, wc:wc + 8], op=ALU.mult),
                            reads=[f"ps{psb}", ta], writes=[ta]))
                        P.add("dve", lambda e: e.tensor_scalar(out=Wb[:, 1:WN - 1], in0=Wa[:, 1:WN - 1], scalar1=cws[:, l, j, 1:2], scalar2=None, op0=ALU.mult),
                              reads=[ta], writes=[tb])
                        P.add("dve", lambda e: e.scalar_tensor_tensor(out=Wb[:, 1:WN - 1], in0=Wa[:, 0:WN - 2], scalar=cws[:, l, j, 0:1],
                                                                      in1=Wb[:, 1:WN - 1], op0=ALU.mult, op1=ALU.add),
                              reads=[ta, tb], writes=[tb])
                        P.add("dve", lambda e: e.scalar_tensor_tensor(out=Wb[:, 1:WN - 1], in0=Wa[:, 2:WN], scalar=cws[:, l, j, 2:3],
                                                                      in1=Wb[:, 1:WN - 1], op0=ALU.mult, op1=ALU.add),
                              reads=[ta, tb], writes=[tb])
                        wslot = load_w(16 + j)
                        for (c0, n) in mtiles:
                            psb = mm_tile(wslot, c0, n)
                            o = new_ost()
                            P.add("dve", lambda e, psb=psb, c0=c0, n=n, o=o: e.tensor_tensor(
                                out=ost[o][:, :n], in0=ps[:, psb, :n], in1=Wb[:, wcol(c0):wcol(c0) + n], op=ALU.mult),
                                reads=[f"ps{psb}", tb], writes=[f"ost{o}"])
                            P.dma("sp", f"ost{o}", mixbuf[:, 12 + j, c0:c0 + n], ost[o][:, :n], reads=[f"ost{o}"])

                    def pool_chunk(g):
                        U, Bb, Cb = W[0], W[1], W[3]
                        w = (2, 4, 8, 16)[g]
                        wslot = load_w(12 + g)
                        for (c0, n) in mtiles:
                            psb = mm_tile(wslot, c0, n)
                            P.add("act", lambda e, psb=psb, c0=c0, n=n: e.activation(out=U[:, wcol(c0):wcol(c0) + n], in_=ps[:, psb, :n], func=AF.Copy),
                                  reads=[f"ps{psb}"], writes=["W0"])
                        psb = mm_tile(wslot, *HL)
                        halo_pair(lambda wc, hc, psb=psb: P.add("dve", lambda e: e.tensor_tensor(
                            out=U[:, wc:wc + 8], in0=ps[:, psb, hc:hc + 8], in1=hmasks[:, hc:hc + 8], op=ALU.mult),
                            reads=[f"ps{psb}"], writes=["W0"]))
                        N = WN

                        def add(dst, dlo, a, alo, b, blo, ln, rd, wt):
                            P.add("dve", lambda e: e.tensor_tensor(out=dst[:, dlo:dlo + ln], in0=a[:, alo:alo + ln], in1=b[:, blo:blo + ln], op=ALU.add),
                                  reads=rd, writes=wt)
                        if g == 0:
                            add(Bb, 1, U, 0, U, 1, N - 1, ["W0"], ["W1"])
                            S, ts = Bb, "W1"
                        else:
                            add(Bb, 0, U, 0, U, 1, N - 1, ["W0"], ["W1"])
                            if g == 1:
                                add(Cb, 2, Bb, 0, Bb, 2, N - 3, ["W1"], ["W3"])
                                S, ts = Cb, "W3"
                            else:
                                add(Cb, 0, Bb, 0, Bb, 2, N - 3, ["W1"], ["W3"])
                                if g == 2:
                                    add(Bb, 4, Cb, 0, Cb, 4, N - 7, ["W3"], ["W1"])
                                    S, ts = Bb, "W1"
                                else:
                                    add(Bb, 0, Cb, 0, Cb, 4, N - 7, ["W3"], ["W1"])
                                    add(Cb, 8, Bb, 0, Bb, 8, N - 15, ["W1"], ["W3"])
                                    S, ts = Cb, "W3"
                        P.add("dve", lambda e: e.scalar_tensor_tensor(out=Dbf[:, 8:N - 8], in0=S[:, 8:N - 8], scalar=1.0 / w, in1=U[:, 8:N - 8],
                                                                      op0=ALU.mult, op1=ALU.subtract),
                              reads=[ts, "W0"], writes=["Dbf"])
                        for ei, ec in enumerate((8, 256, 280, 2320)):
                            P.add("dve", lambda e, ei=ei, ec=ec: e.tensor_tensor(out=tmp8[:], in0=S[:, ec:ec + 8], in1=ptabs[:, g, ei, :], op=ALU.mult),
                                  reads=[ts], writes=["tmp8"])
                            P.add("dve", lambda e, ec=ec: e.tensor_tensor(out=Dbf[:, ec:ec + 8], in0=tmp8[:], in1=U[:, ec:ec + 8], op=ALU.subtract),
                                  reads=["tmp8", "W0", "Dbf"], writes=["Dbf"])
                        for (c0, n) in mtiles:
                            ab = 4 + cnt["aux"] % 4
                            cnt["aux"] += 1
                            P.add("pe", lambda e, ab=ab, c0=c0, n=n: e.matmul(ps[:, ab, :n], lhsT=wpools[:, l, g, :], rhs=Dbf[:, wcol(c0):wcol(c0) + n],
                                                                              start=True, stop=True),
                                  reads=["Dbf"], writes=[f"ps{ab}"])
                            o = new_ost()
                            P.add("act", lambda e, ab=ab, n=n, o=o: e.activation(out=ost[o][:, :n], in_=ps[:, ab, :n], func=AF.Copy, scale=pscs[:, l, g:g + 1]),
                                  reads=[f"ps{ab}"], writes=[f"ost{o}"])
                            P.dma("sp", f"ost{o}", mixbuf[:, 8 + g, c0:c0 + n], ost[o][:, :n], reads=[f"ost{o}"])

                    sub = stop_after.split(":")[1] if (stop_after and ":" in stop_after and stop_after.startswith(f"A1_{l}")) else "all"
                    qk_chunk(8, "k", 0)
                    qk_chunk(9, "k", 1)
                    if sub != "k":
                        v_chunk()
                    if sub not in ("k", "v"):
                        for j in range(4):
                            conv_triple(j)
                    if sub not in ("k", "v", "conv"):
                        for g in range(4):
                            pool_chunk(g)
                    if sub not in ("k", "v", "conv", "pool"):
                        for h in range(8):
                            qk_chunk(h, "q", h)
                    if sub not in ("k", "v", "conv", "pool", "q"):
                        for kvi in range(2):
                            P.add("pool", lambda e, kvi=kvi: e.collective_compute(
                                "AllGather", ALU.bypass, replica_groups=RG, ins=[kv_in[kvi * 128:(kvi + 1) * 128, :]], outs=[kv_all[kvi]]),
                                reads=kvtoks + ["kv_all"], writes=["kv_all"], final_wait=True)
                        if debug and l == 0:
                            P.dma("sp", "dbg_kv", kvdbg, kv_all, reads=["kv_all"])
                    P.emit(nc)
                    if sub != "all":
                        return True
            return stop_after == f"A1_{l}"

        def phase_B(l):
            last = (l == DEPTH - 1)
            with contextlib.ExitStack() as st:
                Kf = sb("Kf", [128, 2, NCTX + 4 * NLAT], BF16, st)
                Vf = sb("Vf", [128, 66, 256], BF16, st)
                qa = sb("qa", [128, 8, T], BF16, st)
                Pt = [sb(f"Pt{i}", [128, 2, 512], BF16, st) for i in range(4)]
                rec = [sb(f"rec{i}", [128, 512], F32, st) for i in range(2)]
                sacc = [sb(f"sacc{i}", [128, 2, 512], F32, st) for i in range(2)]
                ssb = sb("ssb", [128, 512], BF16, st)
                kva = kv_all.rearrange("k (r p) c -> k p r c", r=4, p=128)
                for h in range(2):
                    P.dma("sp", f"ldk{h}", Kf[:, h, NCTX:].rearrange("p (r t) -> p r t", r=4), kva[0][:, :, h * NLAT:(h + 1) * NLAT],
                          writes=["Kf"])
                P.dma("sp", "ldkc", Kf[:, :, 0:NCTX], kvc[:, 0], writes=["Kf"])
                P.dma("sp", "ldv", Vf[:, 2:66, :].rearrange("p (r i) d -> p r (i d)", r=4), kva[1], writes=["Vf"])
                P.dma("sp", "ldvc", Vf[:, 0:2, :], kvc[:, 1], writes=["Vf"])
                P.dma("sp", "ldq", qa[:], qbuf, writes=["qa"])
                iters = []
                if not last:
                    for h in range(8):
                        iters.append((h, 0, 256, [0, 1]))
                for h in range(8):
                    for (c0, n) in TL[1:]:
                        iters.append((h, c0, n, list(range(66))))
                import os as _os
                if _os.environ.get("KDBG_BITERS"):
                    _n = int(_os.environ["KDBG_BITERS"])
                    _keep = iters[:_n] + iters[8:8 + _n]
                    for (h_, c0_, n_, _k) in iters:
                        if (h_, c0_, n_, _k) not in _keep:
                            P.dma("sp", f"dbgq{h_ % 2}", mixbuf[:, h_, c0_:c0_ + n_], qa[:, h_, c0_:c0_ + n_], reads=["qa"])
                    iters = _keep
                cnt = {"s": 0, "p": 0}

                def do_iter(it, h, c0, n, kts):
                    ab = 0
                    kvh = h // 4
                    bo, bs = 6, 7
                    qtok = f"q:{h}:{c0}"
                    nk = len(kts)
                    npair = nk // 2
                    offl = SUM_OFFLOAD and npair >= 8

                    def mode(j):
                        if not offl:
                            return "pe"
                        return ("pe", "dve", "pool", "dve", "dve", "pe", "dve", "pool", "dve", "dve", "pe", "dve", "pool", "dve", "pool")[j % 15]
                    pe_js = [j for j in range(npair) if mode(j) == "pe"]
                    sbank = {}
                    pslot = {}
                    first = {"dve": True, "pool": True}

                    def S2(j):
                        b0 = 2 * (cnt["s"] % 3)
                        cnt["s"] += 1
                        sbank[j] = b0
                        k0, k1 = kts[2 * j], kts[2 * j + 1]

                        def fn(e, b0=b0, k0=k0, k1=k1):
                            e.matmul(ps[:, b0, :n], lhsT=Kf[:, kvh, k0 * 128:(k0 + 1) * 128], rhs=qa[:, h, c0:c0 + n], start=True, stop=True)
                            return e.matmul(ps[:, b0 + 1, :n], lhsT=Kf[:, kvh, k1 * 128:(k1 + 1) * 128], rhs=qa[:, h, c0:c0 + n], start=True, stop=True)
                        P.add("pe", fn, reads=["Kf", "qa", qtok], writes=[f"ps{b0}", f"ps{b0 + 1}"])

                    def E2(j):
                        b0 = sbank[j]
                        p = cnt["p"] % 4
                        cnt["p"] += 1
                        pslot[j] = p
                        P.add("act", lambda e, b0=b0, p=p: e.activation(out=Pt[p][:, :, :n], in_=ps[:, b0:b0 + 2, :n], func=AF.Exp, scale=SCALE),
                              reads=[f"ps{b0}", f"ps{b0 + 1}"], writes=[f"Pt{p}"])

                    def PV2(j):
                        p = pslot[j]
                        k0, k1 = kts[2 * j], kts[2 * j + 1]
                        md = mode(j)

                        def fn(e, p=p, k0=k0, k1=k1, j=j, md=md):
                            e.matmul(ps[:, bo, :n], lhsT=Vf[:, k0, kvh * 128:(kvh + 1) * 128], rhs=Pt[p][:, 0, :n], start=(j == 0), stop=False)
                            r = e.matmul(ps[:, bo, :n], lhsT=Vf[:, k1, kvh * 128:(kvh + 1) * 128], rhs=Pt[p][:, 1, :n], start=False, stop=(j == npair - 1))
                            if md == "pe":
                                lastpe = (j == pe_js[-1]) and not offl
                                e.matmul(ps[:, bs, :n], lhsT=ones[:], rhs=Pt[p][:, 0, :n], start=(j == 0), stop=False)
                                r = e.matmul(ps[:, bs, :n], lhsT=ones[:], rhs=Pt[p][:, 1, :n], start=False, stop=lastpe)
                            return r
                        P.add("pe", fn, reads=[f"Pt{p}", "Vf"], writes=[f"acc{ab}"])
                        if md != "pe":
                            a = 0 if md == "dve" else 1
                            if first[md]:
                                first[md] = False
                                P.add(md, lambda e, a=a, p=p: e.tensor_copy(out=sacc[a][:, :, :n], in_=Pt[p][:, :, :n]),
                                      reads=[f"Pt{p}"], writes=[f"sacc{a}"])
                            else:
                                P.add(md, lambda e, a=a, p=p: e.tensor_tensor(out=sacc[a][:, :, :n], in0=sacc[a][:, :, :n], in1=Pt[p][:, :, :n], op=ALU.add),
                                      reads=[f"Pt{p}", f"sacc{a}"], writes=[f"sacc{a}"])
                    LA = 2
                    for j in range(min(LA, npair)):
                        S2(j)
                    for j in range(npair):
                        if j + LA < npair:
                            S2(j + LA)
                        E2(j)
                        PV2(j)
                    if offl:
                        P.add("dve", lambda e: e.tensor_tensor(out=sacc[0][:, 0, :n], in0=sacc[0][:, 0, :n], in1=sacc[0][:, 1, :n], op=ALU.add),
                              reads=["sacc0"], writes=["sacc0"])
                        P.add("pool", lambda e: e.tensor_tensor(out=sacc[1][:, 0, :n], in0=sacc[1][:, 0, :n], in1=sacc[1][:, 1, :n], op=ALU.add),
                              reads=["sacc1"], writes=["sacc1"])
                        P.add("dve", lambda e: e.tensor_tensor(out=ssb[:, :n], in0=sacc[0][:, 0, :n], in1=sacc[1][:, 0, :n], op=ALU.add),
                              reads=["sacc0", "sacc1"], writes=["ssb"])
                        P.add("pe", lambda e: e.matmul(ps[:, bs, :n], lhsT=ones[:], rhs=ssb[:, :n], start=False, stop=True),
                              reads=["ssb", f"acc{ab}"], writes=[f"acc{ab}"])
                    P.add("dve", lambda e: e.reciprocal(out=rec[ab][:, :n], in_=ps[:, bs, :n]),
                          reads=[f"acc{ab}"], writes=[f"rec{ab}"])
                    P.add("dve", lambda e: e.tensor_tensor(out=qa[:, h, c0:c0 + n], in0=ps[:, bo, :n], in1=rec[ab][:, :n], op=ALU.mult),
                          reads=[f"acc{ab}", f"rec{ab}"], writes=[qtok, f"acc{ab}"])
                    P.dma("sp", f"sta{it % 4}", mixbuf[:, h, c0:c0 + n], qa[:, h, c0:c0 + n], reads=[qtok])
                for it, (h, c0, n, kts) in enumerate(iters):
                    do_iter(it, h, c0, n, kts)
                P.emit(nc)
            return stop_after == f"B_{l}"

        def phase_C(l):
            last = (l == DEPTH - 1)
            with contextlib.ExitStack() as st:
                mix = sb("mix", [128, KC, T], BF16, st)
                wr = [sb(f"wo{i}", [128, KC, 128], BF16, st) for i in range(3)]
                xs = [sb(f"xc{i}", [128, 512], F32, st) for i in range(4)]
                src = xT if l == 0 else xres
                tiles = TL[1:] if last else TL
                P.dma("sp", "ldmix", mix[:], mixbuf, writes=["mix"])
                k = 0
                for m in range(KC):
                    ws = m % 3
                    P.dma("pool", f"wo{ws}", wr[ws][:], wout_(l)[m], writes=[f"wo{ws}"])
                    for (c0, n) in tiles:
                        s = 1 if c0 < NCTX else 0
                        xi = k % 4
                        psb = k % 4
                        k += 1
                        P.dma("sp", f"xc{xi}", xs[xi][:, :n], src[:, m, c0:c0 + n], writes=[f"xc{xi}"])

                        def fn(e, ws=ws, psb=psb, c0=c0, n=n):
                            last_ = None
                            for kc in range(KC):
                                last_ = e.matmul(ps[:, psb, :n], lhsT=wr[ws][:, kc, :], rhs=mix[:, kc, c0:c0 + n], start=(kc == 0), stop=(kc == KC - 1))
                            return last_
                        P.add("pe", fn, reads=[f"wo{ws}", "mix"], writes=[f"ps{psb}"])
                        P.add("dve", lambda e, xi=xi, psb=psb, n=n, m=m, s=s: e.scalar_tensor_tensor(
                            out=xs[xi][:, :n], in0=ps[:, psb, :n], scalar=mv(l, 2, m, s), in1=xs[xi][:, :n], op0=ALU.mult, op1=ALU.add),
                            reads=[f"ps{psb}", f"xc{xi}"], writes=[f"xc{xi}"])
                        P.dma("sp", f"xo{xi}", xres[:, m, c0:c0 + n], xs[xi][:, :n], reads=[f"xc{xi}"])
                P.emit(nc)
            return stop_after == f"C_{l}"

        def phase_D(l):
            last = (l == DEPTH - 1)
            halves = [TL[1:3], TL[3:5]] if last else [TL[0:3], TL[3:5]]
            G = 4

            def blk(a0, n):
                return list(range(a0 // 256, (a0 + n) // 256))

            def acct(a0, n):
                return [f"acc:{k}" for k in blk(a0, n)]

            def accwt(m, a0, n):
                return [f"accw:{m}:{k}" for k in blk(a0, n)]
            with contextlib.ExitStack() as st:
                acc = sb("acc", [128, KC, 1280], F32, st)
                h2 = sb("h2", [128, KC, 1280], BF16, st)
                w1 = [sb(f"w1_{i}", [128, KC, 128], BF16, st) for i in range(6)]
                w2 = [sb(f"w2_{i}", [128, D], BF16, st) for i in range(6)]
                ut = [sb(f"ut{i}", [128, G, 512], BF16, st) for i in range(2)]
                rl = [sb(f"rl{i}", [128, 512], F32, st) for i in range(2)]
                nb = NormBufs(st, "d", 256)
                for hi, tiles in enumerate(halves):
                    hc0 = tiles[0][0]
                    hn = sum(n for _, n in tiles)
                    acctoks = [f"accw:{m}:{c0 - hc0}" for m in range(KC) for (c0, n) in tiles]
                    for ti, (c0_, n_) in enumerate(tiles):
                        P.dma("sp", f"ldacc{ti}", acc[:, :, c0_ - hc0:c0_ - hc0 + n_], xres[:, :, c0_:c0_ + n_],
                              writes=acct(c0_ - hc0, n_) + [t_ for m in range(KC) for t_ in accwt(m, c0_ - hc0, n_)])
                    k = 0
                    for (c0, n) in tiles:
                        s = 1 if c0 < NCTX else 0
                        for o in range(0, n, 256):
                            a0 = c0 - hc0 + o
                            emit_norm(nb, lambda c, a0=a0: acc[:, c, a0:a0 + 256], 256,
                                      lambda c, a0=a0: h2[:, c, a0:a0 + 256], l, 1, s, k % 2, acct(a0, 256), [f"h2:{a0}"])
                            k += 1
                    cnt = {"u": 0, "y": 0, "ut": 0, "rl": 0}
                    for fg in range(NF // G):
                        slots = []
                        for fi in range(G):
                            f = fg * G + fi
                            sl = f % 6
                            slots.append(sl)
                            P.dma("pool", f"w1_{sl}", w1[sl][:], wff1_(l)[f], writes=[f"w1_{sl}"])
                            P.dma("pool", f"w2_{sl}", w2[sl][:], wff2_(l)[f], writes=[f"w2_{sl}"])
                        for (c0, n) in tiles:
                            s = 1 if c0 < NCTX else 0
                            a0 = c0 - hc0
                            h2toks = [f"h2:{a0 + o}" for o in range(0, n, 256)]
                            ui = cnt["ut"] % 2
                            cnt["ut"] += 1
                            for fi in range(G):
                                sl = slots[fi]
                                ub = cnt["u"] % 4
                                cnt["u"] += 1

                                def fn(e, sl=sl, ub=ub, a0=a0, n=n):
                                    last_ = None
                                    for kc in range(KC):
                                        last_ = e.matmul(ps[:, ub, :n], lhsT=w1[sl][:, kc, :], rhs=h2[:, kc, a0:a0 + n], start=(kc == 0), stop=(kc == KC - 1))
                                    return last_
                                P.add("pe", fn, reads=[f"w1_{sl}"] + h2toks, writes=[f"ps{ub}"])
                                ri = cnt["rl"] % 2
                                cnt["rl"] += 1
                                P.add("act", lambda e, ub=ub, ri=ri, n=n: e.activation(out=rl[ri][:, :n], in_=ps[:, ub, :n], func=AF.Relu),
                                      reads=[f"ps{ub}"], writes=[f"rl{ri}"])
                                P.add("dve", lambda e, ri=ri, ui=ui, fi=fi, n=n: e.tensor_tensor(out=ut[ui][:, fi, :n], in0=rl[ri][:, :n], in1=rl[ri][:, :n], op=ALU.mult),
                                      reads=[f"rl{ri}"], writes=[f"ut{ui}"])
                            for m in range(KC):
                                yb = 4 + cnt["y"] % 4
                                cnt["y"] += 1

                                def fn2(e, yb=yb, m=m, ui=ui, n=n, slots=tuple(slots)):
                                    last_ = None
                                    for fi in range(G):
                                        last_ = e.matmul(ps[:, yb, :n], lhsT=w2[slots[fi]][:, m * 128:(m + 1) * 128], rhs=ut[ui][:, fi, :n],
                                                         start=(fi == 0), stop=(fi == G - 1))
                                    return last_
                                P.add("pe", fn2, reads=[f"w2_{sl}" for sl in slots] + [f"ut{ui}"], writes=[f"ps{yb}"])
                                P.add("dve", lambda e, yb=yb, m=m, a0=a0, n=n, s=s: e.scalar_tensor_tensor(
                                    out=acc[:, m, a0:a0 + n], in0=ps[:, yb, :n], scalar=mv(l, 5, m, s), in1=acc[:, m, a0:a0 + n],
                                    op0=ALU.mult, op1=ALU.add), reads=[f"ps{yb}"] + accwt(m, a0, n), writes=accwt(m, a0, n))
                    for ti, (c0_, n_) in enumerate(tiles):
                        a_ = c0_ - hc0
                        tt = [t_ for m in range(KC) for t_ in accwt(m, a_, n_)] + acct(a_, n_)
                        dst = outT[:, :, c0_ - NCTX:c0_ - NCTX + n_] if last else xres[:, :, c0_:c0_ + n_]
                        P.dma("sp", f"stout{ti}", dst, acc[:, :, a_:a_ + n_], reads=tt, writes=[f"accst{ti}"])
                    if not last:
                        ed = edge_in.rearrange("p (c t) -> p c t", t=16)
                        if hi == 0:
                            a = NCTX - hc0
                            P.dma("sp", "edge0", ed[:, :, 0:8], acc[:, :, a:a + 8],
                                  reads=[t_ for m in range(KC) for t_ in accwt(m, a, 256)] + acct(a, 256), writes=["edge0"])
                        else:
                            a = T - 8 - hc0
                            a_t = (a // 256) * 256
                            P.dma("sp", "edge1", ed[:, :, 8:16], acc[:, :, a:a + 8],
                                  reads=[t_ for m in range(KC) for t_ in accwt(m, a_t, 256)] + acct(a_t, 256), writes=["edge1"])
                if not last:
                    P.add("pool", lambda e: e.collective_compute("AllGather", ALU.bypass, replica_groups=RG, ins=[edge_in], outs=[edge_all]),
                          reads=["edge0", "edge1"], writes=["edge_all"], final_wait=True)
                P.emit(nc)
            return stop_after == f"D_{l}"

        phase_M()
        done = stop_after == "M"
        for l in range(nl):
            if done:
                break
            done = phase_A(l)
            if done:
                break
            done = phase_B(l)
            if done:
                break
            done = phase_C(l)
            if done:
                break
            done = phase_D(l)
        if done or True:
            pass
    return nc


def _perm():
    i = np.arange(128)
    return np.where((i // 32) % 2 == 0, i + 32, i - 32)


def _rope_tables(pos):
    n = pos.shape[0]
    row = (pos // 64).astype(np.float32)
    col = (pos % 64).astype(np.float32)
    inv = (np.float32(10000.0) ** (-(np.arange(32, dtype=np.float32)) / np.float32(32))).astype(np.float32)
    ar = row[:, None] * inv
    ac = col[:, None] * inv
    ang = np.concatenate([ar, ar, ac, ac], -1)
    return np.cos(ang).astype(np.float32), np.sin(ang).astype(np.float32)


def prep_shared(inp):
    f = np.float32
    sh = {}
    for l in range(DEPTH):
        sh[f"wmod{l}"] = np.ascontiguousarray(inp["w_mod"][l].reshape(16, 128, 96, 128).transpose(2, 1, 0, 3))
        sh[f"win{l}"] = np.ascontiguousarray(inp["w_in"][l].reshape(16, 128, 28, 128).transpose(2, 1, 0, 3))
        sh[f"wout{l}"] = np.ascontiguousarray(inp["w_out"][l].reshape(16, 128, 16, 128).transpose(2, 1, 0, 3))
        sh[f"wff1{l}"] = np.ascontiguousarray(inp["w_ff1"][l].reshape(16, 128, 64, 128).transpose(2, 1, 0, 3))
        sh[f"wff2{l}"] = np.ascontiguousarray(inp["w_ff2"][l].reshape(64, 128, 2048))
    bm = inp["b_mod"].reshape(2, 96, 128).transpose(2, 0, 1)
    sh["bmod"] = np.ascontiguousarray(np.repeat(bm[..., None], 2, -1)).astype(f)
    g = np.stack([inp["norm1_g"], inp["norm2_g"]], 1)
    sh["g12"] = np.ascontiguousarray(g.reshape(2, 2, 16, 128).transpose(3, 0, 1, 2)).astype(f)
    pm = _perm()
    qg, kg = inp["q_norm_g"], inp["k_norm_g"]
    sh["qkg"] = np.ascontiguousarray(np.stack([qg, qg[:, pm], kg, kg[:, pm]], -1).transpose(1, 0, 2)).astype(f)
    rm = np.zeros((128, 128), np.float32)
    rm[pm, np.arange(128)] = 1.0
    sh["rmat"] = rm.astype(ml_dtypes.bfloat16)
    sh["wpool"] = np.ascontiguousarray(inp["w_pool"].transpose(2, 0, 1, 3)).astype(f)
    sh["pscale"] = np.ascontiguousarray(inp["pool_scale"].reshape(2, 4, 128).transpose(2, 0, 1)).astype(f)
    sh["convw"] = np.ascontiguousarray(inp["conv_w"].reshape(2, 3, 4, 128).transpose(3, 0, 2, 1)).astype(f)
    return sh


def prep_core(inp, r):
    f = np.float32
    b, j = r // 4, r % 4
    x = inp["x"]
    lo, hi = j * NLAT, (j + 1) * NLAT
    cat = np.concatenate([inp["ctx"][b], x[b, lo:hi]], 0)
    m = {}
    m["xT"] = np.ascontiguousarray(cat.reshape(T, 16, 128).transpose(2, 1, 0))
    hl = x[b, lo - 8:lo] if j > 0 else np.zeros((8, D), f)
    hr = x[b, hi:hi + 8] if j < 3 else np.zeros((8, D), f)
    m["xh"] = np.ascontiguousarray(np.concatenate([hl, hr], 0).reshape(16, 16, 128).transpose(2, 1, 0)).astype(f)
    cv = np.stack([inp["c"][b], inp["c_ctx"]], -1)
    m["cv"] = np.ascontiguousarray(cv.reshape(16, 128, 2).transpose(1, 0, 2)).astype(f)
    cos, sin = _rope_tables(np.arange(lo, hi))
    sign = np.where((np.arange(128) // 32) % 2 == 0, -1.0, 1.0).astype(f)
    cs = np.zeros((128, 2, T), f)
    cs[:, 0, :NCTX] = 1.0
    cs[:, 0, NCTX:] = cos.T
    cs[:, 1, NCTX:] = (sin * sign[None, :]).T
    m["cs"] = cs
    pt = np.zeros((4, 4, 8), f)

    def inv_cnt(tpos, n, w):
        lo_ = w // 2
        hi_ = w - 1 - lo_
        st = np.clip(tpos - lo_, 0, n)
        en = np.clip(tpos + hi_ + 1, 0, n)
        return (1.0 / (en - st)).astype(f)
    for g, w in enumerate((2, 4, 8, 16)):
        pt[g, 0] = inv_cnt(np.arange(0, 8), NCTX, w)
        pt[g, 1] = inv_cnt(np.arange(NCTX - 8, NCTX), NCTX, w)
        pt[g, 2] = inv_cnt(np.arange(lo, lo + 8), 4 * NLAT, w)
        pt[g, 3] = inv_cnt(np.arange(hi - 8, hi), 4 * NLAT, w)
    m["ptab"] = np.ascontiguousarray(np.broadcast_to(pt[None], (128, 4, 4, 8))).astype(f)
    hm = np.zeros((16,), f)
    hm[0:8] = 1.0 if j > 0 else 0.0
    hm[8:16] = 1.0 if j < 3 else 0.0
    m["hmask"] = np.ascontiguousarray(np.broadcast_to(hm[None], (128, 16))).astype(f)
    hs = np.zeros((8,), f)
    if j > 0:
        hs[j - 1] = 1.0
    if j < 3:
        hs[4 + j + 1] = 1.0
    m["hsel"] = np.ascontiguousarray(np.broadcast_to(hs[None], (128, 8))).astype(f)
    return m


def make_in_maps(inp):
    inp = {k: np.asarray(v) for k, v in inp.items()}
    sh = prep_shared(inp)
    maps = []
    for r in range(8):
        m = prep_core(inp, r)
        m.update(sh)
        maps.append(m)
    return maps


def assemble(results):
    out = np.empty((2, 4 * NLAT, D), np.float32)
    for r in range(8):
        b, j = r // 4, r % 4
        o = np.asarray(results[r]["outT"])
        out[b, j * NLAT:(j + 1) * NLAT] = o.transpose(2, 1, 0).reshape(NLAT, D)
    return out


def kernel(**inputs):
    maps = make_in_maps(inputs)
    nc = build()
    res = run_bass_kernel_spmd(nc, maps, core_ids=list(range(8)))
    return assemble(res.results)
```
